# Optimizing a Trainium2 kernel written in Bass

```python
import jax, jax.numpy as jnp
from jax import lax
import numpy as np

D_MODEL = 1024
BATCH = 2
SEQ = 8192
DEPTH = 4

N_BRANCH = 4
BRANCH_WIDTH = D_MODEL // 2
CHUNK = 128
SG_GROUPS = 4
SG_GROUP_DIM = BRANCH_WIDTH // SG_GROUPS
CONV_WIDTH = 3
MLA_HEADS = 8
MLA_NOPE_DIM = 64
MLA_ROPE_DIM = 32
MLA_V_DIM = BRANCH_WIDTH // MLA_HEADS
MLA_Q_RANK = 256
MLA_KV_RANK = 128
ROPE_THETA = 10000.0
SB_HEADS = 8
SB_HEAD_DIM = BRANCH_WIDTH // SB_HEADS
Q_BLOCK = 128
D_FF = 4 * D_MODEL
NORM_EPS = 1e-6

COLS_SG = 2 * BRANCH_WIDTH
COLS_CONV = 3 * BRANCH_WIDTH
COLS_MLA = MLA_Q_RANK + MLA_KV_RANK + MLA_ROPE_DIM
COLS_SB = 3 * SB_HEADS * SB_HEAD_DIM
COLS_GATE = N_BRANCH * D_MODEL
SPLIT_POINTS = (COLS_SG, COLS_SG + COLS_CONV, COLS_SG + COLS_CONV + COLS_MLA,
                COLS_SG + COLS_CONV + COLS_MLA + COLS_SB)
IN_COLS = COLS_SG + COLS_CONV + COLS_MLA + COLS_SB + COLS_GATE

kernel_name = 'hybrid_gated_parallel_mixer_trunk'


def _rmsnorm(x, g):
    x32 = x.astype(jnp.float32)
    y = x32 * lax.rsqrt(jnp.mean(x32 * x32, axis=-1, keepdims=True) + NORM_EPS)
    return y.astype(x.dtype) * g


def _layernorm(x, g):
    x32 = x.astype(jnp.float32)
    xc = x32 - jnp.mean(x32, axis=-1, keepdims=True)
    y = xc * lax.rsqrt(jnp.mean(xc * xc, axis=-1, keepdims=True) + NORM_EPS)
    return y.astype(x.dtype) * g


def _rope(x, cos, sin):
    x1, x2 = jnp.split(x, 2, axis=-1)
    return jnp.concatenate([x1 * cos - x2 * sin, x2 * cos + x1 * sin], axis=-1)


def _to_blocks(t):
    b, s, h, d = t.shape
    return t.reshape(b, s // Q_BLOCK, Q_BLOCK, h, d).transpose(1, 0, 2, 3, 4)


def _from_blocks(o):
    nb, b, qb, h, d = o.shape
    return o.transpose(1, 0, 2, 3, 4).reshape(b, nb * qb, h * d)


def _spatial_gating(z, norm_g, w_s, b_s):
    bsz, seq, _ = z.shape
    u, v = jnp.split(jax.nn.gelu(z), 2, axis=-1)
    v = _layernorm(v, norm_g).reshape(bsz, seq // CHUNK, CHUNK, SG_GROUPS, SG_GROUP_DIM)
    causal = jnp.tril(jnp.ones((CHUNK, CHUNK), dtype=w_s.dtype))
    s = jnp.einsum('gts,bnsgc->bntgc', w_s * causal, v) + b_s.T[:, :, None]
    return u * s.reshape(bsz, seq, BRANCH_WIDTH)


def _short_conv(z, conv_w):
    gate_b, gate_c, xv = jnp.split(z, 3, axis=-1)
    t = gate_c * xv
    y = lax.conv_general_dilated(t, conv_w[:, None, :], window_strides=(1,),
                                 padding=[(CONV_WIDTH - 1, 0)],
                                 dimension_numbers=('NWC', 'WIO', 'NWC'),
                                 feature_group_count=BRANCH_WIDTH)
    return gate_b * y


def _latent_attention(z, cos, sin, q_norm_g, w_uq, kv_norm_g, w_ukv):
    bsz, seq, _ = z.shape
    c_q, c_kv, k_pe = jnp.split(z, [MLA_Q_RANK, MLA_Q_RANK + MLA_KV_RANK], axis=-1)
    q = (_rmsnorm(c_q, q_norm_g) @ w_uq).reshape(bsz, seq, MLA_HEADS, MLA_NOPE_DIM + MLA_ROPE_DIM)
    q_nope, q_pe = jnp.split(q, [MLA_NOPE_DIM], axis=-1)
    q_pe = _rope(q_pe, cos[:, :, None, :], sin[:, :, None, :])
    kv = (_rmsnorm(c_kv, kv_norm_g) @ w_ukv).reshape(bsz, seq, MLA_HEADS, MLA_NOPE_DIM + MLA_V_DIM)
    k_nope, v = jnp.split(kv, [MLA_NOPE_DIM], axis=-1)
    k_pe = _rope(k_pe, cos, sin)
    scale = (MLA_NOPE_DIM + MLA_ROPE_DIM) ** -0.5
    k_pos = jnp.arange(seq)

    def block(args):
        i, qn, qp = args
        q_pos = i * Q_BLOCK + jnp.arange(Q_BLOCK)
        s = (jnp.einsum('bqhd,bkhd->bhqk', qn, k_nope, preferred_element_type=jnp.float32)
             + jnp.einsum('bqhr,bkr->bhqk', qp, k_pe, preferred_element_type=jnp.float32)) * scale
        s = jnp.where(k_pos[None, :] <= q_pos[:, None], s, -jnp.inf)
        p = jax.nn.softmax(s, axis=-1).astype(v.dtype)
        return jnp.einsum('bhqk,bkhd->bqhd', p, v)

    out = lax.map(block, (jnp.arange(seq // Q_BLOCK), _to_blocks(q_nope), _to_blocks(q_pe)))
    return _from_blocks(out)


def _stick_breaking(z):
    bsz, seq, _ = z.shape
    q, k, v = [t.reshape(bsz, seq, SB_HEADS, SB_HEAD_DIM) for t in jnp.split(z, 3, axis=-1)]
    scale = SB_HEAD_DIM ** -0.5
    k_pos = jnp.arange(seq)

    def block(args):
        i, qb = args
        q_pos = i * Q_BLOCK + jnp.arange(Q_BLOCK)
        logits = jnp.einsum('bqhd,bkhd->bhqk', qb, k, preferred_element_type=jnp.float32) * scale
        mask = k_pos[None, :] < q_pos[:, None]
        log_1m = jnp.where(mask, jax.nn.log_sigmoid(-logits), 0.0)
        rev = lax.cumsum(log_1m, axis=3, reverse=True)
        between = jnp.concatenate([rev[..., 1:], jnp.zeros_like(rev[..., :1])], axis=-1)
        a = jnp.where(mask, jnp.exp(jax.nn.log_sigmoid(logits) + between), 0.0)
        return jnp.einsum('bhqk,bkhd->bqhd', a.astype(v.dtype), v)

    out = lax.map(block, (jnp.arange(seq // Q_BLOCK), _to_blocks(q)))
    return _from_blocks(out)


def _token_mixing(h, cos, sin, w_in, sg_norm_g, sg_w, sg_b, conv_w,
                  q_norm_g, w_uq, kv_norm_g, w_ukv, w_branch, w_out):
    bsz, seq, _ = h.shape
    z = h @ w_in
    z_sg, z_conv, z_mla, z_sb, z_gate = jnp.split(z, list(SPLIT_POINTS), axis=-1)
    ys = jnp.stack([
        _spatial_gating(z_sg, sg_norm_g, sg_w, sg_b),
        _short_conv(z_conv, conv_w),
        _latent_attention(z_mla, cos, sin, q_norm_g, w_uq, kv_norm_g, w_ukv),
        _stick_breaking(z_sb),
    ], axis=2)
    up = jnp.einsum('bsnw,nwd->bsnd', ys, w_branch)
    gates = jax.nn.sigmoid(z_gate).reshape(bsz, seq, N_BRANCH, D_MODEL)
    merged = jnp.sum(gates * up, axis=2)
    return merged @ w_out


def _squared_relu_mlp(h, w1, w2):
    return jnp.square(jax.nn.relu(h @ w1)) @ w2


def setup_inputs(seed: int = 0) -> dict:
    key = jax.random.key(seed)
    ks = jax.random.split(key, 21)
    f32 = jnp.float32

    def nrm(k, shape, scale):
        return jax.random.normal(k, shape, f32) * scale

    def gain(k, shape):
        return 1.0 + 0.02 * jax.random.normal(k, shape, f32)

    start = jax.random.randint(ks[2], (BATCH, 1), 0, 4096, dtype=jnp.int32)
    positions = start + jnp.arange(SEQ, dtype=jnp.int32)[None, :]
    return {
        'x': nrm(ks[0], (BATCH, SEQ, D_MODEL), 1.0),
        'c': nrm(ks[1], (BATCH, D_MODEL), 1.0),
        'positions': positions,
        'ada_w': nrm(ks[3], (DEPTH, D_MODEL, 6 * D_MODEL), 0.5 * D_MODEL ** -0.5),
        'ada_b': nrm(ks[4], (DEPTH, 6 * D_MODEL), 0.02),
        'norm1_g': gain(ks[5], (DEPTH, D_MODEL)),
        'norm2_g': gain(ks[6], (DEPTH, D_MODEL)),
        'w_in': nrm(ks[7], (DEPTH, D_MODEL, IN_COLS), D_MODEL ** -0.5),
        'sg_norm_g': gain(ks[8], (DEPTH, BRANCH_WIDTH)),
        'sg_w': nrm(ks[9], (DEPTH, SG_GROUPS, CHUNK, CHUNK), CHUNK ** -0.5),
        'sg_b': gain(ks[10], (DEPTH, SG_GROUPS, CHUNK)),
        'conv_w': nrm(ks[11], (DEPTH, CONV_WIDTH, BRANCH_WIDTH), CONV_WIDTH ** -0.5),
        'mla_q_norm_g': gain(ks[12], (DEPTH, MLA_Q_RANK)),
        'mla_w_uq': nrm(ks[13], (DEPTH, MLA_Q_RANK, MLA_HEADS * (MLA_NOPE_DIM + MLA_ROPE_DIM)), MLA_Q_RANK ** -0.5),
        'mla_kv_norm_g': gain(ks[14], (DEPTH, MLA_KV_RANK)),
        'mla_w_ukv': nrm(ks[15], (DEPTH, MLA_KV_RANK, MLA_HEADS * (MLA_NOPE_DIM + MLA_V_DIM)), MLA_KV_RANK ** -0.5),
        'w_branch': nrm(ks[16], (DEPTH, N_BRANCH, BRANCH_WIDTH, D_MODEL), BRANCH_WIDTH ** -0.5),
        'w_out': nrm(ks[17], (DEPTH, D_MODEL, D_MODEL), D_MODEL ** -0.5),
        'mlp_w1': nrm(ks[18], (DEPTH, D_MODEL, D_FF), D_MODEL ** -0.5),
        'mlp_w2': nrm(ks[19], (DEPTH, D_FF, D_MODEL), 0.5 * D_FF ** -0.5),
        'final_norm_g': gain(ks[20], (D_MODEL,)),
    }


def reference(x, c, positions, ada_w, ada_b, norm1_g, norm2_g, w_in, sg_norm_g, sg_w, sg_b,
              conv_w, mla_q_norm_g, mla_w_uq, mla_kv_norm_g, mla_w_ukv, w_branch, w_out,
              mlp_w1, mlp_w2, final_norm_g):
    inv_freq = ROPE_THETA ** (-jnp.arange(0, MLA_ROPE_DIM, 2, dtype=jnp.float32) / MLA_ROPE_DIM)
    ang = positions.astype(jnp.float32)[..., None] * inv_freq
    cos = jnp.cos(ang).astype(x.dtype)
    sin = jnp.sin(ang).astype(x.dtype)
    c_act = jax.nn.silu(c)
    for l in range(DEPTH):
        mod = c_act @ ada_w[l] + ada_b[l]
        sh1, sc1, g1, sh2, sc2, g2 = jnp.split(mod[:, None, :], 6, axis=-1)
        h = _rmsnorm(x, norm1_g[l]) * (1.0 + sc1) + sh1
        x = x + g1 * _token_mixing(h, cos, sin, w_in[l], sg_norm_g[l], sg_w[l], sg_b[l], conv_w[l],
                                   mla_q_norm_g[l], mla_w_uq[l], mla_kv_norm_g[l], mla_w_ukv[l],
                                   w_branch[l], w_out[l])
        h = _rmsnorm(x, norm2_g[l]) * (1.0 + sc2) + sh2
        x = x + g2 * _squared_relu_mlp(h, mlp_w1[l], mlp_w2[l])
    return _rmsnorm(x, final_norm_g)
```

```python
import numpy as np
import concourse.bass as bass
import concourse.mybir as mybir
from concourse.bass_utils import run_bass_kernel_spmd

F32 = mybir.dt.float32
BF16 = mybir.dt.bfloat16
I32 = mybir.dt.int32
AF = mybir.ActivationFunctionType
ALU = mybir.AluOpType

SEM_LIMIT = 30000
N_DMA_SEMS = 12
SLOT = 512

D = 1024
COL_SG, COL_CONV, COL_MLA, COL_SB, COL_GATE, IN_COLS = 0, 1024, 2560, 2976, 4512, 8608
EPS = 1e-6
PVL = 35


class Sched:
    def __init__(self, nc):
        self.nc = nc
        self.ops = []

    def op(self, eng, fn, reads=(), writes=()):
        self.ops.append((eng, fn, tuple(reads), tuple(writes), False))

    def dma(self, q, fn, reads=(), writes=()):
        self.ops.append((q, fn, tuple(reads), tuple(writes), True))

    def emit(self):
        nc = self.nc
        engs = {"pe": nc.tensor, "act": nc.scalar, "dve": nc.vector, "pool": nc.gpsimd, "sp": nc.sync}
        ops = self.ops
        n = len(ops)
        last_w = {}
        rd_c = {}
        rd_d = {}
        deps = [None] * n
        need_sig = [False] * n
        for i, (e, fn, rs, ws, isd) in enumerate(ops):
            d = set()
            for k in rs:
                j = last_w.get(k)
                if j is not None:
                    d.add(j)
            for k in ws:
                j = last_w.get(k)
                if j is not None:
                    d.add(j)
                rc = rd_c.get(k)
                if rc:
                    d.update(rc.values())
                rdd = rd_d.get(k)
                if rdd:
                    d.update(rdd)
            for k in rs:
                if isd:
                    rd_d.setdefault(k, []).append(i)
                else:
                    rd_c.setdefault(k, {})[e] = i
            for k in ws:
                last_w[k] = i
                rd_c[k] = {}
                rd_d[k] = []
            nd = set()
            for j in d:
                if j == i:
                    continue
                ej, _, _, _, jd = ops[j]
                if (not isd) and (not jd) and e == "pe" and ej == "pe":
                    continue
                nd.add(j)
                if not jd:
                    need_sig[j] = True
            deps[i] = nd
        self.sem_ctx = []

        def new_sem(name):
            cm = nc.semaphore(name)
            s = cm.__enter__()
            self.sem_ctx.append(cm)
            return s

        cur = {}
        sig = [None] * n
        cnt = [0]
        dsem, dstate, dcount = {}, {}, {}
        prev_on_sem = [None] * n
        for i, (e, fn, rs, ws, isd) in enumerate(ops):
            if isd:
                if e not in dsem:
                    dsem[e] = [new_sem("d%s%d" % (e, t)) for t in range(N_DMA_SEMS)]
                    dstate[e] = [[0, None] for _ in range(N_DMA_SEMS)]
                    dcount[e] = 0
                t = dcount[e] % N_DMA_SEMS
                dcount[e] += 1
                st = dstate[e][t]
                if st[0] + 16 > SEM_LIMIT:
                    dsem[e][t] = new_sem("d%s%dx%d" % (e, t, cnt[0]))
                    cnt[0] += 1
                    st[0] = 0
                if st[1] is not None:
                    prev_on_sem[i] = st[1]
                st[0] += 16
                sig[i] = (dsem[e][t], st[0], 16)
                st[1] = i
            elif need_sig[i]:
                if e not in cur or cur[e][1] + 1 > SEM_LIMIT:
                    cur[e] = [new_sem("c%s%d" % (e, cnt[0])), 0]
                    cnt[0] += 1
                cur[e][1] += 1
                sig[i] = (cur[e][0], cur[e][1], 1)
        waited = {}
        nwaits = 0
        for i, (e, fn, rs, ws, isd) in enumerate(ops):
            eng = engs[e]
            dl = list(deps[i])
            if prev_on_sem[i] is not None:
                dl.append(prev_on_sem[i])
            mx = {}
            for j in dl:
                s, v, _ = sig[j]
                key = id(s)
                if key not in mx or mx[key][1] < v:
                    mx[key] = (s, v)
            for key, (s, v) in mx.items():
                wk = (e, key)
                if waited.get(wk, 0) >= v:
                    continue
                waited[wk] = v
                eng.wait_ge(s, v)
                nwaits += 1
            inst = fn(eng)
            if sig[i] is not None:
                inst.then_inc(sig[i][0], sig[i][2])
        feng = engs["sp"]
        for e in dsem:
            for t in range(N_DMA_SEMS):
                st = dstate[e][t]
                if st[1] is not None:
                    s, v, _ = sig[st[1]]
                    feng.wait_ge(s, v)
        self.stats = dict(n_ops=n, n_waits=nwaits, n_sems=len(self.sem_ctx))


_DTS = {F32: 4, BF16: 2, I32: 4}


class Tile:
    def __init__(self, h, off, nbytes, esz):
        self.h, self.off, self.nbytes, self.esz = h, off, nbytes, esz
        self._all = tuple(range(off // SLOT, (off + nbytes + SLOT - 1) // SLOT))

    def __getitem__(self, idx):
        return self.h[idx]

    def k(self):
        return self._all

    def ke(self, e0, ne):
        lo = self.off + e0 * self.esz
        hi = lo + ne * self.esz
        return tuple(range(lo // SLOT, (hi + SLOT - 1) // SLOT))


class Cursor:
    def __init__(self, nc, base, limit, tag):
        self.nc, self.cur, self.limit, self.tag, self.n = nc, base, limit, tag, 0

    def alloc(self, name, shape, dt):
        esz = _DTS[dt]
        nb = esz
        for s in shape[1:]:
            nb *= s
        off = (self.cur + SLOT - 1) // SLOT * SLOT
        assert off + nb <= self.limit, (self.tag, name, off, nb, self.limit)
        self.cur = off + nb
        self.n += 1
        h = self.nc.alloc_sbuf_tensor_at("%s_%s_%d" % (self.tag, name, self.n), list(shape), dt, offset=off)
        return Tile(h, off, nb, esz)

    def fork(self, tag):
        return Cursor(self.nc, self.cur, self.limit, tag)


class Ring:
    def __init__(self, tiles):
        self.t, self.i = tiles, 0

    def get(self):
        t = self.t[self.i % len(self.t)]
        self.i += 1
        return t


class PS:
    def __init__(self, h, bank):
        self.h, self.bank = h, bank

    def __getitem__(self, idx):
        return self.h[idx]

    def k(self):
        return (("ps", self.bank),)


def snd_layout(NTOK, NSB):
    o = {}
    cur = 0
    o["KM"] = cur; cur += 8 * 96 * NTOK
    o["VM"] = cur; cur += 8 * NTOK * 65
    o["KS"] = cur; cur += 8 * 64 * NTOK
    o["VS"] = cur; cur += 8 * NTOK * 64
    o["HALO"] = cur; cur += NSB * 4 * 128 * 2
    o["N"] = cur
    return o


def build(mode, NSB, LN):
    nc = bass.Bass("TRN2", target_bir_lowering=False)
    S = Sched(nc)
    NTOK = NSB * 512
    NPV = 8 + PVL * LN + 8 + 1 + 4
    PV_C, PV_L, PV_FNG = 0, 8, 8 + PVL * LN
    PV_SIGN, PV_OH = PV_FNG + 8, PV_FNG + 9
    SL = snd_layout(NTOK, NSB)
    NSND = SL["N"]
    doA = mode in ("A", "F")
    doB = mode in ("B", "F")

    def din(name, shape, dt=F32):
        return nc.dram_tensor(name, list(shape), dt, kind="ExternalInput").ap()

    def dout(name, shape, dt=F32):
        return nc.dram_tensor(name, list(shape), dt, kind="ExternalOutput").ap()

    def dint(name, shape, dt=F32):
        return nc.dram_tensor(name, list(shape), dt, kind="Internal").ap()

    def dAB(name, shape, dt):
        if mode == "A":
            return dout(name, shape, dt)
        if mode == "B":
            return din(name, shape, dt)
        return dint(name, shape, dt)

    xT_d = din("xT", [128, 8, NTOK])
    pv_d = din("pv", [128, NPV])
    adab_d = din("adab", [1, LN * 6144])
    adaw_d = din("ada_w", [LN, 1024, 6144])
    if doA:
        posi_d = din("posi", [1, NTOK], I32)
        invf_d = din("invf", [1, 128])
        win_d = din("w_in", [LN, 1024, IN_COLS])
        sgwT_d = din("sgwT", [LN, 128, 4, 128])
        sgb_d = din("sgb", [LN, 1, 4 * 512])
        wq_d = din("wq", [LN, 256, 8 * 128])
        wukv_d = din("wukv", [LN, 128, 1024])
        wkpe_d = din("wkpe", [LN, 1024, 2 * 96])
        tri_d = din("tri", [128, 128])
    if doB:
        if not doA:
            win_d = din("w_in", [LN, 1024, IN_COLS])
        wbr_d = din("w_branch", [LN, 4, 512, 1024])
        wout_d = din("w_out", [LN, 1024, 1024])
        w1_d = din("w1", [LN, 1024, 4096])
        w2_d = din("w2", [LN, 4096, 1024])
        mns_d = din("mask_ns", [16, 128, 512])
        mst_d = din("mask_s", [16, 128, 512])
        negu_d = din("negU", [128, 128])
        cw_unused = None
    snd_d = [dAB("snd%d" % l, [NSND], BF16) if mode != "B" else None for l in range(LN)]
    if mode == "B":
        gat_d = [din("gat%d" % l, [4 * NSND], BF16) for l in range(LN)]
    elif mode == "F":
        gat_d = [dint("gat%d" % l, [4 * NSND], BF16) for l in range(LN)]
    qm_d = dAB("qm", [NSB, 8, 96, 512], BF16)
    qs_d = dAB("qs", [NSB, 8, 64, 512], BF16)
    ysg_d = dAB("ysg", [128, 4, NTOK], BF16)
    ycv_d = dAB("ycv", [128, 4, NTOK], BF16)
    fx_d = dAB("fx", [128, NSB, 4, 4], F32)
    if mode == "B":
        xTo_d = dout("xTo", [128, 8, NTOK])
    if doB:
        outT_d = dout("outT", [128, 8, NTOK])

    BASE = 16896
    LIMIT = nc.SBUF_PARTITION_SIZE_BYTES
    R = Cursor(nc, BASE, LIMIT, "r")
    xT = R.alloc("xT", [128, 8, NTOK], F32)
    pv = R.alloc("pv", [128, NPV], F32)
    lay = R.alloc("lay", [128, LN * 48], F32)
    ones_bf = R.alloc("ones_bf", [128, 128], BF16)
    ones_f = R.alloc("ones_f", [128, 128], F32)
    wblk = Ring([R.alloc("wblk%d" % i, [128, 8 * 512], BF16) for i in range(3)])
    hTr = Ring([R.alloc("hT%d" % i, [128, 8, 512], BF16) for i in range(2)])
    xsq_r = Ring([R.alloc("xsq%d" % i, [128, 512], BF16) for i in range(2)])
    rstd = R.alloc("rstd", [128, 512], F32)
    fa = Ring([R.alloc("fa%d" % i, [128, 512], F32) for i in range(3)])
    if doA:
        WsT = R.alloc("WsT", [128, 4, 128], BF16)
        tri = R.alloc("tri", [128, 128], BF16)
        bsb = R.alloc("bsb", [128, 4, 512], F32)
        wq = R.alloc("wq", [128, 2, 8 * 128], BF16)
        wukv = R.alloc("wukv", [128, 1024], BF16)
        wkpe = R.alloc("wkpe", [128, 8, 192], BF16)
    if doB:
        negU = R.alloc("negU", [128, 128], BF16)
        negones = R.alloc("negones", [128, 128], BF16)
    ARENA = R.cur

    psb = []
    for i in range(8):
        cm = nc.psum_tensor("psb%d" % i, [128, 512], F32)
        psb.append(PS(cm.__enter__(), i))
    ps_main = Ring(psb[0:4])
    ps_acc = Ring(psb[4:6])
    ps_misc = Ring(psb[6:8])

    def MM(out, lhsT, rhs, start, stop, r, w, **kw):
        S.op("pe", lambda e: e.matmul(out, lhsT, rhs, start=start, stop=stop, **kw), r, w)

    def ACT(out, in_, func, r, w, **kw):
        S.op("act", lambda e: e.activation(out=out, in_=in_, func=func, **kw), r, w)

    def TT(eng, out, a, b, op, r, w):
        S.op(eng, lambda e: e.tensor_tensor(out=out, in0=a, in1=b, op=op), r, w)

    def TS(eng, out, a, s1, s2, op0, op1, r, w):
        if op1 is None:
            S.op(eng, lambda e: e.tensor_scalar(out=out, in0=a, scalar1=s1, scalar2=None, op0=op0), r, w)
        else:
            S.op(eng, lambda e: e.tensor_scalar(out=out, in0=a, scalar1=s1, scalar2=s2, op0=op0, op1=op1), r, w)

    def STT(eng, out, in0, scalar, in1, op0, op1, r, w):
        S.op(eng, lambda e: e.scalar_tensor_tensor(out=out, in0=in0, scalar=scalar, in1=in1, op0=op0, op1=op1), r, w)

    def CP(eng, out, in_, r, w):
        if eng == "act":
            S.op("act", lambda e: e.activation(out=out, in_=in_, func=AF.Identity), r, w)
        else:
            S.op(eng, lambda e: e.tensor_copy(out=out, in_=in_), r, w)

    def RECIP(out, in_, r, w):
        S.op("dve", lambda e: e.reciprocal(out=out, in_=in_), r, w)

    def MEMSET(eng, out, val, w):
        S.op(eng, lambda e: e.memset(out, val), (), w)

    def DMA(q, out, in_, r, w):
        S.dma(q, lambda e: e.dma_start(out=out, in_=in_), r, w)

    def xk(k, m):
        return xT.ke(k * NTOK + m * 512, 512)

    def layc(l, j, k):
        c = l * 48 + j * 8 + k
        return lay[:, c:c + 1]

    def pvl(l, j):
        c = PV_L + l * PVL + j
        return pv[:, c:c + 1]

    DMA("sp", pv[:], pv_d[:, :], (), pv.k())
    for k in range(8):
        DMA("sp", xT[:, k, :], xT_d[:, k, :], (), xT.ke(k * NTOK, NTOK))
    MEMSET("dve", ones_bf[:], 1.0, ones_bf.k())
    MEMSET("dve", ones_f[:], 1.0, ones_f.k())
    if doB:
        MEMSET("dve", negones[:], -1.0, negones.k())
        DMA("pool", negU[:], negu_d[:, :], (), negU.k())
    if doA:
        DMA("pool", tri[:], tri_d[:, :], (), tri.k())

    P = Cursor(nc, ARENA, LIMIT, "p")
    siluc = P.alloc("siluc", [128, 8], BF16)
    modrow = P.alloc("modrow", [1, 6144], F32)
    adab_t = P.alloc("adab", [1, 6144], F32)
    modT = P.alloc("modT", [128, LN * 48], F32)
    ACT(siluc[:], pv[:, PV_C:PV_C + 8], AF.Silu, pv.k(), siluc.k())
    adaw_v = adaw_d.rearrange("l (k p) n -> l p k n", p=128)
    for l in range(LN):
        DMA("sp", adab_t[:], adab_d[:, l * 6144:(l + 1) * 6144], (), adab_t.k())
        for nb in range(12):
            wt = wblk.get()
            wv = wt.h[:, :].rearrange("p (k n) -> p k n", k=8)
            DMA("pool", wv, adaw_v[l, :, :, nb * 512:(nb + 1) * 512], (), wt.k())
            pp = ps_main.get()
            for k in range(8):
                MM(pp[0:1, :], siluc[:, k:k + 1], wv[:, k, :], k == 0, k == 7, siluc.k() + wt.k(), pp.k())
            c0 = nb * 512
            TT("dve", modrow[0:1, c0:c0 + 512], pp[0:1, :], adab_t[0:1, c0:c0 + 512], ALU.add,
               pp.k() + adab_t.ke(c0, 512), modrow.ke(c0, 512))
        pm = ps_misc.get()
        for j in range(48):
            c0 = j * 128
            MM(pm[:, j:j + 1], modrow[0:1, c0:c0 + 128], ones_f[0:1, 0:1], True, True,
               modrow.ke(c0, 128) + ones_f.k(), pm.k())
        CP("dve", modT[:, l * 48:(l + 1) * 48], pm[:, 0:48], pm.k(), modT.k())
        b = l * 48
        for which in range(2):
            sh = modT[:, b + which * 24:b + which * 24 + 8]
            sc = modT[:, b + which * 24 + 8:b + which * 24 + 16]
            g = modT[:, b + which * 24 + 16:b + which * 24 + 24]
            gain = pv[:, PV_L + l * PVL + which * 8:PV_L + l * PVL + which * 8 + 8]
            Acol = lay[:, b + which * 24:b + which * 24 + 8]
            Bcol = lay[:, b + which * 24 + 8:b + which * 24 + 16]
            Gcol = lay[:, b + which * 24 + 16:b + which * 24 + 24]
            STT("dve", Acol, sc, 1.0, gain, ALU.add, ALU.mult, modT.k() + pv.k(), lay.k())
            CP("dve", Bcol, sh, modT.k(), lay.k())
            CP("dve", Gcol, g, modT.k(), lay.k())

    def norm_hT(l, m, which):
        hT = hTr.get()
        ss = ps_misc.get()
        for k in range(8):
            xs = xsq_r.get()
            ACT(xs[:], xT[:, k, m * 512:(m + 1) * 512], AF.Square, xk(k, m), xs.k())
            MM(ss[:], ones_bf[:], xs[:], k == 0, k == 7, ones_bf.k() + xs.k(), ss.k())
        ACT(rstd[:], ss[:], AF.Sqrt, ss.k(), rstd.k(), scale=1.0 / D, bias=EPS)
        RECIP(rstd[:], rstd[:], rstd.k(), rstd.k())
        for k in range(8):
            tmp = fa.get()
            TT("dve", tmp[:], xT[:, k, m * 512:(m + 1) * 512], rstd[:], ALU.mult, xk(k, m) + rstd.k(), tmp.k())
            ACT(hT[:, k, :], tmp[:], AF.Identity, tmp.k() + lay.k(), hT.ke(k * 512, 512),
                scale=layc(l, which * 3, k), bias=layc(l, which * 3 + 1, k))
        return hT

    def load_wblk(src_ap, ncols_total):
        wt = wblk.get()
        Pn, Kc, n = src_ap.shape[0], src_ap.shape[1], src_ap.shape[2]
        wv = wt.h[0:Pn, 0:Kc * n].rearrange("p (k n) -> p k n", k=Kc)
        DMA("pool", wv, src_ap, (), wt.k())
        return wt, wv

    win_v = win_d.rearrange("l (k p) n -> l p k n", p=128)

    def proj_fm(wt, wv, c0, M, hT, pp, prow=None):
        for k in range(8):
            MM(pp[0:M, :], wv[:, k, c0:c0 + M], hT[:, k, :], k == 0, k == 7, wt.k() + hT.k(), pp.k())

    if doA:
        A = Cursor(nc, ARENA, LIMIT, "a")
        Ct = A.alloc("C", [128, 512], F32)
        Sgt = A.alloc("Sg", [128, 512], F32)
        posi = A.alloc("posi", [1, 512], I32)
        posf = A.alloc("posf", [1, 512], F32)
        invf = A.alloc("invf", [1, 128], F32)
        angi = A.alloc("angi", [128, 512], I32)
        uT = A.alloc("uT", [128, 4, 512], BF16)
        vhat = A.alloc("vhat", [128, 4, 512], BF16)
        ysgo = A.alloc("ysgo", [128, 4, 512], BF16)
        ycvo = A.alloc("ycvo", [128, 4, 512], BF16)
        tt_r = Ring([A.alloc("tt%d" % i, [128, 514], F32) for i in range(2)])
        cq_sb = A.alloc("cq_sb", [128, 3, 512], F32)
        cn = A.alloc("cn", [128, 3, 512], BF16)
        qo_r = Ring([A.alloc("qo%d" % i, [96, 512], BF16) for i in range(2)])
        kTo = A.alloc("kTo", [96, 8, 512], BF16)
        vxo = A.alloc("vxo", [128, 4, 8 * 65], BF16)
        sbp_r = Ring([A.alloc("sbp%d" % i, [128, 512], BF16) for i in range(3)])
        vso = A.alloc("vso", [128, 4, 512], BF16)
        fxt = A.alloc("fxt", [128, 4, 4], F32)
        halo_o = A.alloc("halo_o", [128, 4, 2], BF16)
        mvst = A.alloc("mvst", [128, 8], F32)
        brow = A.alloc("brow", [1, 512], F32)
        fb = Ring([A.alloc("fb%d" % i, [128, 512], F32) for i in range(3)])

        DMA("sp", invf[:], invf_d[:, :], (), invf.k())
        MEMSET("dve", vxo[:], 1.0, vxo.k())
        for i in range(2):
            t_ = tt_r.t[i]
            MEMSET("dve", t_[:, 0:2], 0.0, t_.k())

    snd_keys = [[] for _ in range(LN)]

    def sk(l):
        k_ = ("snd", l, len(snd_keys[l]))
        snd_keys[l].append(k_)
        return [k_]

    def phaseA(l):
        sgw_t, sgw_v = load_wblk(sgwT_d[l, :, :, :], 0)
        for g in range(4):
            TT("dve", WsT[:, g, :], sgw_v[:, g, :], tri[:], ALU.mult, sgw_t.k() + tri.k(), WsT.k())
        for g in range(4):
            DMA("sp", brow[:], sgb_d[l, :, g * 512:(g + 1) * 512], (), brow.k())
            pp = ps_misc.get()
            MM(pp[:], ones_f[0:1, :], brow[0:1, :], True, True, ones_f.k() + brow.k(), pp.k())
            CP("act", bsb[:, g, :], pp[:], pp.k(), bsb.ke(g * 512, 512))
        DMA("pool", wq[:], wq_d[l].rearrange("(k p) n -> p k n", p=128), (), wq.k())
        DMA("pool", wukv[:], wukv_d[l, :, :], (), wukv.k())
        DMA("pool", wkpe[:], wkpe_d[l].rearrange("(k p) n -> p k n", p=128), (), wkpe.k())
        snd = snd_d[l]
        for m in range(NSB):
            t0 = m * 512
            DMA("sp", posi[:], posi_d[:, t0:t0 + 512], (), posi.k())
            CP("dve", posf[:], posi[:], posi.k(), posf.k())
            pa = ps_misc.get()
            MM(pa[:], invf[0:1, :], posf[0:1, :], True, True, invf.k() + posf.k(), pa.k())
            for (dst, shift) in ((Sgt, 0.0), (Ct, 0.25)):
                y_ = fb.get()
                TS("dve", y_[:], pa[:], 1.0 / (2 * np.pi), shift, ALU.mult, ALU.add, pa.k(), y_.k())
                CP("dve", angi[:], y_[:], y_.k(), angi.k())
                y2 = fb.get()
                CP("dve", y2[:], angi[:], angi.k(), y2.k())
                TT("dve", y_[:], y_[:], y2[:], ALU.subtract, y_.k() + y2.k(), y_.k())
                ACT(dst[:], y_[:], AF.Sin, y_.k(), dst.k(), scale=float(2 * np.pi))
            TS("dve", Sgt[:], Sgt[:], pv[:, PV_SIGN:PV_SIGN + 1], None, ALU.mult, None, Sgt.k() + pv.k(), Sgt.k())

            hT = norm_hT(l, m, 0)
            wt, wv = load_wblk(win_v[l, :, :, COL_SG:COL_SG + 512], 0)
            for j in range(4):
                pp = ps_main.get()
                proj_fm(wt, wv, j * 128, 128, hT, pp)
                ACT(uT[:, j, :], pp[:], AF.Gelu, pp.k(), uT.ke(j * 512, 512))
            wt, wv = load_wblk(win_v[l, :, :, COL_SG + 512:COL_SG + 1024], 0)
            for r in range(4):
                pp = ps_main.get()
                for k in range(8):
                    MM(pp[:], hT[:, k, r * 128:(r + 1) * 128], wv[:, k, :], k == 0, k == 7, wt.k() + hT.k(), pp.k())
                g_ = fb.get()
                ACT(g_[:], pp[:], AF.Gelu, pp.k(), g_.k())
                S.op("dve", (lambda g_: lambda e: e.bn_stats(out=mvst[:, 0:6], in_=g_[:]))(g_), g_.k(), mvst.k())
                S.op("dve", lambda e: e.bn_aggr(out=mvst[:, 6:8], in_=mvst[:, 0:6]), mvst.k(), mvst.k())
                ACT(mvst[:, 7:8], mvst[:, 7:8], AF.Sqrt, mvst.k(), mvst.k(), bias=EPS)
                RECIP(mvst[:, 7:8], mvst[:, 7:8], mvst.k(), mvst.k())
                TS("dve", vhat[:, r, :], g_[:], mvst[:, 6:7], mvst[:, 7:8], ALU.subtract, ALU.mult,
                   g_.k() + mvst.k(), vhat.ke(r * 512, 512))
            for g in range(4):
                pp = ps_main.get()
                for r in range(4):
                    MM(pp[:, r * 128:(r + 1) * 128], vhat[:, r, g * 128:(g + 1) * 128], WsT[:, g, :], True, True,
                       vhat.k() + WsT.k(), pp.k())
                tmp = fb.get()
                STT("dve", tmp[:], pp[:], pvl(l, 16 + g), bsb[:, g, :], ALU.mult, ALU.add,
                    pp.k() + pv.k() + bsb.ke(g * 512, 512), tmp.k())
                TT("pool", ysgo[:, g, :], tmp[:], uT[:, g, :], ALU.mult, tmp.k() + uT.ke(g * 512, 512), ysgo.ke(g * 512, 512))
            DMA("sp", ysg_d[:, :, t0:t0 + 512], ysgo[:], ysgo.k(), [("ysg", l, m)])
            wts = [load_wblk(win_v[l, :, :, COL_CONV + i * 512:COL_CONV + (i + 1) * 512], 0) for i in range(3)]
            for j in range(4):
                pgc = ps_main.get()
                proj_fm(wts[1][0], wts[1][1], j * 128, 128, hT, pgc)
                gc = fb.get()
                CP("act", gc[:], pgc[:], pgc.k(), gc.k())
                pxv = ps_main.get()
                proj_fm(wts[2][0], wts[2][1], j * 128, 128, hT, pxv)
                t_ = tt_r.get()
                TT("dve", t_[:, 2:514], gc[:], pxv[:], ALU.mult, gc.k() + pxv.k(), t_.k())
                acc = fb.get()
                TS("dve", acc[:], t_[:, 2:514], pvl(l, 20 + 8 + j), None, ALU.mult, None, t_.k() + pv.k(), acc.k())
                STT("dve", acc[:], t_[:, 1:513], pvl(l, 20 + 4 + j), acc[:], ALU.mult, ALU.add, t_.k() + pv.k() + acc.k(), acc.k())
                STT("dve", acc[:], t_[:, 0:512], pvl(l, 20 + j), acc[:], ALU.mult, ALU.add, t_.k() + pv.k() + acc.k(), acc.k())
                pgb = ps_main.get()
                proj_fm(wts[0][0], wts[0][1], j * 128, 128, hT, pgb)
                TT("dve", ycvo[:, j, :], acc[:], pgb[:], ALU.mult, acc.k() + pgb.k(), ycvo.ke(j * 512, 512))
                CP("pool", fxt[:, j, 0:2], acc[:, 0:2], acc.k(), fxt.k())
                CP("dve", fxt[:, j, 2:4], pgb[:, 0:2], pgb.k(), fxt.k())
                CP("pool", halo_o[:, j, :], t_[:, 512:514], t_.k(), halo_o.k())
            DMA("sp", ycv_d[:, :, t0:t0 + 512], ycvo[:], ycvo.k(), [("ycv", l, m)])
            DMA("sp", fx_d[:, m, :, :], fxt[:], fxt.k(), [("fx", l, m)])
            ho = SL["HALO"] + m * 1024
            DMA("sp", snd[ho:ho + 1024].rearrange("(j p e) -> p j e", j=4, p=128), halo_o[:], halo_o.k(), sk(l))
            wt, wv = load_wblk(win_v[l, :, :, COL_MLA:COL_MLA + 384], 0)
            for j in range(3):
                pp = ps_main.get()
                proj_fm(wt, wv, j * 128, 128, hT, pp)
                CP("act", cq_sb[:, j, :], pp[:], pp.k(), cq_sb.ke(j * 512, 512))
            for (j0, nj, gcol) in ((0, 2, 32), (2, 1, 34)):
                ss = ps_misc.get()
                for j in range(j0, j0 + nj):
                    xs = xsq_r.get()
                    ACT(xs[:], cq_sb[:, j, :], AF.Square, cq_sb.ke(j * 512, 512), xs.k())
                    MM(ss[:], ones_bf[:], xs[:], j == j0, j == j0 + nj - 1, ones_bf.k() + xs.k(), ss.k())
                rs_ = fb.get()
                ACT(rs_[:], ss[:], AF.Sqrt, ss.k(), rs_.k(), scale=1.0 / (128 * nj), bias=EPS)
                RECIP(rs_[:], rs_[:], rs_.k(), rs_.k())
                for j in range(j0, j0 + nj):
                    STT("dve", cn[:, j, :], cq_sb[:, j, :], pvl(l, gcol + (j - j0)), rs_[:], ALU.mult, ALU.mult,
                        cq_sb.ke(j * 512, 512) + pv.k() + rs_.k(), cn.ke(j * 512, 512))
            pka = ps_main.get()
            pkb = ps_main.get()
            for k in range(8):
                MM(pka[0:96, :], wkpe[:, k, 0:96], hT[:, k, :], k == 0, k == 7, wkpe.k() + hT.k(), pka.k())
            for k in range(8):
                MM(pkb[0:96, :], wkpe[:, k, 96:192], hT[:, k, :], k == 0, k == 7, wkpe.k() + hT.k(), pkb.k())
            t1 = fb.get()
            t2 = fb.get()
            TT("dve", t1[64:96, :], pka[64:96, :], Ct[64:96, :], ALU.mult, pka.k() + Ct.k(), t1.k())
            TT("dve", t2[64:96, :], pkb[64:96, :], Sgt[64:96, :], ALU.mult, pkb.k() + Sgt.k(), t2.k())
            for h in range(8):
                TT("pool", kTo[64:96, h, :], t1[64:96, :], t2[64:96, :], ALU.add, t1.k() + t2.k(), kTo.ke(h * 512, 512))
            for h in range(8):
                pqa = ps_main.get()
                pqb = ps_main.get()
                for k in range(2):
                    MM(pqa[0:96, :], wq[:, k, h * 128:h * 128 + 96], cn[:, k, :], k == 0, k == 1, wq.k() + cn.k(), pqa.k())
                for k in range(2):
                    MM(pqb[0:96, :], wq[:, k, h * 128 + 32:h * 128 + 128], cn[:, k, :], k == 0, k == 1, wq.k() + cn.k(), pqb.k())
                qo = qo_r.get()
                CP("act", qo[0:64, :], pqa[0:64, :], pqa.k(), qo.k())
                t1 = fb.get()
                t2 = fb.get()
                TT("dve", t1[64:96, :], pqa[64:96, :], Ct[64:96, :], ALU.mult, pqa.k() + Ct.k(), t1.k())
                TT("dve", t2[64:96, :], pqb[64:96, :], Sgt[64:96, :], ALU.mult, pqb.k() + Sgt.k(), t2.k())
                TT("pool", qo[64:96, :], t1[64:96, :], t2[64:96, :], ALU.add, t1.k() + t2.k(), qo.k())
                DMA("sp", qm_d[m, h, :, :], qo[:], qo.k(), [("qm", l, m, h)])
                pkn = ps_main.get()
                MM(pkn[0:64, :], wukv[:, h * 128:h * 128 + 64], cn[:, 2, :], True, True, wukv.k() + cn.k(), pkn.k())
                CP("act", kTo[0:64, h, :], pkn[0:64, :], pkn.k(), kTo.ke(h * 512, 512))
            DMA("sp", snd[SL["KM"]:SL["KM"] + 8 * 96 * NTOK].rearrange("(h r t) -> r h t", h=8, r=96)[:, :, t0:t0 + 512],
                kTo[:], kTo.k(), sk(l))
            wv_v = wukv.h[:, :].rearrange("p (h e) -> p h e", h=8)[:, :, 64:128]
            for r in range(4):
                pp = ps_main.get()
                MM(pp[:].rearrange("p (h e) -> p h e", h=8), cn[:, 2, r * 128:(r + 1) * 128], wv_v, True, True,
                   wukv.k() + cn.k(), pp.k())
                CP("act", vxo[:, r, :].rearrange("p (h e) -> p h e", h=8)[:, :, 0:64],
                   pp[:].rearrange("p (h e) -> p h e", h=8), pp.k(), vxo.ke(r * 520, 520))
            vm = snd[SL["VM"]:SL["VM"] + 8 * NTOK * 65].rearrange("(h t e) -> t h e", h=8, e=65)
            for r in range(4):
                DMA("sp", vm[t0 + r * 128:t0 + (r + 1) * 128, :, :], vxo[:, r, :].rearrange("p (h e) -> p h e", h=8),
                    vxo.ke(r * 520, 520), sk(l))
            for part in range(2):
                wt, wv = load_wblk(win_v[l, :, :, COL_SB + part * 512:COL_SB + (part + 1) * 512], 0)
                for j in range(4):
                    pp = ps_main.get()
                    proj_fm(wt, wv, j * 128, 128, hT, pp)
                    sp_ = sbp_r.get()
                    if part == 0:
                        ACT(sp_[:], pp[:], AF.Identity, pp.k(), sp_.k(), scale=0.125)
                        for hh in range(2):
                            DMA("sp", qs_d[m, 2 * j + hh, :, :], sp_[hh * 64:(hh + 1) * 64, :], sp_.k(), [("qs", l, m, 2 * j + hh)])
                    else:
                        CP("act", sp_[:], pp[:], pp.k(), sp_.k())
                        ks = snd[SL["KS"]:SL["KS"] + 8 * 64 * NTOK].rearrange("(h r t) -> h r t", h=8, r=64)
                        for hh in range(2):
                            DMA("sp", ks[2 * j + hh, :, t0:t0 + 512], sp_[hh * 64:(hh + 1) * 64, :], sp_.k(), sk(l))
            wt, wv = load_wblk(win_v[l, :, :, COL_SB + 1024:COL_SB + 1536], 0)
            for r in range(4):
                pp = ps_main.get()
                for k in range(8):
                    MM(pp[:], hT[:, k, r * 128:(r + 1) * 128], wv[:, k, :], k == 0, k == 7, wt.k() + hT.k(), pp.k())
                CP("act", vso[:, r, :], pp[:], pp.k(), vso.ke(r * 512, 512))
            vs = snd[SL["VS"]:SL["VS"] + 8 * NTOK * 64].rearrange("(h t e) -> t h e", h=8, e=64)
            for r in range(4):
                DMA("sp", vs[t0 + r * 128:t0 + (r + 1) * 128, :, :], vso[:, r, :].rearrange("p (h e) -> p h e", h=8),
                    vso.ke(r * 512, 512), sk(l))

    if doB:
        B = Cursor(nc, ARENA, LIMIT, "b")
        ymla = B.alloc("ymla", [64, 8, 512], BF16)
        ysb = B.alloc("ysb", [64, 8, 512], BF16)
        ysg_l = B.alloc("ysg_l", [128, 4, 512], BF16)
        ycv_l = B.alloc("ycv_l", [128, 4, 512], BF16)
        fx_l = B.alloc("fx_l", [128, 4, 4], F32)
        hsel = B.alloc("hsel", [128, 4, 4, 2], BF16)
        hf = B.alloc("hf", [128, 4, 8], F32)
        X = B.fork("x")
        kc_r = Ring([X.alloc("kc%d" % i, [96, 2048], BF16) for i in range(2)])
        vc_r = Ring([X.alloc("vc%d" % i, [128, 16, 65], BF16) for i in range(2)])
        qT_r = Ring([X.alloc("qT%d" % i, [96, 512], BF16) for i in range(2)])
        pa_r = Ring([X.alloc("pa%d" % i, [128, 512], BF16) for i in range(3)])
        e_r = Ring([X.alloc("e%d" % i, [128, 512], F32) for i in range(2)])
        lp_r = Ring([X.alloc("lp%d" % i, [128, 512], BF16) for i in range(3)])
        lsum_r = Ring([X.alloc("lsum%d" % i, [128, 512], F32) for i in range(2)])
        lsbf_r = Ring([X.alloc("lsbf%d" % i, [128, 512], BF16) for i in range(2)])
        masks_t = X.alloc("masks", [128, 16, 512], BF16)
        rs_t = X.alloc("rs_t", [128, 512], F32)
        bc_t = X.alloc("bc_t", [64, 512], F32)
        Y = B.fork("y")
        macc = Y.alloc("macc", [128, 4, 512], F32)
        sig_r = Ring([Y.alloc("sig%d" % i, [128, 512], F32) for i in range(2)])
        prod_r = Ring([Y.alloc("prod%d" % i, [128, 512], F32) for i in range(2)])
        mergedT = Y.alloc("mergedT", [128, 8, 512], BF16)
        Z = B.fork("z")
        aT = Z.alloc("aT", [128, 32, 512], BF16)
        rl_r = Ring([Z.alloc("rl%d" % i, [128, 512], BF16) for i in range(2)])

    def attention(l, m, kind):
        gat = gat_d[l]
        nkb = 16 * m + 16
        nch = m + 1
        if kind == "mla":
            KOFF, VOFF, KR, VE = SL["KM"], SL["VM"], 96, 65
            mask_d = mns_d
        else:
            KOFF, VOFF, KR, VE = SL["KS"], SL["VS"], 64, 64
            mask_d = mst_d
        gv = gat.rearrange("(c n) -> c n", c=4)
        kview = gv[:, KOFF:KOFF + 8 * KR * NTOK].rearrange("c (h r t) -> h r c t", h=8, r=KR)
        vview = gv[:, VOFF:VOFF + 8 * NTOK * VE].rearrange("c (h t e) -> h t c e", h=8, e=VE)
        DMA("pool", masks_t[:], mask_d.rearrange("j s t -> s j t"), (), masks_t.k())
        for h in range(8):
            qT = qT_r.get()
            if kind == "mla":
                DMA("sp", qT[0:96, :], qm_d[m, h, :, :], [("qm", l, m, h)], qT.k())
            else:
                DMA("sp", qT[0:64, :], qs_d[m, h, :, :], [("qs", l, m, h)], qT.k())
            O = ps_acc.get()
            first = True
            lsum = lsum_r.get() if kind == "sb" else None
            order = range(nch - 1, -1, -1) if kind == "sb" else range(nch)
            for ch in order:
                kc = kc_r.get()
                vc = vc_r.get()
                DMA("sp", kc.h[0:KR, :].rearrange("r (c t) -> r c t", c=4),
                    kview[h, :, :, ch * 512:(ch + 1) * 512], [("gat", l)], kc.k())
                for c_ in range(4):
                    DMA("sp", vc.h[:, c_ * 4:(c_ + 1) * 4, 0:VE],
                        vview[h, ch * 512:(ch + 1) * 512, c_, :].rearrange("(j p) e -> p j e", p=128),
                        [("gat", l)], vc.k())
                kbs = range(15, -1, -1) if kind == "sb" else range(16)
                for kb in kbs:
                    kbg = ch * 16 + kb
                    masked = kbg >= nkb - 16
                    lastk = (kbg == 0) if kind == "sb" else (kbg == nkb - 1)
                    mj = kbg - (nkb - 16)
                    Sp = ps_main.get()
                    if kind == "mla":
                        MM(Sp[:], kc[0:96, kb * 128:(kb + 1) * 128], qT[0:96, :], True, True, kc.k() + qT.k(), Sp.k())
                        Pt = pa_r.get()
                        ACT(Pt[:], Sp[:], AF.Exp, Sp.k(), Pt.k(), scale=float(96 ** -0.5))
                        if masked:
                            TT("dve", Pt[:], Pt[:], masks_t[:, mj, :], ALU.mult, Pt.k() + masks_t.k(), Pt.k())
                        MM(O[0:65, :], vc[:, kb, 0:65], Pt[:], first, lastk, vc.k() + Pt.k(), O.k())
                    else:
                        MM(Sp[:], kc[0:64, kb * 128:(kb + 1) * 128], qT[0:64, :], True, False, kc.k() + qT.k(), Sp.k())
                        E = e_r.get()
                        ACT(E[:], Sp[:], AF.Exp, Sp.k(), E.k())
                        Lp = lp_r.get()
                        ACT(Lp[:], E[:], AF.Ln, E.k(), Lp.k(), bias=1.0)
                        if masked:
                            TT("dve", Lp[:], Lp[:], masks_t[:, mj, :], ALU.mult, Lp.k() + masks_t.k(), Lp.k())
                        MM(Sp[:], negU[:], Lp[:], False, first, negU.k() + Lp.k(), Sp.k(), skip_group_check=True)
                        if not first:
                            lsbf = lsbf_r.get()
                            CP("pool", lsbf[:], lsum[:], lsum.k(), lsbf.k())
                            MM(Sp[:], negones[:], lsbf[:], False, True, negones.k() + lsbf.k(), Sp.k(), skip_group_check=True)
                            TT("dve", lsum[:], lsum[:], Lp[:], ALU.add, lsum.k() + Lp.k(), lsum.k())
                        else:
                            CP("dve", lsum[:], Lp[:], Lp.k(), lsum.k())
                        At = pa_r.get()
                        ACT(At[:], Sp[:], AF.Exp, Sp.k(), At.k())
                        if masked:
                            TT("pool", At[:], At[:], masks_t[:, mj, :], ALU.mult, At.k() + masks_t.k(), At.k())
                        MM(O[0:64, :], vc[:, kb, 0:64], At[:], first, lastk, vc.k() + At.k(), O.k())
                    first = False
            if kind == "mla":
                CP("act", rs_t[64:65, :], O[64:65, :], O.k(), rs_t.k())
                RECIP(rs_t[64:65, :], rs_t[64:65, :], rs_t.k(), rs_t.k())
                pb = ps_misc.get()
                MM(pb[0:64, :], ones_f[64:65, 0:64], rs_t[64:65, :], True, True, ones_f.k() + rs_t.k(), pb.k())
                CP("act", bc_t[:], pb[0:64, :], pb.k(), bc_t.k())
                TT("dve", ymla[:, h, :], O[0:64, :], bc_t[:], ALU.mult, O.k() + bc_t.k(), ymla.ke(h * 512, 512))
            else:
                CP("act", ysb[:, h, :], O[0:64, :], O.k(), ysb.ke(h * 512, 512))

    def phaseB(l, last):
        gat = gat_d[l]
        wbr_v = wbr_d
        for m in range(NSB):
            t0 = m * 512
            DMA("sp", ysg_l[:], ysg_d[:, :, t0:t0 + 512], [("ysg", l, m)], ysg_l.k())
            DMA("sp", ycv_l[:], ycv_d[:, :, t0:t0 + 512], [("ycv", l, m)], ycv_l.k())
            DMA("sp", fx_l[:], fx_d[:, m, :, :], [("fx", l, m)], fx_l.k())
            gv = gat.rearrange("(c n) -> c n", c=4)
            hv = gv[:, SL["HALO"]:SL["HALO"] + NSB * 1024].rearrange("c (m j p e) -> c m p j e", m=NSB, j=4, p=128)
            MEMSET("dve", hsel[:], 0.0, hsel.k())
            for q in range(4):
                if q == 0:
                    if m == 0:
                        continue
                    src = hv[3, m - 1]
                else:
                    src = hv[q - 1, m]
                DMA("sp", hsel[:, q, :, :], src, [("gat", l)], hsel.k())
            MEMSET("dve", hf[:], 0.0, hf.k())
            for q in range(4):
                STT("dve", hf[:, :, 0:2], hsel[:, q, :, :], pv[:, PV_OH + q:PV_OH + q + 1], hf[:, :, 0:2], ALU.mult, ALU.add,
                    hsel.k() + pv.k() + hf.k(), hf.k())
            for j in range(4):
                w0, w1 = pvl(l, 20 + j), pvl(l, 24 + j)
                TS("dve", hf[:, j, 2:3], hf[:, j, 1:2], w1, None, ALU.mult, None, hf.k() + pv.k(), hf.k())
                STT("dve", hf[:, j, 2:3], hf[:, j, 0:1], w0, hf[:, j, 2:3], ALU.mult, ALU.add, hf.k() + pv.k(), hf.k())
                TS("dve", hf[:, j, 3:4], hf[:, j, 1:2], w0, None, ALU.mult, None, hf.k() + pv.k(), hf.k())
                TT("dve", hf[:, j, 2:4], hf[:, j, 2:4], fx_l[:, j, 0:2], ALU.add, hf.k() + fx_l.k(), hf.k())
                TT("dve", ycv_l[:, j, 0:2], hf[:, j, 2:4], fx_l[:, j, 2:4], ALU.mult, hf.k() + fx_l.k(), ycv_l.ke(j * 512, 512))
            attention(l, m, "mla")
            attention(l, m, "sb")
            hT = norm_hT(l, m, 0)
            for grp in range(2):
                for nbr in range(4):
                    gt, gvw = load_wblk(win_v[l, :, :, COL_GATE + nbr * 1024 + grp * 512:COL_GATE + nbr * 1024 + (grp + 1) * 512], 0)
                    if nbr < 2:
                        bt, bvw = load_wblk(wbr_v[l, nbr].rearrange("(k p) n -> p k n", p=128)[:, :, grp * 512:(grp + 1) * 512], 0)
                    else:
                        bt, bvw = load_wblk(wbr_v[l, nbr].rearrange("(h p) n -> p h n", p=64)[:, :, grp * 512:(grp + 1) * 512], 0)
                    for dcl in range(4):
                        pg = ps_main.get()
                        proj_fm(gt, gvw, dcl * 128, 128, hT, pg)
                        sg_ = sig_r.get()
                        ACT(sg_[:], pg[:], AF.Sigmoid, pg.k(), sg_.k())
                        pu = ps_main.get()
                        if nbr < 2:
                            src = ysg_l if nbr == 0 else ycv_l
                            for k in range(4):
                                MM(pu[:], bvw[:, k, dcl * 128:(dcl + 1) * 128], src[:, k, :], k == 0, k == 3, bt.k() + src.k(), pu.k())
                        else:
                            src = ymla if nbr == 2 else ysb
                            for hh in range(8):
                                MM(pu[:], bvw[0:64, hh, dcl * 128:(dcl + 1) * 128], src[:, hh, :], hh == 0, hh == 7, bt.k() + src.k(), pu.k())
                        if nbr == 0:
                            TT("dve", macc[:, dcl, :], sg_[:], pu[:], ALU.mult, sg_.k() + pu.k(), macc.ke(dcl * 512, 512))
                        else:
                            pr = prod_r.get()
                            TT("dve", pr[:], sg_[:], pu[:], ALU.mult, sg_.k() + pu.k(), pr.k())
                            if nbr < 3:
                                TT("pool", macc[:, dcl, :], macc[:, dcl, :], pr[:], ALU.add, macc.ke(dcl * 512, 512) + pr.k(), macc.ke(dcl * 512, 512))
                            else:
                                dc = grp * 4 + dcl
                                TT("pool", mergedT[:, dc, :], macc[:, dcl, :], pr[:], ALU.add, macc.ke(dcl * 512, 512) + pr.k(), mergedT.ke(dc * 512, 512))
            for half in range(2):
                wt, wv = load_wblk(wout_d[l].rearrange("(k p) n -> p k n", p=128)[:, :, half * 512:(half + 1) * 512], 0)
                for dcl in range(4):
                    dc = half * 4 + dcl
                    pp = ps_main.get()
                    for k in range(8):
                        MM(pp[:], wv[:, k, dcl * 128:(dcl + 1) * 128], mergedT[:, k, :], k == 0, k == 7, wt.k() + mergedT.k(), pp.k())
                    STT("dve", xT[:, dc, t0:t0 + 512], pp[:], layc(l, 2, dc), xT[:, dc, t0:t0 + 512], ALU.mult, ALU.add,
                        pp.k() + lay.k() + xk(dc, m), xk(dc, m))
            h2 = norm_hT(l, m, 1)
            for nb in range(8):
                wt, wv = load_wblk(w1_d[l].rearrange("(k p) n -> p k n", p=128)[:, :, nb * 512:(nb + 1) * 512], 0)
                for j in range(4):
                    pp = ps_main.get()
                    proj_fm(wt, wv, j * 128, 128, h2, pp)
                    rl = rl_r.get()
                    ACT(rl[:], pp[:], AF.Relu, pp.k(), rl.k())
                    fi = nb * 4 + j
                    TT("pool", aT[:, fi, :], rl[:], rl[:], ALU.mult, rl.k(), aT.ke(fi * 512, 512))
            for dc in range(8):
                wt = wblk.get()
                wv = wt.h[:, :].rearrange("p (k n) -> p k n", k=32)
                DMA("pool", wv, w2_d[l].rearrange("(k p) n -> p k n", p=128)[:, :, dc * 128:(dc + 1) * 128], (), wt.k())
                pp = ps_main.get()
                for k in range(32):
                    MM(pp[:], wv[:, k, :], aT[:, k, :], k == 0, k == 31, wt.k() + aT.ke(k * 512, 512), pp.k())
                STT("dve", xT[:, dc, t0:t0 + 512], pp[:], layc(l, 5, dc), xT[:, dc, t0:t0 + 512], ALU.mult, ALU.add,
                    pp.k() + lay.k() + xk(dc, m), xk(dc, m))
            if mode == "B":
                for k in range(8):
                    DMA("sp", xTo_d[:, k, t0:t0 + 512], xT[:, k, t0:t0 + 512], xk(k, m), [("xTo", k, m)])
            if last:
                ss = ps_misc.get()
                for k in range(8):
                    xs = xsq_r.get()
                    ACT(xs[:], xT[:, k, t0:t0 + 512], AF.Square, xk(k, m), xs.k())
                    MM(ss[:], ones_bf[:], xs[:], k == 0, k == 7, ones_bf.k() + xs.k(), ss.k())
                ACT(rstd[:], ss[:], AF.Sqrt, ss.k(), rstd.k(), scale=1.0 / D, bias=EPS)
                RECIP(rstd[:], rstd[:], rstd.k(), rstd.k())
                for k in range(8):
                    tmp = fa.get()
                    STT("dve", tmp[:], xT[:, k, t0:t0 + 512], pv[:, PV_FNG + k:PV_FNG + k + 1], rstd[:], ALU.mult, ALU.mult,
                        xk(k, m) + pv.k() + rstd.k(), tmp.k())
                    DMA("sp", outT_d[:, k, t0:t0 + 512], tmp[:], tmp.k(), [("outT", k, m)])

    for l in range(LN):
        if doA:
            phaseA(l)
        if mode == "F":
            S.dma("pool", (lambda l: lambda e: e.collective_compute(
                "AllGather", ALU.bypass, replica_groups=[[0, 1, 2, 3], [4, 5, 6, 7]],
                ins=[snd_d[l][:]], outs=[gat_d[l][:]]))(l), snd_keys[l], [("gat", l)])
        if doB:
            phaseB(l, last=(mode == "B" or l == LN - 1))
    S.emit()
    return nc, S.stats


def _consts():
    tri = np.triu(np.ones((128, 128), np.float32))
    negU = -np.tril(np.ones((128, 128), np.float32))
    return tri, negU


def _masks(c):
    tri_ns = np.triu(np.ones((128, 128), np.float32))
    tri_s = np.triu(np.ones((128, 128), np.float32), 1)
    mns = np.zeros((16, 128, 512), np.float32)
    mst = np.zeros((16, 128, 512), np.float32)
    for j in range(16):
        for r in range(4):
            qb = 4 * c + r
            if j < qb:
                mns[j, :, r * 128:(r + 1) * 128] = 1.0
                mst[j, :, r * 128:(r + 1) * 128] = 1.0
            elif j == qb:
                mns[j, :, r * 128:(r + 1) * 128] = tri_ns
                mst[j, :, r * 128:(r + 1) * 128] = tri_s
    return mns, mst


def _chunkcols(v):
    return np.ascontiguousarray(v.reshape(-1, 128).T)


def make_pv(inp, b, c, layers):
    cols = [_chunkcols(inp["c"][b])]
    for l in layers:
        cols.append(_chunkcols(inp["norm1_g"][l]))
        cols.append(_chunkcols(inp["norm2_g"][l]))
        cols.append(_chunkcols(inp["sg_norm_g"][l]))
        for j in range(3):
            cols.append(_chunkcols(inp["conv_w"][l, j]))
        cols.append(_chunkcols(inp["mla_q_norm_g"][l]))
        cols.append(_chunkcols(inp["mla_kv_norm_g"][l]))
    cols.append(_chunkcols(inp["final_norm_g"]))
    sign = np.where((np.arange(128) % 32) < 16, -1.0, 1.0).astype(np.float32)[:, None]
    cols.append(sign)
    oh = np.zeros((128, 4), np.float32)
    oh[:, c] = 1.0
    cols.append(oh)
    return np.ascontiguousarray(np.concatenate(cols, axis=1).astype(np.float32))


def layer_weights(inp, layers):
    ls = list(layers)
    w = {}
    w["ada_w"] = np.ascontiguousarray(inp["ada_w"][ls])
    w["adab"] = np.ascontiguousarray(inp["ada_b"][ls].reshape(1, -1))
    w["w_in"] = np.ascontiguousarray(inp["w_in"][ls])
    w["sgwT"] = np.ascontiguousarray(np.transpose(inp["sg_w"][ls], (0, 3, 1, 2)))
    w["sgb"] = np.ascontiguousarray(np.tile(inp["sg_b"][ls][:, :, None, :], (1, 1, 4, 1)).reshape(len(ls), 1, 4 * 512))
    uq = inp["mla_w_uq"][ls].reshape(len(ls), 256, 8, 96)
    pe = uq[..., 64:96]
    pesw = np.concatenate([pe[..., 16:32], pe[..., 0:16]], axis=-1)
    w["wq"] = np.ascontiguousarray(np.concatenate([uq, pesw], axis=-1).reshape(len(ls), 256, 8 * 128))
    w["wukv"] = np.ascontiguousarray(inp["mla_w_ukv"][ls])
    kpe = inp["w_in"][ls][:, :, COL_MLA + 384:COL_MLA + 416]
    kpesw = np.concatenate([kpe[..., 16:32], kpe[..., 0:16]], axis=-1)
    z = np.zeros(kpe.shape[:2] + (64,), np.float32)
    w["wkpe"] = np.ascontiguousarray(np.concatenate([z, kpe, z, kpesw], axis=-1))
    w["w_branch"] = np.ascontiguousarray(inp["w_branch"][ls])
    w["w_out"] = np.ascontiguousarray(inp["w_out"][ls])
    w["w1"] = np.ascontiguousarray(inp["mlp_w1"][ls])
    w["w2"] = np.ascontiguousarray(inp["mlp_w2"][ls])
    return w


A_KEYS = ["ada_w", "adab", "w_in", "sgwT", "sgb", "wq", "wukv", "wkpe"]
B_KEYS = ["ada_w", "adab", "w_in", "w_branch", "w_out", "w1", "w2"]

_cache = {}


def _get(mode, NSB, LN):
    key = (mode, NSB, LN)
    if key not in _cache:
        _cache[key] = build(mode, NSB, LN)
    return _cache[key][0]


def run_model(inp, NSB, depth, fused=False):
    x = np.asarray(inp["x"], np.float32)
    Bsz, SEQ, _ = x.shape
    NTOK = NSB * 512
    assert SEQ == 4 * NTOK and Bsz == 2
    inv_freq = (10000.0 ** (-np.arange(0, 32, 2, dtype=np.float32) / 32)).astype(np.float32)
    invf = np.tile(inv_freq, 8)[None, :].astype(np.float32)
    tri, negU = _consts()
    cores = [(b, c) for b in range(2) for c in range(4)]

    def tok_idx(c):
        return np.concatenate([np.arange((4 * m + c) * 512, (4 * m + c + 1) * 512) for m in range(NSB)])

    xTs = []
    for (b, c) in cores:
        xt = x[b, tok_idx(c), :].T
        xTs.append(np.ascontiguousarray(xt.reshape(8, 128, NTOK).transpose(1, 0, 2)))
    posis = [np.ascontiguousarray(np.asarray(inp["positions"])[b, tok_idx(c)][None, :].astype(np.int32)) for (b, c) in cores]
    masks = [_masks(c) for (b, c) in cores]
    outT = None
    if fused:
        w = layer_weights(inp, range(depth))
        nc = _get("F", NSB, depth)
        in_maps = []
        for i, (b, c) in enumerate(cores):
            d = dict(xT=xTs[i], pv=make_pv(inp, b, c, range(depth)), posi=posis[i], invf=invf, tri=tri, negU=negU,
                     mask_ns=masks[i][0], mask_s=masks[i][1])
            for k_ in set(A_KEYS + B_KEYS):
                d[k_] = w[k_]
            in_maps.append(d)
        res = run_bass_kernel_spmd(nc, in_maps, core_ids=list(range(8)))
        outT = [r["outT"] for r in res.results]
    else:
        ncA = _get("A", NSB, 1)
        ncB = _get("B", NSB, 1)
        for l in range(depth):
            w = layer_weights(inp, [l])
            in_maps = []
            for i, (b, c) in enumerate(cores):
                d = dict(xT=xTs[i], pv=make_pv(inp, b, c, [l]), posi=posis[i], invf=invf, tri=tri)
                for k_ in A_KEYS:
                    d[k_] = w[k_]
                in_maps.append(d)
            resA = run_bass_kernel_spmd(ncA, in_maps, core_ids=list(range(8))).results
            in_maps = []
            for i, (b, c) in enumerate(cores):
                gat = np.concatenate([resA[b * 4 + cc]["snd0"] for cc in range(4)])
                d = dict(xT=xTs[i], pv=make_pv(inp, b, c, [l]), negU=negU, mask_ns=masks[i][0], mask_s=masks[i][1],
                         gat0=gat, qm=resA[i]["qm"], qs=resA[i]["qs"], ysg=resA[i]["ysg"], ycv=resA[i]["ycv"], fx=resA[i]["fx"])
                for k_ in B_KEYS:
                    d[k_] = w[k_]
                in_maps.append(d)
            resB = run_bass_kernel_spmd(ncB, in_maps, core_ids=list(range(8))).results
            xTs = [r["xTo"] for r in resB]
            outT = [r["outT"] for r in resB]
    out = np.zeros((2, SEQ, D), np.float32)
    for i, (b, c) in enumerate(cores):
        o = outT[i].transpose(1, 0, 2).reshape(1024, NTOK).T
        out[b, tok_idx(c), :] = o
    return out


def kernel(**inputs):
    return run_model(inputs, NSB=4, depth=4, fused=False)
```

```python
import numpy as np
import concourse.bass as bass
import concourse.mybir as mybir
from concourse.bass_utils import run_bass_kernel_spmd

F32 = mybir.dt.float32
BF16 = mybir.dt.bfloat16
I32 = mybir.dt.int32
AF = mybir.ActivationFunctionType
ALU = mybir.AluOpType

SEM_LIMIT = 30000
N_DMA_SEMS = 12
SLOT = 512

D = 1024
COL_SG, COL_CONV, COL_MLA, COL_SB, COL_GATE, IN_COLS = 0, 1024, 2560, 2976, 4512, 8608
EPS = 1e-6
PVL = 35


class Sched:
    def __init__(self, nc):
        self.nc = nc
        self.ops = []

    def op(self, eng, fn, reads=(), writes=()):
        self.ops.append((eng, fn, tuple(reads), tuple(writes), False))

    def dma(self, q, fn, reads=(), writes=()):
        self.ops.append((q, fn, tuple(reads), tuple(writes), True))

    def coll(self, fn, reads=(), writes=()):
        self.ops.append(("pool", fn, tuple(reads), tuple(writes), "coll"))

    def emit(self):
        nc = self.nc
        engs = {"pe": nc.tensor, "act": nc.scalar, "dve": nc.vector, "pool": nc.gpsimd, "sp": nc.sync}
        ops = self.ops
        n = len(ops)
        last_w = {}
        rd_c = {}
        rd_d = {}
        deps = [None] * n
        need_sig = [False] * n
        for i, (e, fn, rs, ws, isd) in enumerate(ops):
            d = set()
            for k in rs:
                j = last_w.get(k)
                if j is not None:
                    d.add(j)
            for k in ws:
                j = last_w.get(k)
                if j is not None:
                    d.add(j)
                rc = rd_c.get(k)
                if rc:
                    d.update(rc.values())
                rdd = rd_d.get(k)
                if rdd:
                    d.update(rdd)
            for k in rs:
                if isd:
                    rd_d.setdefault(k, []).append(i)
                else:
                    rd_c.setdefault(k, {})[e] = i
            for k in ws:
                last_w[k] = i
                rd_c[k] = {}
                rd_d[k] = []
            nd = set()
            for j in d:
                if j == i:
                    continue
                ej, _, _, _, jd = ops[j]
                if (not isd) and (not jd) and e == "pe" and ej == "pe":
                    continue
                nd.add(j)
                if not jd:
                    need_sig[j] = True
            deps[i] = nd
        self.sem_ctx = []

        def new_sem(name):
            cm = nc.semaphore(name)
            s = cm.__enter__()
            self.sem_ctx.append(cm)
            return s

        cur = {}
        sig = [None] * n
        cnt = [0]
        dsem, dstate, dcount = {}, {}, {}
        prev_on_sem = [None] * n
        for i, (e, fn, rs, ws, isd) in enumerate(ops):
            if isd == "coll":
                sig[i] = (new_sem("coll%d" % cnt[0]), 1, 1)
                cnt[0] += 1
            elif isd:
                if e not in dsem:
                    dsem[e] = [new_sem("d%s%d" % (e, t)) for t in range(N_DMA_SEMS)]
                    dstate[e] = [[0, None] for _ in range(N_DMA_SEMS)]
                    dcount[e] = 0
                t = dcount[e] % N_DMA_SEMS
                dcount[e] += 1
                st = dstate[e][t]
                if st[0] + 16 > SEM_LIMIT:
                    dsem[e][t] = new_sem("d%s%dx%d" % (e, t, cnt[0]))
                    cnt[0] += 1
                    st[0] = 0
                if st[1] is not None:
                    prev_on_sem[i] = st[1]
                st[0] += 16
                sig[i] = (dsem[e][t], st[0], 16)
                st[1] = i
            elif need_sig[i]:
                if e not in cur or cur[e][1] + 1 > SEM_LIMIT:
                    cur[e] = [new_sem("c%s%d" % (e, cnt[0])), 0]
                    cnt[0] += 1
                cur[e][1] += 1
                sig[i] = (cur[e][0], cur[e][1], 1)
        waited = {}
        nwaits = 0
        for i, (e, fn, rs, ws, isd) in enumerate(ops):
            eng = engs[e]
            dl = list(deps[i])
            if prev_on_sem[i] is not None:
                dl.append(prev_on_sem[i])
            mx = {}
            for j in dl:
                s, v, _ = sig[j]
                key = id(s)
                if key not in mx or mx[key][1] < v:
                    mx[key] = (s, v)
            for key, (s, v) in mx.items():
                wk = (e, key)
                if waited.get(wk, 0) >= v:
                    continue
                waited[wk] = v
                eng.wait_ge(s, v)
                nwaits += 1
            inst = fn(eng)
            if sig[i] is not None:
                inst.then_inc(sig[i][0], sig[i][2])
        feng = engs["sp"]
        for e in dsem:
            for t in range(N_DMA_SEMS):
                st = dstate[e][t]
                if st[1] is not None:
                    s, v, _ = sig[st[1]]
                    feng.wait_ge(s, v)
        self.stats = dict(n_ops=n, n_waits=nwaits, n_sems=len(self.sem_ctx))


_DTS = {F32: 4, BF16: 2, I32: 4}


class Tile:
    def __init__(self, h, off, nbytes, esz):
        self.h, self.off, self.nbytes, self.esz = h, off, nbytes, esz
        self._all = tuple(range(off // SLOT, (off + nbytes + SLOT - 1) // SLOT))

    def __getitem__(self, idx):
        return self.h[idx]

    def k(self):
        return self._all

    def ke(self, e0, ne):
        lo = self.off + e0 * self.esz
        hi = lo + ne * self.esz
        return tuple(range(lo // SLOT, (hi + SLOT - 1) // SLOT))


class Cursor:
    def __init__(self, nc, base, limit, tag):
        self.nc, self.cur, self.limit, self.tag, self.n = nc, base, limit, tag, 0

    def alloc(self, name, shape, dt):
        esz = _DTS[dt]
        nb = esz
        for s in shape[1:]:
            nb *= s
        off = (self.cur + SLOT - 1) // SLOT * SLOT
        assert off + nb <= self.limit, (self.tag, name, off, nb, self.limit)
        self.cur = off + nb
        self.n += 1
        h = self.nc.alloc_sbuf_tensor_at("%s_%s_%d" % (self.tag, name, self.n), list(shape), dt, offset=off)
        return Tile(h, off, nb, esz)

    def fork(self, tag):
        return Cursor(self.nc, self.cur, self.limit, tag)


class Ring:
    def __init__(self, tiles):
        self.t, self.i = tiles, 0

    def get(self):
        t = self.t[self.i % len(self.t)]
        self.i += 1
        return t


class PS:
    def __init__(self, h, bank):
        self.h, self.bank = h, bank

    def __getitem__(self, idx):
        return self.h[idx]

    def k(self):
        return (("ps", self.bank),)


def snd_layout(NTOK, NSB):
    o = {}
    cur = 0
    o["KM"] = cur; cur += 8 * 96 * NTOK
    o["VM"] = cur; cur += 8 * NTOK * 65
    o["KS"] = cur; cur += 8 * 64 * NTOK
    o["VS"] = cur; cur += 8 * NTOK * 64
    o["HALO"] = cur; cur += NSB * 4 * 128 * 2
    o["N"] = cur
    return o


def build(mode, NSB, LN):
    nc = bass.Bass("TRN2", target_bir_lowering=False)
    S = Sched(nc)
    NTOK = NSB * 512
    NPV = 8 + PVL * LN + 8 + 1 + 4
    PV_C, PV_L, PV_FNG = 0, 8, 8 + PVL * LN
    PV_SIGN, PV_OH = PV_FNG + 8, PV_FNG + 9
    SL = snd_layout(NTOK, NSB)
    NSND = SL["N"]
    doA = mode in ("A", "F")
    doB = mode in ("B", "F")

    def din(name, shape, dt=F32):
        return nc.dram_tensor(name, list(shape), dt, kind="ExternalInput").ap()

    def dout(name, shape, dt=F32):
        return nc.dram_tensor(name, list(shape), dt, kind="ExternalOutput").ap()

    def dint(name, shape, dt=F32):
        return nc.dram_tensor(name, list(shape), dt, kind="Internal").ap()

    def dAB(name, shape, dt):
        if mode == "A":
            return dout(name, shape, dt)
        if mode == "B":
            return din(name, shape, dt)
        return dint(name, shape, dt)

    xT_d = din("xT", [128, 8, NTOK])
    pv_d = din("pv", [128, NPV])
    adab_d = din("adab", [1, LN * 6144])
    adaw_d = din("ada_w", [LN, 1024, 6144])
    if doA:
        posi_d = din("posi", [1, NTOK], I32)
        invf_d = din("invf", [1, 128])
        win_d = din("w_in", [LN, 1024, IN_COLS])
        sgwT_d = din("sgwT", [LN, 128, 4, 128])
        sgb_d = din("sgb", [LN, 1, 4 * 512])
        wq_d = din("wq", [LN, 256, 8 * 128])
        wukv_d = din("wukv", [LN, 128, 1024])
        wkpe_d = din("wkpe", [LN, 1024, 2 * 96])
        tri_d = din("tri", [128, 128])
    if doB:
        if not doA:
            win_d = din("w_in", [LN, 1024, IN_COLS])
        wbr_d = din("w_branch", [LN, 4, 512, 1024])
        wout_d = din("w_out", [LN, 1024, 1024])
        w1_d = din("w1", [LN, 1024, 4096])
        w2_d = din("w2", [LN, 4096, 1024])
        mns_d = din("mask_ns", [16, 128, 512])
        mst_d = din("mask_s", [16, 128, 512])
        negu_d = din("negU", [128, 128])
        cw_unused = None
    CHK = {"KM": 2 * 96 * NTOK, "VM": 2 * NTOK * 65, "KS": 2 * 64 * NTOK, "VS": 2 * NTOK * 64, "HL": NSB * 1024}
    CHN = [(kd, j) for kd in ("HL", "KM", "VM", "KS", "VS") for j in range(1 if kd == "HL" else 4)]
    snd_t = [dict() for _ in range(LN)]
    gat_t = [dict() for _ in range(LN)]
    for l in range(LN):
        for (kd, j) in CHN:
            nm = "%d_%s%d" % (l, kd, j)
            if mode != "B":
                snd_t[l][(kd, j)] = dAB("snd" + nm, [CHK[kd]], BF16)
            if mode == "B":
                gat_t[l][(kd, j)] = din("gat" + nm, [4 * CHK[kd]], BF16)
            elif mode == "F":
                gat_t[l][(kd, j)] = dint("gat" + nm, [4 * CHK[kd]], BF16)
    qm_d = dAB("qm", [NSB, 8, 96, 512], BF16)
    qs_d = dAB("qs", [NSB, 8, 64, 512], BF16)
    ysg_d = dAB("ysg", [128, 4, NTOK], BF16)
    ycv_d = dAB("ycv", [128, 4, NTOK], BF16)
    fx_d = dAB("fx", [128, NSB, 4, 4], F32)
    if mode == "B":
        xTo_d = dout("xTo", [128, 8, NTOK])
    if doB:
        outT_d = dout("outT", [128, 8, NTOK])

    BASE = 16896
    LIMIT = nc.SBUF_PARTITION_SIZE_BYTES
    R = Cursor(nc, BASE, LIMIT, "r")
    xT = R.alloc("xT", [128, 8, NTOK], F32)
    pv = R.alloc("pv", [128, NPV], F32)
    lay = R.alloc("lay", [128, LN * 48], F32)
    ones_bf = R.alloc("ones_bf", [128, 128], BF16)
    ones_f = R.alloc("ones_f", [128, 128], F32)
    wblk = Ring([R.alloc("wblk%d" % i, [128, 8 * 512], BF16) for i in range(3)])
    hTr = Ring([R.alloc("hT%d" % i, [128, 8, 512], BF16) for i in range(2)])
    xsq_r = Ring([R.alloc("xsq%d" % i, [128, 512], BF16) for i in range(2)])
    rstd = R.alloc("rstd", [128, 512], F32)
    fa = Ring([R.alloc("fa%d" % i, [128, 512], F32) for i in range(3)])
    if doA:
        WsT = R.alloc("WsT", [128, 4, 128], BF16)
        tri = R.alloc("tri", [128, 128], BF16)
        bsb = R.alloc("bsb", [128, 4, 512], F32)
        wq = R.alloc("wq", [128, 2, 8 * 128], BF16)
        wukv = R.alloc("wukv", [128, 1024], BF16)
        wkpe = R.alloc("wkpe", [128, 8, 192], BF16)
    if doB:
        negU = R.alloc("negU", [128, 128], BF16)
        negones = R.alloc("negones", [128, 128], BF16)
    ARENA = R.cur

    psb = []
    for i in range(8):
        cm = nc.psum_tensor("psb%d" % i, [128, 512], F32)
        psb.append(PS(cm.__enter__(), i))
    ps_main = Ring(psb[0:4])
    ps_acc = Ring(psb[4:6])
    ps_misc = Ring(psb[6:8])

    def MM(out, lhsT, rhs, start, stop, r, w, **kw):
        S.op("pe", lambda e: e.matmul(out, lhsT, rhs, start=start, stop=stop, **kw), r, w)

    def ACT(out, in_, func, r, w, **kw):
        S.op("act", lambda e: e.activation(out=out, in_=in_, func=func, **kw), r, w)

    def TT(eng, out, a, b, op, r, w):
        S.op(eng, lambda e: e.tensor_tensor(out=out, in0=a, in1=b, op=op), r, w)

    def TS(eng, out, a, s1, s2, op0, op1, r, w):
        if op1 is None:
            S.op(eng, lambda e: e.tensor_scalar(out=out, in0=a, scalar1=s1, scalar2=None, op0=op0), r, w)
        else:
            S.op(eng, lambda e: e.tensor_scalar(out=out, in0=a, scalar1=s1, scalar2=s2, op0=op0, op1=op1), r, w)

    def STT(eng, out, in0, scalar, in1, op0, op1, r, w):
        S.op(eng, lambda e: e.scalar_tensor_tensor(out=out, in0=in0, scalar=scalar, in1=in1, op0=op0, op1=op1), r, w)

    def CP(eng, out, in_, r, w):
        if eng == "act":
            S.op("act", lambda e: e.activation(out=out, in_=in_, func=AF.Identity), r, w)
        else:
            S.op(eng, lambda e: e.tensor_copy(out=out, in_=in_), r, w)

    def RECIP(out, in_, r, w):
        S.op("dve", lambda e: e.reciprocal(out=out, in_=in_), r, w)

    def MEMSET(eng, out, val, w):
        S.op(eng, lambda e: e.memset(out, val), (), w)

    def DMA(q, out, in_, r, w):
        S.dma(q, lambda e: e.dma_start(out=out, in_=in_), r, w)

    def xk(k, m):
        return xT.ke(k * NTOK + m * 512, 512)

    def layc(l, j, k):
        c = l * 48 + j * 8 + k
        return lay[:, c:c + 1]

    def pvl(l, j):
        c = PV_L + l * PVL + j
        return pv[:, c:c + 1]

    DMA("sp", pv[:], pv_d[:, :], (), pv.k())
    for k in range(8):
        DMA("sp", xT[:, k, :], xT_d[:, k, :], (), xT.ke(k * NTOK, NTOK))
    MEMSET("dve", ones_bf[:], 1.0, ones_bf.k())
    MEMSET("dve", ones_f[:], 1.0, ones_f.k())
    if doB:
        MEMSET("dve", negones[:], -1.0, negones.k())
        DMA("pool", negU[:], negu_d[:, :], (), negU.k())
    if doA:
        DMA("pool", tri[:], tri_d[:, :], (), tri.k())

    P = Cursor(nc, ARENA, LIMIT, "p")
    siluc = P.alloc("siluc", [128, 8], BF16)
    modrow = P.alloc("modrow", [1, 6144], F32)
    adab_t = P.alloc("adab", [1, 6144], F32)
    modT = P.alloc("modT", [128, LN * 48], F32)
    ACT(siluc[:], pv[:, PV_C:PV_C + 8], AF.Silu, pv.k(), siluc.k())
    adaw_v = adaw_d.rearrange("l (k p) n -> l p k n", p=128)
    for l in range(LN):
        DMA("sp", adab_t[:], adab_d[:, l * 6144:(l + 1) * 6144], (), adab_t.k())
        for nb in range(12):
            wt = wblk.get()
            wv = wt.h[:, :].rearrange("p (k n) -> p k n", k=8)
            DMA("pool", wv, adaw_v[l, :, :, nb * 512:(nb + 1) * 512], (), wt.k())
            pp = ps_main.get()
            for k in range(8):
                MM(pp[0:1, :], siluc[:, k:k + 1], wv[:, k, :], k == 0, k == 7, siluc.k() + wt.k(), pp.k())
            c0 = nb * 512
            TT("dve", modrow[0:1, c0:c0 + 512], pp[0:1, :], adab_t[0:1, c0:c0 + 512], ALU.add,
               pp.k() + adab_t.ke(c0, 512), modrow.ke(c0, 512))
        pm = ps_misc.get()
        for j in range(48):
            c0 = j * 128
            MM(pm[:, j:j + 1], modrow[0:1, c0:c0 + 128], ones_f[0:1, 0:1], True, True,
               modrow.ke(c0, 128) + ones_f.k(), pm.k())
        CP("dve", modT[:, l * 48:(l + 1) * 48], pm[:, 0:48], pm.k(), modT.k())
        b = l * 48
        for which in range(2):
            sh = modT[:, b + which * 24:b + which * 24 + 8]
            sc = modT[:, b + which * 24 + 8:b + which * 24 + 16]
            g = modT[:, b + which * 24 + 16:b + which * 24 + 24]
            gain = pv[:, PV_L + l * PVL + which * 8:PV_L + l * PVL + which * 8 + 8]
            Acol = lay[:, b + which * 24:b + which * 24 + 8]
            Bcol = lay[:, b + which * 24 + 8:b + which * 24 + 16]
            Gcol = lay[:, b + which * 24 + 16:b + which * 24 + 24]
            STT("dve", Acol, sc, 1.0, gain, ALU.add, ALU.mult, modT.k() + pv.k(), lay.k())
            CP("dve", Bcol, sh, modT.k(), lay.k())
            CP("dve", Gcol, g, modT.k(), lay.k())

    def norm_hT(l, m, which):
        hT = hTr.get()
        ss = ps_misc.get()
        for k in range(8):
            xs = xsq_r.get()
            ACT(xs[:], xT[:, k, m * 512:(m + 1) * 512], AF.Square, xk(k, m), xs.k())
            MM(ss[:], ones_bf[:], xs[:], k == 0, k == 7, ones_bf.k() + xs.k(), ss.k())
        ACT(rstd[:], ss[:], AF.Sqrt, ss.k(), rstd.k(), scale=1.0 / D, bias=EPS)
        RECIP(rstd[:], rstd[:], rstd.k(), rstd.k())
        for k in range(8):
            tmp = fa.get()
            TT("dve", tmp[:], xT[:, k, m * 512:(m + 1) * 512], rstd[:], ALU.mult, xk(k, m) + rstd.k(), tmp.k())
            ACT(hT[:, k, :], tmp[:], AF.Identity, tmp.k() + lay.k(), hT.ke(k * 512, 512),
                scale=layc(l, which * 3, k), bias=layc(l, which * 3 + 1, k))
        return hT

    def load_wblk(src_ap, ncols_total):
        wt = wblk.get()
        Pn, Kc, n = src_ap.shape[0], src_ap.shape[1], src_ap.shape[2]
        wv = wt.h[0:Pn, 0:Kc * n].rearrange("p (k n) -> p k n", k=Kc)
        DMA("pool", wv, src_ap, (), wt.k())
        return wt, wv

    win_v = win_d.rearrange("l (k p) n -> l p k n", p=128)

    def proj_fm(wt, wv, c0, M, hT, pp, prow=None):
        for k in range(8):
            MM(pp[0:M, :], wv[:, k, c0:c0 + M], hT[:, k, :], k == 0, k == 7, wt.k() + hT.k(), pp.k())

    if doA:
        A = Cursor(nc, ARENA, LIMIT, "a")
        Ct = A.alloc("C", [128, 512], F32)
        Sgt = A.alloc("Sg", [128, 512], F32)
        posi = A.alloc("posi", [1, 512], I32)
        posf = A.alloc("posf", [1, 512], F32)
        invf = A.alloc("invf", [1, 128], F32)
        angi = A.alloc("angi", [128, 512], I32)
        uT = A.alloc("uT", [128, 4, 512], BF16)
        vhat = A.alloc("vhat", [128, 4, 512], BF16)
        ysgo = A.alloc("ysgo", [128, 4, 512], BF16)
        ycvo = A.alloc("ycvo", [128, 4, 512], BF16)
        tt_r = Ring([A.alloc("tt%d" % i, [128, 514], F32) for i in range(2)])
        cq_sb = A.alloc("cq_sb", [128, 3, 512], F32)
        cn = A.alloc("cn", [128, 3, 512], BF16)
        qo_r = Ring([A.alloc("qo%d" % i, [96, 512], BF16) for i in range(2)])
        kTo = A.alloc("kTo", [96, 8, 512], BF16)
        vxo = A.alloc("vxo", [128, 4, 8 * 65], BF16)
        sbp_r = Ring([A.alloc("sbp%d" % i, [128, 512], BF16) for i in range(3)])
        vso = A.alloc("vso", [128, 4, 512], BF16)
        fxt = A.alloc("fxt", [128, 4, 4], F32)
        halo_o = A.alloc("halo_o", [128, 4, 2], BF16)
        mvst = A.alloc("mvst", [128, 8], F32)
        brow = A.alloc("brow", [1, 512], F32)
        fb = fa


    snd_keys = [dict() for _ in range(LN)]

    def sk(l, kd, j):
        lst = snd_keys[l].setdefault((kd, j), [])
        k_ = ("snd", l, kd, j, len(lst))
        lst.append(k_)
        return [k_]

    def phaseA(l):
        DMA("sp", invf[:], invf_d[:, :], (), invf.k())
        MEMSET("dve", vxo[:], 1.0, vxo.k())
        for i in range(2):
            t_ = tt_r.t[i]
            MEMSET("dve", t_[:, 0:2], 0.0, t_.k())
        sgw_t, sgw_v = load_wblk(sgwT_d[l, :, :, :], 0)
        for g in range(4):
            TT("dve", WsT[:, g, :], sgw_v[:, g, :], tri[:], ALU.mult, sgw_t.k() + tri.k(), WsT.k())
        for g in range(4):
            DMA("sp", brow[:], sgb_d[l, :, g * 512:(g + 1) * 512], (), brow.k())
            pp = ps_misc.get()
            MM(pp[:], ones_f[0:1, :], brow[0:1, :], True, True, ones_f.k() + brow.k(), pp.k())
            CP("act", bsb[:, g, :], pp[:], pp.k(), bsb.ke(g * 512, 512))
        DMA("pool", wq[:], wq_d[l].rearrange("(k p) n -> p k n", p=128), (), wq.k())
        DMA("pool", wukv[:], wukv_d[l, :, :], (), wukv.k())
        DMA("pool", wkpe[:], wkpe_d[l].rearrange("(k p) n -> p k n", p=128), (), wkpe.k())
        sndt = snd_t[l]
        for m in range(NSB):
            t0 = m * 512
            DMA("sp", posi[:], posi_d[:, t0:t0 + 512], (), posi.k())
            CP("dve", posf[:], posi[:], posi.k(), posf.k())
            pa = ps_misc.get()
            MM(pa[:], invf[0:1, :], posf[0:1, :], True, True, invf.k() + posf.k(), pa.k())
            for (dst, shift) in ((Sgt, 0.0), (Ct, 0.25)):
                y_ = fb.get()
                TS("dve", y_[:], pa[:], 1.0 / (2 * np.pi), shift, ALU.mult, ALU.add, pa.k(), y_.k())
                CP("dve", angi[:], y_[:], y_.k(), angi.k())
                y2 = fb.get()
                CP("dve", y2[:], angi[:], angi.k(), y2.k())
                TT("dve", y_[:], y_[:], y2[:], ALU.subtract, y_.k() + y2.k(), y_.k())
                ACT(dst[:], y_[:], AF.Sin, y_.k(), dst.k(), scale=float(2 * np.pi))
            TS("dve", Sgt[:], Sgt[:], pv[:, PV_SIGN:PV_SIGN + 1], None, ALU.mult, None, Sgt.k() + pv.k(), Sgt.k())

            hT = norm_hT(l, m, 0)
            wt, wv = load_wblk(win_v[l, :, :, COL_SG:COL_SG + 512], 0)
            for j in range(4):
                pp = ps_main.get()
                proj_fm(wt, wv, j * 128, 128, hT, pp)
                ACT(uT[:, j, :], pp[:], AF.Gelu, pp.k(), uT.ke(j * 512, 512))
            wt, wv = load_wblk(win_v[l, :, :, COL_SG + 512:COL_SG + 1024], 0)
            for r in range(4):
                pp = ps_main.get()
                for k in range(8):
                    MM(pp[:], hT[:, k, r * 128:(r + 1) * 128], wv[:, k, :], k == 0, k == 7, wt.k() + hT.k(), pp.k())
                g_ = fb.get()
                ACT(g_[:], pp[:], AF.Gelu, pp.k(), g_.k())
                S.op("dve", (lambda g_: lambda e: e.bn_stats(out=mvst[:, 0:6], in_=g_[:]))(g_), g_.k(), mvst.k())
                S.op("dve", lambda e: e.bn_aggr(out=mvst[:, 6:8], in_=mvst[:, 0:6]), mvst.k(), mvst.k())
                ACT(mvst[:, 7:8], mvst[:, 7:8], AF.Sqrt, mvst.k(), mvst.k(), bias=EPS)
                RECIP(mvst[:, 7:8], mvst[:, 7:8], mvst.k(), mvst.k())
                TS("dve", vhat[:, r, :], g_[:], mvst[:, 6:7], mvst[:, 7:8], ALU.subtract, ALU.mult,
                   g_.k() + mvst.k(), vhat.ke(r * 512, 512))
            for g in range(4):
                pp = ps_main.get()
                for r in range(4):
                    MM(pp[:, r * 128:(r + 1) * 128], vhat[:, r, g * 128:(g + 1) * 128], WsT[:, g, :], True, True,
                       vhat.k() + WsT.k(), pp.k())
                tmp = fb.get()
                STT("dve", tmp[:], pp[:], pvl(l, 16 + g), bsb[:, g, :], ALU.mult, ALU.add,
                    pp.k() + pv.k() + bsb.ke(g * 512, 512), tmp.k())
                TT("pool", ysgo[:, g, :], tmp[:], uT[:, g, :], ALU.mult, tmp.k() + uT.ke(g * 512, 512), ysgo.ke(g * 512, 512))
            DMA("sp", ysg_d[:, :, t0:t0 + 512], ysgo[:], ysgo.k(), [("ysg", m)])
            wts = [load_wblk(win_v[l, :, :, COL_CONV + i * 512:COL_CONV + (i + 1) * 512], 0) for i in range(3)]
            for j in range(4):
                pgc = ps_main.get()
                proj_fm(wts[1][0], wts[1][1], j * 128, 128, hT, pgc)
                gc = fb.get()
                CP("act", gc[:], pgc[:], pgc.k(), gc.k())
                pxv = ps_main.get()
                proj_fm(wts[2][0], wts[2][1], j * 128, 128, hT, pxv)
                t_ = tt_r.get()
                TT("dve", t_[:, 2:514], gc[:], pxv[:], ALU.mult, gc.k() + pxv.k(), t_.k())
                acc = fb.get()
                TS("dve", acc[:], t_[:, 2:514], pvl(l, 20 + 8 + j), None, ALU.mult, None, t_.k() + pv.k(), acc.k())
                STT("dve", acc[:], t_[:, 1:513], pvl(l, 20 + 4 + j), acc[:], ALU.mult, ALU.add, t_.k() + pv.k() + acc.k(), acc.k())
                STT("dve", acc[:], t_[:, 0:512], pvl(l, 20 + j), acc[:], ALU.mult, ALU.add, t_.k() + pv.k() + acc.k(), acc.k())
                pgb = ps_main.get()
                proj_fm(wts[0][0], wts[0][1], j * 128, 128, hT, pgb)
                TT("dve", ycvo[:, j, :], acc[:], pgb[:], ALU.mult, acc.k() + pgb.k(), ycvo.ke(j * 512, 512))
                CP("pool", fxt[:, j, 0:2], acc[:, 0:2], acc.k(), fxt.k())
                CP("dve", fxt[:, j, 2:4], pgb[:, 0:2], pgb.k(), fxt.k())
                CP("pool", halo_o[:, j, :], t_[:, 512:514], t_.k(), halo_o.k())
            DMA("sp", ycv_d[:, :, t0:t0 + 512], ycvo[:], ycvo.k(), [("ycv", m)])
            DMA("sp", fx_d[:, m, :, :], fxt[:], fxt.k(), [("fx", m)])
            ho = SL["HALO"] + m * 1024
            DMA("sp", sndt[("HL", 0)][m * 1024:(m + 1) * 1024].rearrange("(j p e) -> p j e", j=4, p=128), halo_o[:], halo_o.k(), sk(l, "HL", 0))
            wt, wv = load_wblk(win_v[l, :, :, COL_MLA:COL_MLA + 384], 0)
            for j in range(3):
                pp = ps_main.get()
                proj_fm(wt, wv, j * 128, 128, hT, pp)
                CP("act", cq_sb[:, j, :], pp[:], pp.k(), cq_sb.ke(j * 512, 512))
            for (j0, nj, gcol) in ((0, 2, 32), (2, 1, 34)):
                ss = ps_misc.get()
                for j in range(j0, j0 + nj):
                    xs = xsq_r.get()
                    ACT(xs[:], cq_sb[:, j, :], AF.Square, cq_sb.ke(j * 512, 512), xs.k())
                    MM(ss[:], ones_bf[:], xs[:], j == j0, j == j0 + nj - 1, ones_bf.k() + xs.k(), ss.k())
                rs_ = fb.get()
                ACT(rs_[:], ss[:], AF.Sqrt, ss.k(), rs_.k(), scale=1.0 / (128 * nj), bias=EPS)
                RECIP(rs_[:], rs_[:], rs_.k(), rs_.k())
                for j in range(j0, j0 + nj):
                    STT("dve", cn[:, j, :], cq_sb[:, j, :], pvl(l, gcol + (j - j0)), rs_[:], ALU.mult, ALU.mult,
                        cq_sb.ke(j * 512, 512) + pv.k() + rs_.k(), cn.ke(j * 512, 512))
            pka = ps_main.get()
            pkb = ps_main.get()
            for k in range(8):
                MM(pka[0:96, :], wkpe[:, k, 0:96], hT[:, k, :], k == 0, k == 7, wkpe.k() + hT.k(), pka.k())
            for k in range(8):
                MM(pkb[0:96, :], wkpe[:, k, 96:192], hT[:, k, :], k == 0, k == 7, wkpe.k() + hT.k(), pkb.k())
            t1 = fb.get()
            t2 = fb.get()
            TT("dve", t1[64:96, :], pka[64:96, :], Ct[64:96, :], ALU.mult, pka.k() + Ct.k(), t1.k())
            TT("dve", t2[64:96, :], pkb[64:96, :], Sgt[64:96, :], ALU.mult, pkb.k() + Sgt.k(), t2.k())
            for h in range(8):
                TT("pool", kTo[64:96, h, :], t1[64:96, :], t2[64:96, :], ALU.add, t1.k() + t2.k(), kTo.ke(h * 512, 512))
            for h in range(8):
                pqa = ps_main.get()
                pqb = ps_main.get()
                for k in range(2):
                    MM(pqa[0:96, :], wq[:, k, h * 128:h * 128 + 96], cn[:, k, :], k == 0, k == 1, wq.k() + cn.k(), pqa.k())
                for k in range(2):
                    MM(pqb[0:96, :], wq[:, k, h * 128 + 32:h * 128 + 128], cn[:, k, :], k == 0, k == 1, wq.k() + cn.k(), pqb.k())
                qo = qo_r.get()
                CP("act", qo[0:64, :], pqa[0:64, :], pqa.k(), qo.k())
                t1 = fb.get()
                t2 = fb.get()
                TT("dve", t1[64:96, :], pqa[64:96, :], Ct[64:96, :], ALU.mult, pqa.k() + Ct.k(), t1.k())
                TT("dve", t2[64:96, :], pqb[64:96, :], Sgt[64:96, :], ALU.mult, pqb.k() + Sgt.k(), t2.k())
                TT("pool", qo[64:96, :], t1[64:96, :], t2[64:96, :], ALU.add, t1.k() + t2.k(), qo.k())
                DMA("sp", qm_d[m, h, :, :], qo[:], qo.k(), [("qm", m, h)])
                pkn = ps_main.get()
                MM(pkn[0:64, :], wukv[:, h * 128:h * 128 + 64], cn[:, 2, :], True, True, wukv.k() + cn.k(), pkn.k())
                CP("act", kTo[0:64, h, :], pkn[0:64, :], pkn.k(), kTo.ke(h * 512, 512))
            for j in range(4):
                DMA("sp", sndt[("KM", j)].rearrange("(h r t) -> r h t", h=2, r=96)[:, :, t0:t0 + 512],
                    kTo[:, 2 * j:2 * j + 2, :], kTo.k(), sk(l, "KM", j))
            wv_v = wukv.h[:, :].rearrange("p (h e) -> p h e", h=8)[:, :, 64:128]
            for r in range(4):
                pp = ps_main.get()
                MM(pp[:].rearrange("p (h e) -> p h e", h=8), cn[:, 2, r * 128:(r + 1) * 128], wv_v, True, True,
                   wukv.k() + cn.k(), pp.k())
                CP("act", vxo[:, r, :].rearrange("p (h e) -> p h e", h=8)[:, :, 0:64],
                   pp[:].rearrange("p (h e) -> p h e", h=8), pp.k(), vxo.ke(r * 520, 520))
            for j in range(4):
                vm = sndt[("VM", j)].rearrange("(h t e) -> t h e", h=2, e=65)
                for r in range(4):
                    DMA("sp", vm[t0 + r * 128:t0 + (r + 1) * 128, :, :],
                        vxo[:, r, :].rearrange("p (h e) -> p h e", h=8)[:, 2 * j:2 * j + 2, :],
                        vxo.ke(r * 520, 520), sk(l, "VM", j))
            for part in range(2):
                wt, wv = load_wblk(win_v[l, :, :, COL_SB + part * 512:COL_SB + (part + 1) * 512], 0)
                for j in range(4):
                    pp = ps_main.get()
                    proj_fm(wt, wv, j * 128, 128, hT, pp)
                    sp_ = sbp_r.get()
                    if part == 0:
                        ACT(sp_[:], pp[:], AF.Identity, pp.k(), sp_.k(), scale=0.125)
                        for hh in range(2):
                            DMA("sp", qs_d[m, 2 * j + hh, :, :], sp_[hh * 64:(hh + 1) * 64, :], sp_.k(), [("qs", m, 2 * j + hh)])
                    else:
                        CP("act", sp_[:], pp[:], pp.k(), sp_.k())
                        ks = sndt[("KS", j)].rearrange("(h r t) -> h r t", h=2, r=64)
                        for hh in range(2):
                            DMA("sp", ks[hh, :, t0:t0 + 512], sp_[hh * 64:(hh + 1) * 64, :], sp_.k(), sk(l, "KS", j))
            wt, wv = load_wblk(win_v[l, :, :, COL_SB + 1024:COL_SB + 1536], 0)
            for r in range(4):
                pp = ps_main.get()
                for k in range(8):
                    MM(pp[:], hT[:, k, r * 128:(r + 1) * 128], wv[:, k, :], k == 0, k == 7, wt.k() + hT.k(), pp.k())
                CP("act", vso[:, r, :], pp[:], pp.k(), vso.ke(r * 512, 512))
            for j in range(4):
                vs = sndt[("VS", j)].rearrange("(h t e) -> t h e", h=2, e=64)
                for r in range(4):
                    DMA("sp", vs[t0 + r * 128:t0 + (r + 1) * 128, :, :],
                        vso[:, r, :].rearrange("p (h e) -> p h e", h=8)[:, 2 * j:2 * j + 2, :],
                        vso.ke(r * 512, 512), sk(l, "VS", j))

    if doB:
        B = Cursor(nc, ARENA, LIMIT, "b")
        ymla = B.alloc("ymla", [64, 8, 512], BF16)
        ysb = B.alloc("ysb", [64, 8, 512], BF16)
        ysg_l = B.alloc("ysg_l", [128, 4, 512], BF16)
        ycv_l = B.alloc("ycv_l", [128, 4, 512], BF16)
        fx_l = B.alloc("fx_l", [128, 4, 4], F32)
        hsel = B.alloc("hsel", [128, 4, 4, 2], BF16)
        hf = B.alloc("hf", [128, 4, 8], F32)
        X = B.fork("x")
        kc_r = Ring([X.alloc("kc%d" % i, [96, 2048], BF16) for i in range(2)])
        vc_r = Ring([X.alloc("vc%d" % i, [128, 16, 65], BF16) for i in range(2)])
        qT_r = Ring([X.alloc("qT%d" % i, [96, 512], BF16) for i in range(2)])
        pa_r = Ring([X.alloc("pa%d" % i, [128, 512], BF16) for i in range(3)])
        e_r = fa
        lp_r = Ring([X.alloc("lp%d" % i, [128, 512], BF16) for i in range(3)])
        lsum_r = Ring([X.alloc("lsum%d" % i, [128, 512], F32) for i in range(2)])
        lsbf_r = Ring([X.alloc("lsbf%d" % i, [128, 512], BF16) for i in range(2)])
        masks_t = X.alloc("masks", [128, 16, 512], BF16)

        Y = B.fork("y")
        macc = Y.alloc("macc", [128, 4, 512], F32)
        sig_r = Ring([Y.alloc("sig%d" % i, [128, 512], F32) for i in range(2)])
        prod_r = Ring([Y.alloc("prod%d" % i, [128, 512], F32) for i in range(2)])
        mergedT = Y.alloc("mergedT", [128, 8, 512], BF16)
        Z = B.fork("z")
        aT = Z.alloc("aT", [128, 32, 512], BF16)
        rl_r = Ring([Z.alloc("rl%d" % i, [128, 512], BF16) for i in range(2)])

    def attention(l, m, kind):
        gatt = gat_t[l]
        nkb = 16 * m + 16
        nch = m + 1
        mla = kind == "mla"
        if mla:
            KK, VK, KR, VE = "KM", "VM", 96, 65
            mask_d = mns_d
        else:
            KK, VK, KR, VE = "KS", "VS", 64, 64
            mask_d = mst_d
        kviews = [gatt[(KK, j)].rearrange("(c h r t) -> h r c t", c=4, h=2, r=KR) for j in range(4)]
        vviews = [gatt[(VK, j)].rearrange("(c h t e) -> h t c e", c=4, h=2, e=VE) for j in range(4)]
        DMA("pool", masks_t[:], mask_d.rearrange("j s t -> s j t"), (), masks_t.k())
        chorder = list(range(nch)) if mla else list(range(nch - 1, -1, -1))
        kborder = list(range(16)) if mla else list(range(15, -1, -1))
        chunks = [(h, ch) for h in range(8) for ch in chorder]
        loaded = {}

        def load_chunk(ci):
            if ci >= len(chunks) or ci in loaded:
                return
            h, ch = chunks[ci]
            kc = kc_r.get()
            vc = vc_r.get()
            DMA("sp", kc.h[0:KR, :].rearrange("r (c t) -> r c t", c=4),
                kviews[h // 2][h % 2, :, :, ch * 512:(ch + 1) * 512], [("gat", l, KK, h // 2)], kc.k())
            for c_ in range(4):
                DMA("sp", vc.h[:, c_ * 4:(c_ + 1) * 4, 0:VE],
                    vviews[h // 2][h % 2, ch * 512:(ch + 1) * 512, c_, :].rearrange("(j p) e -> p j e", p=128),
                    [("gat", l, VK, h // 2)], vc.k())
            loaded[ci] = (kc, vc)

        qts = {}

        def load_q(h):
            if h >= 8 or h in qts:
                return
            qT = qT_r.get()
            if mla:
                DMA("sp", qT[0:96, :], qm_d[m, h, :, :], [("qm", m, h)], qT.k())
            else:
                DMA("sp", qT[0:64, :], qs_d[m, h, :, :], [("qs", m, h)], qT.k())
            qts[h] = qT

        load_q(0)
        load_chunk(0)
        for h in range(8):
            qT = qts[h]
            O = ps_acc.get()
            lsum = lsum_r.get() if not mla else None
            items = []
            for ci_l, ch in enumerate(chorder):
                for kb in kborder:
                    items.append((h * nch + ci_l, ch, kb))
            n = len(items)
            st = [dict() for _ in range(n)]

            def s1(i):
                ci, ch, kb = items[i]
                if kb == kborder[0]:
                    load_chunk(ci)
                if kb == kborder[3]:
                    load_chunk(ci + 1)
                    load_q(h + 1)
                kc, vc = loaded[ci]
                kbg = ch * 16 + kb
                d = st[i]
                d["vc"], d["kb"] = vc, kb
                d["masked"] = kbg >= nkb - 16
                d["mj"] = kbg - (nkb - 16)
                Sp = ps_main.get()
                d["Sp"] = Sp
                if mla:
                    MM(Sp[:], kc[0:96, kb * 128:(kb + 1) * 128], qT[0:96, :], True, True, kc.k() + qT.k(), Sp.k())
                    Pt = pa_r.get()
                    ACT(Pt[:], Sp[:], AF.Exp, Sp.k(), Pt.k(), scale=float(96 ** -0.5))
                    if d["masked"]:
                        TT("dve", Pt[:], Pt[:], masks_t[:, d["mj"], :], ALU.mult, Pt.k() + masks_t.k(), Pt.k())
                    d["A"] = Pt
                else:
                    MM(Sp[:], kc[0:64, kb * 128:(kb + 1) * 128], qT[0:64, :], True, False, kc.k() + qT.k(), Sp.k())
                    E = e_r.get()
                    ACT(E[:], Sp[:], AF.Exp, Sp.k(), E.k())
                    Lp = lp_r.get()
                    ACT(Lp[:], E[:], AF.Ln, E.k(), Lp.k(), bias=1.0)
                    if d["masked"]:
                        TT("dve", Lp[:], Lp[:], masks_t[:, d["mj"], :], ALU.mult, Lp.k() + masks_t.k(), Lp.k())
                    d["Lp"] = Lp

            def s2(i):
                d = st[i]
                Sp, Lp = d["Sp"], d["Lp"]
                first = i == 0
                MM(Sp[:], negU[:], Lp[:], False, first, negU.k() + Lp.k(), Sp.k(), skip_group_check=True)
                if not first:
                    lsbf = lsbf_r.get()
                    CP("pool", lsbf[:], lsum[:], lsum.k(), lsbf.k())
                    MM(Sp[:], negones[:], lsbf[:], False, True, negones.k() + lsbf.k(), Sp.k(), skip_group_check=True)
                    if i < n - 1:
                        TT("dve", lsum[:], lsum[:], Lp[:], ALU.add, lsum.k() + Lp.k(), lsum.k())
                else:
                    CP("dve", lsum[:], Lp[:], Lp.k(), lsum.k())
                At = pa_r.get()
                ACT(At[:], Sp[:], AF.Exp, Sp.k(), At.k())
                if d["masked"]:
                    TT("pool", At[:], At[:], masks_t[:, d["mj"], :], ALU.mult, At.k() + masks_t.k(), At.k())
                d["A"] = At

            def s3(i):
                d = st[i]
                vc, kb, At = d["vc"], d["kb"], d["A"]
                if mla:
                    MM(O[0:65, :], vc[:, kb, 0:65], At[:], i == 0, i == n - 1, vc.k() + At.k(), O.k())
                else:
                    MM(O[0:64, :], vc[:, kb, 0:64], At[:], i == 0, i == n - 1, vc.k() + At.k(), O.k())

            if mla:
                for t in range(n + 1):
                    if t < n:
                        s1(t)
                    if t >= 1:
                        s3(t - 1)
            else:
                for t in range(n + 2):
                    if t < n:
                        s1(t)
                    if 1 <= t <= n:
                        s2(t - 1)
                    if t >= 2:
                        s3(t - 2)
            if mla:
                rs_t = fa.get()
                bc_t = fa.get()
                CP("act", rs_t[64:65, :], O[64:65, :], O.k(), rs_t.k())
                RECIP(rs_t[64:65, :], rs_t[64:65, :], rs_t.k(), rs_t.k())
                pb = ps_misc.get()
                MM(pb[0:64, :], ones_f[64:65, 0:64], rs_t[64:65, :], True, True, ones_f.k() + rs_t.k(), pb.k())
                CP("act", bc_t[0:64, :], pb[0:64, :], pb.k(), bc_t.k())
                TT("dve", ymla[:, h, :], O[0:64, :], bc_t[0:64, :], ALU.mult, O.k() + bc_t.k(), ymla.ke(h * 512, 512))
            else:
                CP("act", ysb[:, h, :], O[0:64, :], O.k(), ysb.ke(h * 512, 512))

    def phaseB(l, last):
        wbr_v = wbr_d
        for m in range(NSB):
            t0 = m * 512
            DMA("sp", ysg_l[:], ysg_d[:, :, t0:t0 + 512], [("ysg", m)], ysg_l.k())
            DMA("sp", ycv_l[:], ycv_d[:, :, t0:t0 + 512], [("ycv", m)], ycv_l.k())
            DMA("sp", fx_l[:], fx_d[:, m, :, :], [("fx", m)], fx_l.k())
            hv = gat_t[l][("HL", 0)].rearrange("(c m j p e) -> c m p j e", c=4, m=NSB, j=4, p=128)
            MEMSET("dve", hsel[:], 0.0, hsel.k())
            for q in range(4):
                if q == 0:
                    if m == 0:
                        continue
                    src = hv[3, m - 1]
                else:
                    src = hv[q - 1, m]
                DMA("sp", hsel[:, q, :, :], src, [("gat", l, "HL", 0)], hsel.k())
            MEMSET("dve", hf[:], 0.0, hf.k())
            for q in range(4):
                STT("dve", hf[:, :, 0:2], hsel[:, q, :, :], pv[:, PV_OH + q:PV_OH + q + 1], hf[:, :, 0:2], ALU.mult, ALU.add,
                    hsel.k() + pv.k() + hf.k(), hf.k())
            for j in range(4):
                w0, w1 = pvl(l, 20 + j), pvl(l, 24 + j)
                TS("dve", hf[:, j, 2:3], hf[:, j, 1:2], w1, None, ALU.mult, None, hf.k() + pv.k(), hf.k())
                STT("dve", hf[:, j, 2:3], hf[:, j, 0:1], w0, hf[:, j, 2:3], ALU.mult, ALU.add, hf.k() + pv.k(), hf.k())
                TS("dve", hf[:, j, 3:4], hf[:, j, 1:2], w0, None, ALU.mult, None, hf.k() + pv.k(), hf.k())
                TT("dve", hf[:, j, 2:4], hf[:, j, 2:4], fx_l[:, j, 0:2], ALU.add, hf.k() + fx_l.k(), hf.k())
                TT("dve", ycv_l[:, j, 0:2], hf[:, j, 2:4], fx_l[:, j, 2:4], ALU.mult, hf.k() + fx_l.k(), ycv_l.ke(j * 512, 512))
            attention(l, m, "mla")
            attention(l, m, "sb")
            hT = norm_hT(l, m, 0)
            for grp in range(2):
                for nbr in range(4):
                    gt, gvw = load_wblk(win_v[l, :, :, COL_GATE + nbr * 1024 + grp * 512:COL_GATE + nbr * 1024 + (grp + 1) * 512], 0)
                    if nbr < 2:
                        bt, bvw = load_wblk(wbr_v[l, nbr].rearrange("(k p) n -> p k n", p=128)[:, :, grp * 512:(grp + 1) * 512], 0)
                    else:
                        bt, bvw = load_wblk(wbr_v[l, nbr].rearrange("(h p) n -> p h n", p=64)[:, :, grp * 512:(grp + 1) * 512], 0)
                    for dcl in range(4):
                        pg = ps_main.get()
                        proj_fm(gt, gvw, dcl * 128, 128, hT, pg)
                        sg_ = sig_r.get()
                        ACT(sg_[:], pg[:], AF.Sigmoid, pg.k(), sg_.k())
                        pu = ps_main.get()
                        if nbr < 2:
                            src = ysg_l if nbr == 0 else ycv_l
                            for k in range(4):
                                MM(pu[:], bvw[:, k, dcl * 128:(dcl + 1) * 128], src[:, k, :], k == 0, k == 3, bt.k() + src.k(), pu.k())
                        else:
                            src = ymla if nbr == 2 else ysb
                            for hh in range(8):
                                MM(pu[:], bvw[0:64, hh, dcl * 128:(dcl + 1) * 128], src[:, hh, :], hh == 0, hh == 7, bt.k() + src.k(), pu.k())
                        if nbr == 0:
                            TT("dve", macc[:, dcl, :], sg_[:], pu[:], ALU.mult, sg_.k() + pu.k(), macc.ke(dcl * 512, 512))
                        else:
                            pr = prod_r.get()
                            TT("dve", pr[:], sg_[:], pu[:], ALU.mult, sg_.k() + pu.k(), pr.k())
                            if nbr < 3:
                                TT("pool", macc[:, dcl, :], macc[:, dcl, :], pr[:], ALU.add, macc.ke(dcl * 512, 512) + pr.k(), macc.ke(dcl * 512, 512))
                            else:
                                dc = grp * 4 + dcl
                                TT("pool", mergedT[:, dc, :], macc[:, dcl, :], pr[:], ALU.add, macc.ke(dcl * 512, 512) + pr.k(), mergedT.ke(dc * 512, 512))
            for half in range(2):
                wt, wv = load_wblk(wout_d[l].rearrange("(k p) n -> p k n", p=128)[:, :, half * 512:(half + 1) * 512], 0)
                for dcl in range(4):
                    dc = half * 4 + dcl
                    pp = ps_main.get()
                    for k in range(8):
                        MM(pp[:], wv[:, k, dcl * 128:(dcl + 1) * 128], mergedT[:, k, :], k == 0, k == 7, wt.k() + mergedT.k(), pp.k())
                    STT("dve", xT[:, dc, t0:t0 + 512], pp[:], layc(l, 2, dc), xT[:, dc, t0:t0 + 512], ALU.mult, ALU.add,
                        pp.k() + lay.k() + xk(dc, m), xk(dc, m))
            h2 = norm_hT(l, m, 1)
            for nb in range(8):
                wt, wv = load_wblk(w1_d[l].rearrange("(k p) n -> p k n", p=128)[:, :, nb * 512:(nb + 1) * 512], 0)
                for j in range(4):
                    pp = ps_main.get()
                    proj_fm(wt, wv, j * 128, 128, h2, pp)
                    rl = rl_r.get()
                    ACT(rl[:], pp[:], AF.Relu, pp.k(), rl.k())
                    fi = nb * 4 + j
                    TT("pool", aT[:, fi, :], rl[:], rl[:], ALU.mult, rl.k(), aT.ke(fi * 512, 512))
            for dc in range(8):
                wt = wblk.get()
                wv = wt.h[:, :].rearrange("p (k n) -> p k n", k=32)
                DMA("pool", wv, w2_d[l].rearrange("(k p) n -> p k n", p=128)[:, :, dc * 128:(dc + 1) * 128], (), wt.k())
                pp = ps_main.get()
                for k in range(32):
                    MM(pp[:], wv[:, k, :], aT[:, k, :], k == 0, k == 31, wt.k() + aT.ke(k * 512, 512), pp.k())
                STT("dve", xT[:, dc, t0:t0 + 512], pp[:], layc(l, 5, dc), xT[:, dc, t0:t0 + 512], ALU.mult, ALU.add,
                    pp.k() + lay.k() + xk(dc, m), xk(dc, m))
            if mode == "B":
                for k in range(8):
                    DMA("sp", xTo_d[:, k, t0:t0 + 512], xT[:, k, t0:t0 + 512], xk(k, m), [("xTo", k, m)])
            if last:
                ss = ps_misc.get()
                for k in range(8):
                    xs = xsq_r.get()
                    ACT(xs[:], xT[:, k, t0:t0 + 512], AF.Square, xk(k, m), xs.k())
                    MM(ss[:], ones_bf[:], xs[:], k == 0, k == 7, ones_bf.k() + xs.k(), ss.k())
                ACT(rstd[:], ss[:], AF.Sqrt, ss.k(), rstd.k(), scale=1.0 / D, bias=EPS)
                RECIP(rstd[:], rstd[:], rstd.k(), rstd.k())
                for k in range(8):
                    tmp = fa.get()
                    STT("dve", tmp[:], xT[:, k, t0:t0 + 512], pv[:, PV_FNG + k:PV_FNG + k + 1], rstd[:], ALU.mult, ALU.mult,
                        xk(k, m) + pv.k() + rstd.k(), tmp.k())
                    DMA("sp", outT_d[:, k, t0:t0 + 512], tmp[:], tmp.k(), [("outT", k, m)])

    for l in range(LN):
        if doA:
            phaseA(l)
        if mode == "F":
            for (kd, j) in CHN:
                S.coll((lambda l, kd, j: lambda e: e.collective_compute(
                    "AllGather", ALU.bypass, replica_groups=[[0, 1, 2, 3], [4, 5, 6, 7]],
                    ins=[snd_t[l][(kd, j)].rearrange("(a b) -> a b", b=1024).opt()],
                    outs=[gat_t[l][(kd, j)].rearrange("(a b) -> a b", b=1024).opt()]))(l, kd, j),
                    snd_keys[l][(kd, j)], [("gat", l, kd, j)])
        if doB:
            phaseB(l, last=(mode == "B" or l == LN - 1))
    S.emit()
    return nc, S.stats


def _consts():
    tri = np.triu(np.ones((128, 128), np.float32))
    negU = -np.tril(np.ones((128, 128), np.float32))
    return tri, negU


def _masks(c):
    tri_ns = np.triu(np.ones((128, 128), np.float32))
    tri_s = np.triu(np.ones((128, 128), np.float32), 1)
    mns = np.zeros((16, 128, 512), np.float32)
    mst = np.zeros((16, 128, 512), np.float32)
    for j in range(16):
        for r in range(4):
            qb = 4 * c + r
            if j < qb:
                mns[j, :, r * 128:(r + 1) * 128] = 1.0
                mst[j, :, r * 128:(r + 1) * 128] = 1.0
            elif j == qb:
                mns[j, :, r * 128:(r + 1) * 128] = tri_ns
                mst[j, :, r * 128:(r + 1) * 128] = tri_s
    return mns, mst


def _chunkcols(v):
    return np.ascontiguousarray(v.reshape(-1, 128).T)


def make_pv(inp, b, c, layers):
    cols = [_chunkcols(inp["c"][b])]
    for l in layers:
        cols.append(_chunkcols(inp["norm1_g"][l]))
        cols.append(_chunkcols(inp["norm2_g"][l]))
        cols.append(_chunkcols(inp["sg_norm_g"][l]))
        for j in range(3):
            cols.append(_chunkcols(inp["conv_w"][l, j]))
        cols.append(_chunkcols(inp["mla_q_norm_g"][l]))
        cols.append(_chunkcols(inp["mla_kv_norm_g"][l]))
    cols.append(_chunkcols(inp["final_norm_g"]))
    sign = np.where((np.arange(128) % 32) < 16, -1.0, 1.0).astype(np.float32)[:, None]
    cols.append(sign)
    oh = np.zeros((128, 4), np.float32)
    oh[:, c] = 1.0
    cols.append(oh)
    return np.ascontiguousarray(np.concatenate(cols, axis=1).astype(np.float32))


def layer_weights(inp, layers):
    ls = list(layers)
    w = {}
    w["ada_w"] = np.ascontiguousarray(inp["ada_w"][ls])
    w["adab"] = np.ascontiguousarray(inp["ada_b"][ls].reshape(1, -1))
    w["w_in"] = np.ascontiguousarray(inp["w_in"][ls])
    w["sgwT"] = np.ascontiguousarray(np.transpose(inp["sg_w"][ls], (0, 3, 1, 2)))
    w["sgb"] = np.ascontiguousarray(np.tile(inp["sg_b"][ls][:, :, None, :], (1, 1, 4, 1)).reshape(len(ls), 1, 4 * 512))
    uq = inp["mla_w_uq"][ls].reshape(len(ls), 256, 8, 96)
    pe = uq[..., 64:96]
    pesw = np.concatenate([pe[..., 16:32], pe[..., 0:16]], axis=-1)
    w["wq"] = np.ascontiguousarray(np.concatenate([uq, pesw], axis=-1).reshape(len(ls), 256, 8 * 128))
    w["wukv"] = np.ascontiguousarray(inp["mla_w_ukv"][ls])
    kpe = inp["w_in"][ls][:, :, COL_MLA + 384:COL_MLA + 416]
    kpesw = np.concatenate([kpe[..., 16:32], kpe[..., 0:16]], axis=-1)
    z = np.zeros(kpe.shape[:2] + (64,), np.float32)
    w["wkpe"] = np.ascontiguousarray(np.concatenate([z, kpe, z, kpesw], axis=-1))
    w["w_branch"] = np.ascontiguousarray(inp["w_branch"][ls])
    w["w_out"] = np.ascontiguousarray(inp["w_out"][ls])
    w["w1"] = np.ascontiguousarray(inp["mlp_w1"][ls])
    w["w2"] = np.ascontiguousarray(inp["mlp_w2"][ls])
    return w


A_KEYS = ["ada_w", "adab", "w_in", "sgwT", "sgb", "wq", "wukv", "wkpe"]
B_KEYS = ["ada_w", "adab", "w_in", "w_branch", "w_out", "w1", "w2"]

_cache = {}


def _get(mode, NSB, LN):
    key = (mode, NSB, LN)
    if key not in _cache:
        _cache[key] = build(mode, NSB, LN)
    return _cache[key][0]


def run_model(inp, NSB, depth, fused=False):
    x = np.asarray(inp["x"], np.float32)
    Bsz, SEQ, _ = x.shape
    NTOK = NSB * 512
    assert SEQ == 4 * NTOK and Bsz == 2
    inv_freq = (10000.0 ** (-np.arange(0, 32, 2, dtype=np.float32) / 32)).astype(np.float32)
    invf = np.tile(inv_freq, 8)[None, :].astype(np.float32)
    tri, negU = _consts()
    cores = [(b, c) for b in range(2) for c in range(4)]

    def tok_idx(c):
        return np.concatenate([np.arange((4 * m + c) * 512, (4 * m + c + 1) * 512) for m in range(NSB)])

    xTs = []
    for (b, c) in cores:
        xt = x[b, tok_idx(c), :].T
        xTs.append(np.ascontiguousarray(xt.reshape(8, 128, NTOK).transpose(1, 0, 2)))
    posis = [np.ascontiguousarray(np.asarray(inp["positions"])[b, tok_idx(c)][None, :].astype(np.int32)) for (b, c) in cores]
    masks = [_masks(c) for (b, c) in cores]
    outT = None
    if fused:
        w = layer_weights(inp, range(depth))
        nc = _get("F", NSB, depth)
        in_maps = []
        for i, (b, c) in enumerate(cores):
            d = dict(xT=xTs[i], pv=make_pv(inp, b, c, range(depth)), posi=posis[i], invf=invf, tri=tri, negU=negU,
                     mask_ns=masks[i][0], mask_s=masks[i][1])
            for k_ in set(A_KEYS + B_KEYS):
                d[k_] = w[k_]
            in_maps.append(d)
        res = run_bass_kernel_spmd(nc, in_maps, core_ids=list(range(8)))
        outT = [r["outT"] for r in res.results]
    else:
        ncA = _get("A", NSB, 1)
        ncB = _get("B", NSB, 1)
        for l in range(depth):
            w = layer_weights(inp, [l])
            in_maps = []
            for i, (b, c) in enumerate(cores):
                d = dict(xT=xTs[i], pv=make_pv(inp, b, c, [l]), posi=posis[i], invf=invf, tri=tri)
                for k_ in A_KEYS:
                    d[k_] = w[k_]
                in_maps.append(d)
            resA = run_bass_kernel_spmd(ncA, in_maps, core_ids=list(range(8))).results
            in_maps = []
            for i, (b, c) in enumerate(cores):
                gats = {}
                for kd in ("HL", "KM", "VM", "KS", "VS"):
                    for j in range(1 if kd == "HL" else 4):
                        nm = "0_%s%d" % (kd, j)
                        gats["gat" + nm] = np.concatenate([resA[b * 4 + cc]["snd" + nm] for cc in range(4)])
                d = dict(xT=xTs[i], pv=make_pv(inp, b, c, [l]), negU=negU, mask_ns=masks[i][0], mask_s=masks[i][1],
                         qm=resA[i]["qm"], qs=resA[i]["qs"], ysg=resA[i]["ysg"], ycv=resA[i]["ycv"], fx=resA[i]["fx"])
                for k_ in B_KEYS:
                    d[k_] = w[k_]
                d.update(gats)
                in_maps.append(d)
            resB = run_bass_kernel_spmd(ncB, in_maps, core_ids=list(range(8))).results
            xTs = [r["xTo"] for r in resB]
            outT = [r["outT"] for r in resB]
    out = np.zeros((2, SEQ, D), np.float32)
    for i, (b, c) in enumerate(cores):
        o = outT[i].transpose(1, 0, 2).reshape(1024, NTOK).T
        out[b, tok_idx(c), :] = o
    return out


def kernel(**inputs):
    return run_model(inputs, NSB=4, depth=4, fused=True)
```

```python
import numpy as np
import concourse.bass as bass
import concourse.mybir as mybir
from concourse.bass_utils import run_bass_kernel_spmd

F32 = mybir.dt.float32
BF16 = mybir.dt.bfloat16
I32 = mybir.dt.int32
AF = mybir.ActivationFunctionType
ALU = mybir.AluOpType

SEM_LIMIT = 30000
N_DMA_SEMS = 12
SLOT = 512

D = 1024
COL_SG, COL_CONV, COL_MLA, COL_SB, COL_GATE, IN_COLS = 0, 1024, 2560, 2976, 4512, 8608
EPS = 1e-6
PVL = 35


class Sched:
    def __init__(self, nc):
        self.nc = nc
        self.ops = []
        self.tags = {}

    def hoist(self):
        groups = {}
        for i, t in self.tags.items():
            groups.setdefault(t, []).append(i)
        order = sorted(groups)
        nxt = {order[i]: order[i + 1] for i in range(len(order) - 1)}
        firstpos = {t: min(v) for t, v in groups.items()}
        new = []
        done = set()
        for i, o in enumerate(self.ops):
            t = self.tags.get(i)
            if t is None:
                new.append(o)
                continue
            if t == order[0] and t not in done:
                for j in sorted(groups[t]):
                    new.append(self.ops[j])
                done.add(t)
            if i == firstpos[t]:
                n_ = nxt.get(t)
                if n_ is not None and n_ not in done:
                    for j in sorted(groups[n_]):
                        new.append(self.ops[j])
                    done.add(n_)
        assert len(new) == len(self.ops)
        self.ops = new
        self.tags = {}

    def op(self, eng, fn, reads=(), writes=()):
        self.ops.append((eng, fn, tuple(reads), tuple(writes), False))

    def dma(self, q, fn, reads=(), writes=(), tag=None):
        self.ops.append((q, fn, tuple(reads), tuple(writes), True))
        if tag is not None:
            self.tags[len(self.ops) - 1] = tag

    def coll(self, fn, reads=(), writes=()):
        self.ops.append(("pool", fn, tuple(reads), tuple(writes), "coll"))

    def emit(self):
        self.hoist()
        nc = self.nc
        engs = {"pe": nc.tensor, "act": nc.scalar, "dve": nc.vector, "pool": nc.gpsimd, "sp": nc.sync}
        ops = self.ops
        n = len(ops)
        last_w = {}
        rd_c = {}
        rd_d = {}
        deps = [None] * n
        need_sig = [False] * n
        for i, (e, fn, rs, ws, isd) in enumerate(ops):
            d = set()
            for k in rs:
                j = last_w.get(k)
                if j is not None:
                    d.add(j)
            for k in ws:
                j = last_w.get(k)
                if j is not None:
                    d.add(j)
                rc = rd_c.get(k)
                if rc:
                    d.update(rc.values())
                rdd = rd_d.get(k)
                if rdd:
                    d.update(rdd)
            for k in rs:
                if isd:
                    rd_d.setdefault(k, []).append(i)
                else:
                    rd_c.setdefault(k, {})[e] = i
            for k in ws:
                last_w[k] = i
                rd_c[k] = {}
                rd_d[k] = []
            nd = set()
            for j in d:
                if j == i:
                    continue
                ej, _, _, _, jd = ops[j]
                if (not isd) and (not jd) and e == "pe" and ej == "pe":
                    continue
                nd.add(j)
                if not jd:
                    need_sig[j] = True
            deps[i] = nd
        self.sem_ctx = []

        def new_sem(name):
            cm = nc.semaphore(name)
            s = cm.__enter__()
            self.sem_ctx.append(cm)
            return s

        cur = {}
        sig = [None] * n
        cnt = [0]
        dsem, dstate, dcount = {}, {}, {}
        prev_on_sem = [None] * n
        for i, (e, fn, rs, ws, isd) in enumerate(ops):
            if isd == "coll":
                sig[i] = (new_sem("coll%d" % cnt[0]), 1, 1)
                cnt[0] += 1
            elif isd:
                nds = 3 if e == "pool" else N_DMA_SEMS
                if e not in dsem:
                    dsem[e] = [new_sem("d%s%d" % (e, t)) for t in range(nds)]
                    dstate[e] = [[0, None] for _ in range(nds)]
                    dcount[e] = 0
                t = dcount[e] % nds
                dcount[e] += 1
                st = dstate[e][t]
                if st[0] + 16 > SEM_LIMIT:
                    dsem[e][t] = new_sem("d%s%dx%d" % (e, t, cnt[0]))
                    cnt[0] += 1
                    st[0] = 0
                if st[1] is not None:
                    prev_on_sem[i] = st[1]
                st[0] += 16
                sig[i] = (dsem[e][t], st[0], 16)
                st[1] = i
            elif need_sig[i]:
                if e not in cur or cur[e][1] + 1 > SEM_LIMIT:
                    cur[e] = [new_sem("c%s%d" % (e, cnt[0])), 0]
                    cnt[0] += 1
                cur[e][1] += 1
                sig[i] = (cur[e][0], cur[e][1], 1)
        waited = {}
        nwaits = 0
        for i, (e, fn, rs, ws, isd) in enumerate(ops):
            eng = engs[e]
            dl = list(deps[i])
            if prev_on_sem[i] is not None:
                dl.append(prev_on_sem[i])
            mx = {}
            for j in dl:
                s, v, _ = sig[j]
                key = id(s)
                if key not in mx or mx[key][1] < v:
                    mx[key] = (s, v)
            for key, (s, v) in mx.items():
                wk = (e, key)
                if waited.get(wk, 0) >= v:
                    continue
                waited[wk] = v
                eng.wait_ge(s, v)
                nwaits += 1
            inst = fn(eng)
            if sig[i] is not None:
                inst.then_inc(sig[i][0], sig[i][2])
        feng = engs["sp"]
        for e in dsem:
            for t in range(len(dstate[e])):
                st = dstate[e][t]
                if st[1] is not None:
                    s, v, _ = sig[st[1]]
                    feng.wait_ge(s, v)
        self.stats = dict(n_ops=n, n_waits=nwaits, n_sems=len(self.sem_ctx))


_DTS = {F32: 4, BF16: 2, I32: 4}


class Tile:
    def __init__(self, h, off, nbytes, esz):
        self.h, self.off, self.nbytes, self.esz = h, off, nbytes, esz
        self._all = tuple(range(off // SLOT, (off + nbytes + SLOT - 1) // SLOT))

    def __getitem__(self, idx):
        return self.h[idx]

    def k(self):
        return self._all

    def ke(self, e0, ne):
        lo = self.off + e0 * self.esz
        hi = lo + ne * self.esz
        return tuple(range(lo // SLOT, (hi + SLOT - 1) // SLOT))


class Cursor:
    def __init__(self, nc, base, limit, tag):
        self.nc, self.cur, self.limit, self.tag, self.n = nc, base, limit, tag, 0

    def alloc(self, name, shape, dt):
        esz = _DTS[dt]
        nb = esz
        for s in shape[1:]:
            nb *= s
        off = (self.cur + SLOT - 1) // SLOT * SLOT
        assert off + nb <= self.limit, (self.tag, name, off, nb, self.limit)
        self.cur = off + nb
        self.n += 1
        h = self.nc.alloc_sbuf_tensor_at("%s_%s_%d" % (self.tag, name, self.n), list(shape), dt, offset=off)
        return Tile(h, off, nb, esz)

    def fork(self, tag):
        return Cursor(self.nc, self.cur, self.limit, tag)


class Ring:
    def __init__(self, tiles):
        self.t, self.i = tiles, 0

    def get(self):
        t = self.t[self.i % len(self.t)]
        self.i += 1
        return t


class PS:
    def __init__(self, h, bank):
        self.h, self.bank = h, bank

    def __getitem__(self, idx):
        return self.h[idx]

    def k(self):
        return (("ps", self.bank),)


def snd_layout(NTOK, NSB):
    o = {}
    cur = 0
    o["KM"] = cur; cur += 8 * 96 * NTOK
    o["VM"] = cur; cur += 8 * NTOK * 65
    o["KS"] = cur; cur += 8 * 64 * NTOK
    o["VS"] = cur; cur += 8 * NTOK * 64
    o["HALO"] = cur; cur += NSB * 4 * 128 * 2
    o["N"] = cur
    return o


def build(mode, NSB, LN):
    nc = bass.Bass("TRN2", target_bir_lowering=False)
    S = Sched(nc)
    NTOK = NSB * 512
    NPV = 8 + PVL * LN + 8 + 1 + 4
    PV_C, PV_L, PV_FNG = 0, 8, 8 + PVL * LN
    PV_SIGN, PV_OH = PV_FNG + 8, PV_FNG + 9
    SL = snd_layout(NTOK, NSB)
    NSND = SL["N"]
    doA = mode in ("A", "F")
    doB = mode in ("B", "F")

    def din(name, shape, dt=F32):
        return nc.dram_tensor(name, list(shape), dt, kind="ExternalInput").ap()

    def dout(name, shape, dt=F32):
        return nc.dram_tensor(name, list(shape), dt, kind="ExternalOutput").ap()

    def dint(name, shape, dt=F32):
        return nc.dram_tensor(name, list(shape), dt, kind="Internal").ap()

    def dAB(name, shape, dt):
        if mode == "A":
            return dout(name, shape, dt)
        if mode == "B":
            return din(name, shape, dt)
        return dint(name, shape, dt)

    xT_d = din("xT", [128, 8, NTOK])
    pv_d = din("pv", [128, NPV])
    adab_d = din("adab", [1, LN * 6144])
    adaw_d = din("ada_w", [LN, 1024, 6144])
    if doA:
        posi_d = din("posi", [1, NTOK], I32)
        invf_d = din("invf", [1, 128])
        win_d = din("w_in", [LN, 1024, IN_COLS])
        sgwT_d = din("sgwT", [LN, 128, 4, 128])
        sgb_d = din("sgb", [LN, 1, 4 * 512])
        wq_d = din("wq", [LN, 256, 8 * 128])
        wukv_d = din("wukv", [LN, 128, 1024])
        wkpe_d = din("wkpe", [LN, 1024, 2 * 96])
        tri_d = din("tri", [128, 128])
    if doB:
        if not doA:
            win_d = din("w_in", [LN, 1024, IN_COLS])
        wbr_d = din("w_branch", [LN, 4, 512, 1024])
        wout_d = din("w_out", [LN, 1024, 1024])
        w1_d = din("w1", [LN, 1024, 4096])
        w2_d = din("w2", [LN, 4096, 1024])
        mns_d = din("mask_ns", [16, 128, 512])
        mst_d = din("mask_s", [16, 128, 512])
        negu_d = din("negU", [128, 128])
        cw_unused = None
    CHK = {"KM": 2 * 96 * NTOK, "VM": 2 * NTOK * 65, "KS": 2 * 64 * NTOK, "VS": 2 * NTOK * 64, "HL": NSB * 1024}
    CHN = [("HL", 0)] + [(kd, j) for j in range(4) for kd in ("KM", "VM")] + [(kd, j) for j in range(4) for kd in ("KS", "VS")]
    snd_t = [dict() for _ in range(LN)]
    gat_t = [dict() for _ in range(LN)]
    for l in range(LN):
        for (kd, j) in CHN:
            nm = "%d_%s%d" % (l, kd, j)
            if mode != "B":
                snd_t[l][(kd, j)] = dAB("snd" + nm, [CHK[kd]], BF16)
            if mode == "B":
                gat_t[l][(kd, j)] = din("gat" + nm, [4 * CHK[kd]], BF16)
            elif mode == "F":
                gat_t[l][(kd, j)] = dint("gat" + nm, [4 * CHK[kd]], BF16)
    qm_d = dAB("qm", [NSB, 8, 96, 512], BF16)
    qs_d = dAB("qs", [NSB, 8, 64, 512], BF16)
    ysg_d = dAB("ysg", [128, 4, NTOK], BF16)
    ycv_d = dAB("ycv", [128, 4, NTOK], BF16)
    fx_d = dAB("fx", [128, NSB, 4, 4], F32)
    if mode == "B":
        xTo_d = dout("xTo", [128, 8, NTOK])
    if doB:
        outT_d = dout("outT", [128, 8, NTOK])

    BASE = 16896
    LIMIT = nc.SBUF_PARTITION_SIZE_BYTES
    R = Cursor(nc, BASE, LIMIT, "r")
    xT = R.alloc("xT", [128, 8, NTOK], F32)
    pv = R.alloc("pv", [128, NPV], F32)
    lay = R.alloc("lay", [128, LN * 48], F32)
    ones_bf = R.alloc("ones_bf", [128, 128], BF16)
    ones_f = R.alloc("ones_f", [128, 128], F32)
    wblk = Ring([R.alloc("wblk%d" % i, [128, 8 * 512], BF16) for i in range(3)])
    hTr = Ring([R.alloc("hT%d" % i, [128, 8, 512], BF16) for i in range(2)])
    xsq_r = Ring([R.alloc("xsq%d" % i, [128, 512], BF16) for i in range(2)])
    rstd = R.alloc("rstd", [128, 512], F32)
    fa = Ring([R.alloc("fa%d" % i, [128, 512], F32) for i in range(3)])
    if doA:
        WsT = R.alloc("WsT", [128, 4, 128], BF16)
        tri = R.alloc("tri", [128, 128], BF16)
        bsb = R.alloc("bsb", [128, 4, 512], F32)
        wq = R.alloc("wq", [128, 2, 8 * 128], BF16)
        wukv = R.alloc("wukv", [128, 1024], BF16)
        wkpe = R.alloc("wkpe", [128, 8, 192], BF16)
    if doB:
        negU = R.alloc("negU", [128, 128], BF16)
        negones = R.alloc("negones", [128, 128], BF16)
    ARENA = R.cur

    psb = []
    for i in range(8):
        cm = nc.psum_tensor("psb%d" % i, [128, 512], F32)
        psb.append(PS(cm.__enter__(), i))
    ps_main = Ring(psb[0:4])
    ps_acc = Ring(psb[4:6])
    ps_misc = Ring(psb[6:8])

    def MM(out, lhsT, rhs, start, stop, r, w, **kw):
        S.op("pe", lambda e: e.matmul(out, lhsT, rhs, start=start, stop=stop, **kw), r, w)

    def ACT(out, in_, func, r, w, **kw):
        S.op("act", lambda e: e.activation(out=out, in_=in_, func=func, **kw), r, w)

    def TT(eng, out, a, b, op, r, w):
        S.op(eng, lambda e: e.tensor_tensor(out=out, in0=a, in1=b, op=op), r, w)

    def TS(eng, out, a, s1, s2, op0, op1, r, w):
        if op1 is None:
            S.op(eng, lambda e: e.tensor_scalar(out=out, in0=a, scalar1=s1, scalar2=None, op0=op0), r, w)
        else:
            S.op(eng, lambda e: e.tensor_scalar(out=out, in0=a, scalar1=s1, scalar2=s2, op0=op0, op1=op1), r, w)

    def STT(eng, out, in0, scalar, in1, op0, op1, r, w):
        S.op(eng, lambda e: e.scalar_tensor_tensor(out=out, in0=in0, scalar=scalar, in1=in1, op0=op0, op1=op1), r, w)

    def CP(eng, out, in_, r, w):
        if eng == "act":
            S.op("act", lambda e: e.activation(out=out, in_=in_, func=AF.Identity), r, w)
        else:
            S.op(eng, lambda e: e.tensor_copy(out=out, in_=in_), r, w)

    def RECIP(out, in_, r, w):
        S.op("dve", lambda e: e.reciprocal(out=out, in_=in_), r, w)

    def MEMSET(eng, out, val, w):
        S.op(eng, lambda e: e.memset(out, val), (), w)

    def DMA(q, out, in_, r, w, tag=None):
        S.dma(q, lambda e: e.dma_start(out=out, in_=in_), r, w, tag=tag)

    wcount = [0]

    def wget():
        wcount[0] += 1
        return wblk.get(), wcount[0]

    def xk(k, m):
        return xT.ke(k * NTOK + m * 512, 512)

    def layc(l, j, k):
        c = l * 48 + j * 8 + k
        return lay[:, c:c + 1]

    def pvl(l, j):
        c = PV_L + l * PVL + j
        return pv[:, c:c + 1]

    DMA("sp", pv[:], pv_d[:, :], (), pv.k())
    for k in range(8):
        DMA("sp", xT[:, k, :], xT_d[:, k, :], (), xT.ke(k * NTOK, NTOK))
    MEMSET("dve", ones_bf[:], 1.0, ones_bf.k())
    MEMSET("dve", ones_f[:], 1.0, ones_f.k())
    if doB:
        MEMSET("dve", negones[:], -1.0, negones.k())
        DMA("pool", negU[:], negu_d[:, :], (), negU.k())
    if doA:
        DMA("pool", tri[:], tri_d[:, :], (), tri.k())

    P = Cursor(nc, ARENA, LIMIT, "p")
    siluc = P.alloc("siluc", [128, 8], BF16)
    modrow = P.alloc("modrow", [1, 6144], F32)
    adab_t = P.alloc("adab", [1, 6144], F32)
    modT = P.alloc("modT", [128, LN * 48], F32)
    ACT(siluc[:], pv[:, PV_C:PV_C + 8], AF.Silu, pv.k(), siluc.k())
    adaw_v = adaw_d.rearrange("l (k p) n -> l p k n", p=128)
    for l in range(LN):
        DMA("sp", adab_t[:], adab_d[:, l * 6144:(l + 1) * 6144], (), adab_t.k())
        for nb in range(12):
            wt, wtag = wget()
            wv = wt.h[:, :].rearrange("p (k n) -> p k n", k=8)
            DMA("pool", wv, adaw_v[l, :, :, nb * 512:(nb + 1) * 512], (), wt.k(), tag=wtag)
            pp = ps_main.get()
            for k in range(8):
                MM(pp[0:1, :], siluc[:, k:k + 1], wv[:, k, :], k == 0, k == 7, siluc.k() + wt.k(), pp.k())
            c0 = nb * 512
            TT("dve", modrow[0:1, c0:c0 + 512], pp[0:1, :], adab_t[0:1, c0:c0 + 512], ALU.add,
               pp.k() + adab_t.ke(c0, 512), modrow.ke(c0, 512))
        pm = ps_misc.get()
        for j in range(48):
            c0 = j * 128
            MM(pm[:, j:j + 1], modrow[0:1, c0:c0 + 128], ones_f[0:1, 0:1], True, True,
               modrow.ke(c0, 128) + ones_f.k(), pm.k())
        CP("dve", modT[:, l * 48:(l + 1) * 48], pm[:, 0:48], pm.k(), modT.k())
        b = l * 48
        for which in range(2):
            sh = modT[:, b + which * 24:b + which * 24 + 8]
            sc = modT[:, b + which * 24 + 8:b + which * 24 + 16]
            g = modT[:, b + which * 24 + 16:b + which * 24 + 24]
            gain = pv[:, PV_L + l * PVL + which * 8:PV_L + l * PVL + which * 8 + 8]
            Acol = lay[:, b + which * 24:b + which * 24 + 8]
            Bcol = lay[:, b + which * 24 + 8:b + which * 24 + 16]
            Gcol = lay[:, b + which * 24 + 16:b + which * 24 + 24]
            STT("dve", Acol, sc, 1.0, gain, ALU.add, ALU.mult, modT.k() + pv.k(), lay.k())
            CP("dve", Bcol, sh, modT.k(), lay.k())
            CP("dve", Gcol, g, modT.k(), lay.k())

    def norm_hT(l, m, which):
        hT = hTr.get()
        ss = ps_misc.get()
        for k in range(8):
            xs = xsq_r.get()
            ACT(xs[:], xT[:, k, m * 512:(m + 1) * 512], AF.Square, xk(k, m), xs.k())
            MM(ss[:], ones_bf[:], xs[:], k == 0, k == 7, ones_bf.k() + xs.k(), ss.k())
        ACT(rstd[:], ss[:], AF.Sqrt, ss.k(), rstd.k(), scale=1.0 / D, bias=EPS)
        RECIP(rstd[:], rstd[:], rstd.k(), rstd.k())
        for k in range(8):
            tmp = fa.get()
            TT("dve", tmp[:], xT[:, k, m * 512:(m + 1) * 512], rstd[:], ALU.mult, xk(k, m) + rstd.k(), tmp.k())
            ACT(hT[:, k, :], tmp[:], AF.Identity, tmp.k() + lay.k(), hT.ke(k * 512, 512),
                scale=layc(l, which * 3, k), bias=layc(l, which * 3 + 1, k))
        return hT

    def load_wblk(src_ap, ncols_total):
        wt, wtag = wget()
        Pn, Kc, n = src_ap.shape[0], src_ap.shape[1], src_ap.shape[2]
        wv = wt.h[0:Pn, 0:Kc * n].rearrange("p (k n) -> p k n", k=Kc)
        DMA("pool", wv, src_ap, (), wt.k(), tag=wtag)
        return wt, wv

    win_v = win_d.rearrange("l (k p) n -> l p k n", p=128)

    def proj_fm(wt, wv, c0, M, hT, pp, prow=None):
        for k in range(8):
            MM(pp[0:M, :], wv[:, k, c0:c0 + M], hT[:, k, :], k == 0, k == 7, wt.k() + hT.k(), pp.k())

    if doA:
        A = Cursor(nc, ARENA, LIMIT, "a")
        Ct = A.alloc("C", [128, 512], F32)
        Sgt = A.alloc("Sg", [128, 512], F32)
        posi = A.alloc("posi", [1, 512], I32)
        posf = A.alloc("posf", [1, 512], F32)
        invf = A.alloc("invf", [1, 128], F32)
        angi = A.alloc("angi", [128, 512], I32)
        uT = A.alloc("uT", [128, 4, 512], BF16)
        vhat = A.alloc("vhat", [128, 4, 512], BF16)
        ysgo = A.alloc("ysgo", [128, 4, 512], BF16)
        ycvo = A.alloc("ycvo", [128, 4, 512], BF16)
        tt_r = Ring([A.alloc("tt%d" % i, [128, 514], F32) for i in range(2)])
        cq_sb = A.alloc("cq_sb", [128, 3, 512], F32)
        cn = A.alloc("cn", [128, 3, 512], BF16)
        qo_r = Ring([A.alloc("qo%d" % i, [96, 512], BF16) for i in range(2)])
        kTo = A.alloc("kTo", [96, 8, 512], BF16)
        vxo = A.alloc("vxo", [128, 4, 8 * 65], BF16)
        sbp_r = Ring([A.alloc("sbp%d" % i, [128, 512], BF16) for i in range(3)])
        vso = A.alloc("vso", [128, 4, 512], BF16)
        fxt = A.alloc("fxt", [128, 4, 4], F32)
        halo_o = A.alloc("halo_o", [128, 4, 2], BF16)
        mvst = A.alloc("mvst", [128, 8], F32)
        brow = A.alloc("brow", [1, 512], F32)
        fb = fa


    snd_keys = [dict() for _ in range(LN)]

    def sk(l, kd, j):
        lst = snd_keys[l].setdefault((kd, j), [])
        k_ = ("snd", l, kd, j, len(lst))
        lst.append(k_)
        return [k_]

    def phaseA(l):
        DMA("sp", invf[:], invf_d[:, :], (), invf.k())
        MEMSET("dve", vxo[:], 1.0, vxo.k())
        for i in range(2):
            t_ = tt_r.t[i]
            MEMSET("dve", t_[:, 0:2], 0.0, t_.k())
        sgw_t, sgw_v = load_wblk(sgwT_d[l, :, :, :], 0)
        for g in range(4):
            TT("dve", WsT[:, g, :], sgw_v[:, g, :], tri[:], ALU.mult, sgw_t.k() + tri.k(), WsT.k())
        for g in range(4):
            DMA("sp", brow[:], sgb_d[l, :, g * 512:(g + 1) * 512], (), brow.k())
            pp = ps_misc.get()
            MM(pp[:], ones_f[0:1, :], brow[0:1, :], True, True, ones_f.k() + brow.k(), pp.k())
            CP("act", bsb[:, g, :], pp[:], pp.k(), bsb.ke(g * 512, 512))
        DMA("pool", wq[:], wq_d[l].rearrange("(k p) n -> p k n", p=128), (), wq.k())
        DMA("pool", wukv[:], wukv_d[l, :, :], (), wukv.k())
        DMA("pool", wkpe[:], wkpe_d[l].rearrange("(k p) n -> p k n", p=128), (), wkpe.k())
        sndt = snd_t[l]
        for m in range(NSB):
            t0 = m * 512
            DMA("sp", posi[:], posi_d[:, t0:t0 + 512], (), posi.k())
            CP("dve", posf[:], posi[:], posi.k(), posf.k())
            pa = ps_misc.get()
            MM(pa[:], invf[0:1, :], posf[0:1, :], True, True, invf.k() + posf.k(), pa.k())
            for (dst, shift) in ((Sgt, 0.0), (Ct, 0.25)):
                y_ = fb.get()
                TS("dve", y_[:], pa[:], 1.0 / (2 * np.pi), shift, ALU.mult, ALU.add, pa.k(), y_.k())
                CP("dve", angi[:], y_[:], y_.k(), angi.k())
                y2 = fb.get()
                CP("dve", y2[:], angi[:], angi.k(), y2.k())
                TT("dve", y_[:], y_[:], y2[:], ALU.subtract, y_.k() + y2.k(), y_.k())
                ACT(dst[:], y_[:], AF.Sin, y_.k(), dst.k(), scale=float(2 * np.pi))
            TS("dve", Sgt[:], Sgt[:], pv[:, PV_SIGN:PV_SIGN + 1], None, ALU.mult, None, Sgt.k() + pv.k(), Sgt.k())

            hT = norm_hT(l, m, 0)
            wt, wv = load_wblk(win_v[l, :, :, COL_SG:COL_SG + 512], 0)
            for j in range(4):
                pp = ps_main.get()
                proj_fm(wt, wv, j * 128, 128, hT, pp)
                ACT(uT[:, j, :], pp[:], AF.Gelu, pp.k(), uT.ke(j * 512, 512))
            wt, wv = load_wblk(win_v[l, :, :, COL_SG + 512:COL_SG + 1024], 0)
            for r in range(4):
                pp = ps_main.get()
                for k in range(8):
                    MM(pp[:], hT[:, k, r * 128:(r + 1) * 128], wv[:, k, :], k == 0, k == 7, wt.k() + hT.k(), pp.k())
                g_ = fb.get()
                ACT(g_[:], pp[:], AF.Gelu, pp.k(), g_.k())
                S.op("dve", (lambda g_: lambda e: e.bn_stats(out=mvst[:, 0:6], in_=g_[:]))(g_), g_.k(), mvst.k())
                S.op("dve", lambda e: e.bn_aggr(out=mvst[:, 6:8], in_=mvst[:, 0:6]), mvst.k(), mvst.k())
                ACT(mvst[:, 7:8], mvst[:, 7:8], AF.Sqrt, mvst.k(), mvst.k(), bias=EPS)
                RECIP(mvst[:, 7:8], mvst[:, 7:8], mvst.k(), mvst.k())
                TS("dve", vhat[:, r, :], g_[:], mvst[:, 6:7], mvst[:, 7:8], ALU.subtract, ALU.mult,
                   g_.k() + mvst.k(), vhat.ke(r * 512, 512))
            for g in range(4):
                pp = ps_main.get()
                for r in range(4):
                    MM(pp[:, r * 128:(r + 1) * 128], vhat[:, r, g * 128:(g + 1) * 128], WsT[:, g, :], True, True,
                       vhat.k() + WsT.k(), pp.k())
                tmp = fb.get()
                STT("dve", tmp[:], pp[:], pvl(l, 16 + g), bsb[:, g, :], ALU.mult, ALU.add,
                    pp.k() + pv.k() + bsb.ke(g * 512, 512), tmp.k())
                TT("pool", ysgo[:, g, :], tmp[:], uT[:, g, :], ALU.mult, tmp.k() + uT.ke(g * 512, 512), ysgo.ke(g * 512, 512))
            DMA("sp", ysg_d[:, :, t0:t0 + 512], ysgo[:], ysgo.k(), [("ysg", m)])
            for j in range(4):
                cwt, cwtag = wget()
                cwv = cwt.h[:, 0:8 * 384].rearrange("p (k n) -> p k n", k=8)
                for i in range(3):
                    DMA("pool", cwv[:, :, i * 128:(i + 1) * 128],
                        win_v[l, :, :, COL_CONV + i * 512 + j * 128:COL_CONV + i * 512 + (j + 1) * 128], (), cwt.k(), tag=cwtag)
                pgc = ps_main.get()
                proj_fm(cwt, cwv, 128, 128, hT, pgc)
                gc = fb.get()
                CP("act", gc[:], pgc[:], pgc.k(), gc.k())
                pxv = ps_main.get()
                proj_fm(cwt, cwv, 256, 128, hT, pxv)
                t_ = tt_r.get()
                TT("dve", t_[:, 2:514], gc[:], pxv[:], ALU.mult, gc.k() + pxv.k(), t_.k())
                acc = fb.get()
                TS("dve", acc[:], t_[:, 2:514], pvl(l, 20 + 8 + j), None, ALU.mult, None, t_.k() + pv.k(), acc.k())
                STT("dve", acc[:], t_[:, 1:513], pvl(l, 20 + 4 + j), acc[:], ALU.mult, ALU.add, t_.k() + pv.k() + acc.k(), acc.k())
                STT("dve", acc[:], t_[:, 0:512], pvl(l, 20 + j), acc[:], ALU.mult, ALU.add, t_.k() + pv.k() + acc.k(), acc.k())
                pgb = ps_main.get()
                proj_fm(cwt, cwv, 0, 128, hT, pgb)
                TT("dve", ycvo[:, j, :], acc[:], pgb[:], ALU.mult, acc.k() + pgb.k(), ycvo.ke(j * 512, 512))
                CP("pool", fxt[:, j, 0:2], acc[:, 0:2], acc.k(), fxt.k())
                CP("dve", fxt[:, j, 2:4], pgb[:, 0:2], pgb.k(), fxt.k())
                CP("pool", halo_o[:, j, :], t_[:, 512:514], t_.k(), halo_o.k())
            DMA("sp", ycv_d[:, :, t0:t0 + 512], ycvo[:], ycvo.k(), [("ycv", m)])
            DMA("sp", fx_d[:, m, :, :], fxt[:], fxt.k(), [("fx", m)])
            ho = SL["HALO"] + m * 1024
            DMA("sp", sndt[("HL", 0)][m * 1024:(m + 1) * 1024].rearrange("(j p e) -> p j e", j=4, p=128), halo_o[:], halo_o.k(), sk(l, "HL", 0))
            wt, wv = load_wblk(win_v[l, :, :, COL_MLA:COL_MLA + 384], 0)
            for j in range(3):
                pp = ps_main.get()
                proj_fm(wt, wv, j * 128, 128, hT, pp)
                CP("act", cq_sb[:, j, :], pp[:], pp.k(), cq_sb.ke(j * 512, 512))
            for (j0, nj, gcol) in ((0, 2, 32), (2, 1, 34)):
                ss = ps_misc.get()
                for j in range(j0, j0 + nj):
                    xs = xsq_r.get()
                    ACT(xs[:], cq_sb[:, j, :], AF.Square, cq_sb.ke(j * 512, 512), xs.k())
                    MM(ss[:], ones_bf[:], xs[:], j == j0, j == j0 + nj - 1, ones_bf.k() + xs.k(), ss.k())
                rs_ = fb.get()
                ACT(rs_[:], ss[:], AF.Sqrt, ss.k(), rs_.k(), scale=1.0 / (128 * nj), bias=EPS)
                RECIP(rs_[:], rs_[:], rs_.k(), rs_.k())
                for j in range(j0, j0 + nj):
                    STT("dve", cn[:, j, :], cq_sb[:, j, :], pvl(l, gcol + (j - j0)), rs_[:], ALU.mult, ALU.mult,
                        cq_sb.ke(j * 512, 512) + pv.k() + rs_.k(), cn.ke(j * 512, 512))
            pka = ps_main.get()
            pkb = ps_main.get()
            for k in range(8):
                MM(pka[0:96, :], wkpe[:, k, 0:96], hT[:, k, :], k == 0, k == 7, wkpe.k() + hT.k(), pka.k())
            for k in range(8):
                MM(pkb[0:96, :], wkpe[:, k, 96:192], hT[:, k, :], k == 0, k == 7, wkpe.k() + hT.k(), pkb.k())
            t1 = fb.get()
            t2 = fb.get()
            TT("dve", t1[64:96, :], pka[64:96, :], Ct[64:96, :], ALU.mult, pka.k() + Ct.k(), t1.k())
            TT("dve", t2[64:96, :], pkb[64:96, :], Sgt[64:96, :], ALU.mult, pkb.k() + Sgt.k(), t2.k())
            for h in range(8):
                TT("pool", kTo[64:96, h, :], t1[64:96, :], t2[64:96, :], ALU.add, t1.k() + t2.k(), kTo.ke(h * 512, 512))
            for h in range(8):
                pqa = ps_main.get()
                pqb = ps_main.get()
                for k in range(2):
                    MM(pqa[0:96, :], wq[:, k, h * 128:h * 128 + 96], cn[:, k, :], k == 0, k == 1, wq.k() + cn.k(), pqa.k())
                for k in range(2):
                    MM(pqb[0:96, :], wq[:, k, h * 128 + 32:h * 128 + 128], cn[:, k, :], k == 0, k == 1, wq.k() + cn.k(), pqb.k())
                qo = qo_r.get()
                CP("act", qo[0:64, :], pqa[0:64, :], pqa.k(), qo.k())
                t1 = fb.get()
                t2 = fb.get()
                TT("dve", t1[64:96, :], pqa[64:96, :], Ct[64:96, :], ALU.mult, pqa.k() + Ct.k(), t1.k())
                TT("dve", t2[64:96, :], pqb[64:96, :], Sgt[64:96, :], ALU.mult, pqb.k() + Sgt.k(), t2.k())
                TT("pool", qo[64:96, :], t1[64:96, :], t2[64:96, :], ALU.add, t1.k() + t2.k(), qo.k())
                DMA("sp", qm_d[m, h, :, :], qo[:], qo.k(), [("qm", m, h)])
                pkn = ps_main.get()
                MM(pkn[0:64, :], wukv[:, h * 128:h * 128 + 64], cn[:, 2, :], True, True, wukv.k() + cn.k(), pkn.k())
                CP("act", kTo[0:64, h, :], pkn[0:64, :], pkn.k(), kTo.ke(h * 512, 512))
            for j in range(4):
                DMA("sp", sndt[("KM", j)].rearrange("(h r t) -> r h t", h=2, r=96)[:, :, t0:t0 + 512],
                    kTo[:, 2 * j:2 * j + 2, :], kTo.k(), sk(l, "KM", j))
            wv_v = wukv.h[:, :].rearrange("p (h e) -> p h e", h=8)[:, :, 64:128]
            for r in range(4):
                pp = ps_main.get()
                MM(pp[:].rearrange("p (h e) -> p h e", h=8), cn[:, 2, r * 128:(r + 1) * 128], wv_v, True, True,
                   wukv.k() + cn.k(), pp.k())
                CP("act", vxo[:, r, :].rearrange("p (h e) -> p h e", h=8)[:, :, 0:64],
                   pp[:].rearrange("p (h e) -> p h e", h=8), pp.k(), vxo.ke(r * 520, 520))
            for j in range(4):
                vm = sndt[("VM", j)].rearrange("(h t e) -> t h e", h=2, e=65)
                for r in range(4):
                    DMA("sp", vm[t0 + r * 128:t0 + (r + 1) * 128, :, :],
                        vxo[:, r, :].rearrange("p (h e) -> p h e", h=8)[:, 2 * j:2 * j + 2, :],
                        vxo.ke(r * 520, 520), sk(l, "VM", j))
            for part in range(2):
                wt, wv = load_wblk(win_v[l, :, :, COL_SB + part * 512:COL_SB + (part + 1) * 512], 0)
                for j in range(4):
                    pp = ps_main.get()
                    proj_fm(wt, wv, j * 128, 128, hT, pp)
                    sp_ = sbp_r.get()
                    if part == 0:
                        ACT(sp_[:], pp[:], AF.Identity, pp.k(), sp_.k(), scale=0.125)
                        for hh in range(2):
                            DMA("sp", qs_d[m, 2 * j + hh, :, :], sp_[hh * 64:(hh + 1) * 64, :], sp_.k(), [("qs", m, 2 * j + hh)])
                    else:
                        CP("act", sp_[:], pp[:], pp.k(), sp_.k())
                        ks = sndt[("KS", j)].rearrange("(h r t) -> h r t", h=2, r=64)
                        for hh in range(2):
                            DMA("sp", ks[hh, :, t0:t0 + 512], sp_[hh * 64:(hh + 1) * 64, :], sp_.k(), sk(l, "KS", j))
            wt, wv = load_wblk(win_v[l, :, :, COL_SB + 1024:COL_SB + 1536], 0)
            for r in range(4):
                pp = ps_main.get()
                for k in range(8):
                    MM(pp[:], hT[:, k, r * 128:(r + 1) * 128], wv[:, k, :], k == 0, k == 7, wt.k() + hT.k(), pp.k())
                CP("act", vso[:, r, :], pp[:], pp.k(), vso.ke(r * 512, 512))
            for j in range(4):
                vs = sndt[("VS", j)].rearrange("(h t e) -> t h e", h=2, e=64)
                for r in range(4):
                    DMA("sp", vs[t0 + r * 128:t0 + (r + 1) * 128, :, :],
                        vso[:, r, :].rearrange("p (h e) -> p h e", h=8)[:, 2 * j:2 * j + 2, :],
                        vso.ke(r * 512, 512), sk(l, "VS", j))

    if doB:
        B = Cursor(nc, ARENA, LIMIT, "b")
        ymla = B.alloc("ymla", [64, 8, 512], BF16)
        ysb = B.alloc("ysb", [64, 8, 512], BF16)
        ysg_l = B.alloc("ysg_l", [128, 4, 512], BF16)
        ycv_l = B.alloc("ycv_l", [128, 4, 512], BF16)
        fx_l = B.alloc("fx_l", [128, 4, 4], F32)
        hsel = B.alloc("hsel", [128, 4, 4, 2], BF16)
        hf = B.alloc("hf", [128, 4, 8], F32)
        X = B.fork("x")
        kc_r = Ring([X.alloc("kc%d" % i, [96, 2048], BF16) for i in range(2)])
        vc_r = Ring([X.alloc("vc%d" % i, [128, 16, 65], BF16) for i in range(2)])
        qT_r = Ring([X.alloc("qT%d" % i, [96, 512], BF16) for i in range(2)])
        pa_r = Ring([X.alloc("pa%d" % i, [128, 512], BF16) for i in range(3)])
        e_r = fa
        lp_r = Ring([X.alloc("lp%d" % i, [128, 512], BF16) for i in range(3)])
        lsum_r = Ring([X.alloc("lsum%d" % i, [128, 512], F32) for i in range(2)])
        lsbf_r = Ring([X.alloc("lsbf%d" % i, [128, 512], BF16) for i in range(2)])
        masks_t = X.alloc("masks", [128, 16, 512], BF16)

        Y = B.fork("y")
        macc = Y.alloc("macc", [128, 4, 512], F32)
        sig_r = Ring([Y.alloc("sig%d" % i, [128, 512], F32) for i in range(2)])
        prod_r = Ring([Y.alloc("prod%d" % i, [128, 512], F32) for i in range(2)])
        mergedT = Y.alloc("mergedT", [128, 8, 512], BF16)
        Z = B.fork("z")
        aT = Z.alloc("aT", [128, 32, 512], BF16)
        rl_r = Ring([Z.alloc("rl%d" % i, [128, 512], BF16) for i in range(2)])

    def attention(l, m, kind):
        gatt = gat_t[l]
        nkb = 16 * m + 16
        nch = m + 1
        mla = kind == "mla"
        if mla:
            KK, VK, KR, VE = "KM", "VM", 96, 65
            mask_d = mns_d
        else:
            KK, VK, KR, VE = "KS", "VS", 64, 64
            mask_d = mst_d
        kviews = [gatt[(KK, j)].rearrange("(c h r t) -> h r c t", c=4, h=2, r=KR) for j in range(4)]
        vviews = [gatt[(VK, j)].rearrange("(c h t e) -> h t c e", c=4, h=2, e=VE) for j in range(4)]
        DMA("pool", masks_t[:], mask_d.rearrange("j s t -> s j t"), (), masks_t.k())
        chorder = list(range(nch)) if mla else list(range(nch - 1, -1, -1))
        kborder = list(range(16)) if mla else list(range(15, -1, -1))
        chunks = [(h, ch) for h in range(8) for ch in chorder]
        loaded = {}

        def load_chunk(ci):
            if ci >= len(chunks) or ci in loaded:
                return
            h, ch = chunks[ci]
            kc = kc_r.get()
            vc = vc_r.get()
            DMA("sp", kc.h[0:KR, :].rearrange("r (c t) -> r c t", c=4),
                kviews[h // 2][h % 2, :, :, ch * 512:(ch + 1) * 512], [("gat", l, KK, h // 2)], kc.k())
            for c_ in range(4):
                DMA("sp", vc.h[:, c_ * 4:(c_ + 1) * 4, 0:VE],
                    vviews[h // 2][h % 2, ch * 512:(ch + 1) * 512, c_, :].rearrange("(j p) e -> p j e", p=128),
                    [("gat", l, VK, h // 2)], vc.k())
            loaded[ci] = (kc, vc)

        qts = {}

        def load_q(h):
            if h >= 8 or h in qts:
                return
            qT = qT_r.get()
            if mla:
                DMA("sp", qT[0:96, :], qm_d[m, h, :, :], [("qm", m, h)], qT.k())
            else:
                DMA("sp", qT[0:64, :], qs_d[m, h, :, :], [("qs", m, h)], qT.k())
            qts[h] = qT

        load_q(0)
        load_chunk(0)
        for h in range(8):
            qT = qts[h]
            O = ps_acc.get()
            lsum = lsum_r.get() if not mla else None
            items = []
            for ci_l, ch in enumerate(chorder):
                for kb in kborder:
                    items.append((h * nch + ci_l, ch, kb))
            n = len(items)
            st = [dict() for _ in range(n)]

            def s1(i):
                ci, ch, kb = items[i]
                if kb == kborder[0]:
                    load_chunk(ci)
                if kb == kborder[3]:
                    load_chunk(ci + 1)
                    load_q(h + 1)
                kc, vc = loaded[ci]
                kbg = ch * 16 + kb
                d = st[i]
                d["vc"], d["kb"] = vc, kb
                d["masked"] = kbg >= nkb - 16
                d["mj"] = kbg - (nkb - 16)
                Sp = ps_main.get()
                d["Sp"] = Sp
                if mla:
                    MM(Sp[:], kc[0:96, kb * 128:(kb + 1) * 128], qT[0:96, :], True, True, kc.k() + qT.k(), Sp.k())
                    Pt = pa_r.get()
                    ACT(Pt[:], Sp[:], AF.Exp, Sp.k(), Pt.k(), scale=float(96 ** -0.5))
                    if d["masked"]:
                        TT("dve", Pt[:], Pt[:], masks_t[:, d["mj"], :], ALU.mult, Pt.k() + masks_t.k(), Pt.k())
                    d["A"] = Pt
                else:
                    MM(Sp[:], kc[0:64, kb * 128:(kb + 1) * 128], qT[0:64, :], True, False, kc.k() + qT.k(), Sp.k())
                    E = e_r.get()
                    ACT(E[:], Sp[:], AF.Exp, Sp.k(), E.k())
                    Lp = lp_r.get()
                    ACT(Lp[:], E[:], AF.Ln, E.k(), Lp.k(), bias=1.0)
                    if d["masked"]:
                        TT("dve", Lp[:], Lp[:], masks_t[:, d["mj"], :], ALU.mult, Lp.k() + masks_t.k(), Lp.k())
                    d["Lp"] = Lp

            def s2(i):
                d = st[i]
                Sp, Lp = d["Sp"], d["Lp"]
                first = i == 0
                MM(Sp[:], negU[:], Lp[:], False, first, negU.k() + Lp.k(), Sp.k(), skip_group_check=True)
                if not first:
                    lsbf = lsbf_r.get()
                    CP("pool", lsbf[:], lsum[:], lsum.k(), lsbf.k())
                    MM(Sp[:], negones[:], lsbf[:], False, True, negones.k() + lsbf.k(), Sp.k(), skip_group_check=True)
                    if i < n - 1:
                        TT("dve", lsum[:], lsum[:], Lp[:], ALU.add, lsum.k() + Lp.k(), lsum.k())
                else:
                    CP("dve", lsum[:], Lp[:], Lp.k(), lsum.k())
                At = pa_r.get()
                ACT(At[:], Sp[:], AF.Exp, Sp.k(), At.k())
                if d["masked"]:
                    TT("pool", At[:], At[:], masks_t[:, d["mj"], :], ALU.mult, At.k() + masks_t.k(), At.k())
                d["A"] = At

            def s3(i):
                d = st[i]
                vc, kb, At = d["vc"], d["kb"], d["A"]
                if mla:
                    MM(O[0:65, :], vc[:, kb, 0:65], At[:], i == 0, i == n - 1, vc.k() + At.k(), O.k())
                else:
                    MM(O[0:64, :], vc[:, kb, 0:64], At[:], i == 0, i == n - 1, vc.k() + At.k(), O.k())

            if mla:
                for t in range(n + 1):
                    if t < n:
                        s1(t)
                    if t >= 1:
                        s3(t - 1)
            else:
                for t in range(n + 2):
                    if t < n:
                        s1(t)
                    if 1 <= t <= n:
                        s2(t - 1)
                    if t >= 2:
                        s3(t - 2)
            if mla:
                rs_t = fa.get()
                bc_t = fa.get()
                CP("act", rs_t[64:65, :], O[64:65, :], O.k(), rs_t.k())
                RECIP(rs_t[64:65, :], rs_t[64:65, :], rs_t.k(), rs_t.k())
                pb = ps_misc.get()
                MM(pb[0:64, :], ones_f[64:65, 0:64], rs_t[64:65, :], True, True, ones_f.k() + rs_t.k(), pb.k())
                CP("act", bc_t[0:64, :], pb[0:64, :], pb.k(), bc_t.k())
                TT("dve", ymla[:, h, :], O[0:64, :], bc_t[0:64, :], ALU.mult, O.k() + bc_t.k(), ymla.ke(h * 512, 512))
            else:
                CP("act", ysb[:, h, :], O[0:64, :], O.k(), ysb.ke(h * 512, 512))

    def phaseB(l, last):
        wbr_v = wbr_d
        for m in range(NSB):
            t0 = m * 512
            DMA("sp", ysg_l[:], ysg_d[:, :, t0:t0 + 512], [("ysg", m)], ysg_l.k())
            DMA("sp", ycv_l[:], ycv_d[:, :, t0:t0 + 512], [("ycv", m)], ycv_l.k())
            DMA("sp", fx_l[:], fx_d[:, m, :, :], [("fx", m)], fx_l.k())
            hv = gat_t[l][("HL", 0)].rearrange("(c m j p e) -> c m p j e", c=4, m=NSB, j=4, p=128)
            MEMSET("dve", hsel[:], 0.0, hsel.k())
            for q in range(4):
                if q == 0:
                    if m == 0:
                        continue
                    src = hv[3, m - 1]
                else:
                    src = hv[q - 1, m]
                DMA("sp", hsel[:, q, :, :], src, [("gat", l, "HL", 0)], hsel.k())
            MEMSET("dve", hf[:], 0.0, hf.k())
            for q in range(4):
                STT("dve", hf[:, :, 0:2], hsel[:, q, :, :], pv[:, PV_OH + q:PV_OH + q + 1], hf[:, :, 0:2], ALU.mult, ALU.add,
                    hsel.k() + pv.k() + hf.k(), hf.k())
            for j in range(4):
                w0, w1 = pvl(l, 20 + j), pvl(l, 24 + j)
                TS("dve", hf[:, j, 2:3], hf[:, j, 1:2], w1, None, ALU.mult, None, hf.k() + pv.k(), hf.k())
                STT("dve", hf[:, j, 2:3], hf[:, j, 0:1], w0, hf[:, j, 2:3], ALU.mult, ALU.add, hf.k() + pv.k(), hf.k())
                TS("dve", hf[:, j, 3:4], hf[:, j, 1:2], w0, None, ALU.mult, None, hf.k() + pv.k(), hf.k())
                TT("dve", hf[:, j, 2:4], hf[:, j, 2:4], fx_l[:, j, 0:2], ALU.add, hf.k() + fx_l.k(), hf.k())
                TT("dve", ycv_l[:, j, 0:2], hf[:, j, 2:4], fx_l[:, j, 2:4], ALU.mult, hf.k() + fx_l.k(), ycv_l.ke(j * 512, 512))
            attention(l, m, "mla")
            attention(l, m, "sb")
            hT = norm_hT(l, m, 0)
            for grp in range(2):
                for nbr in range(4):
                    gt, gvw = load_wblk(win_v[l, :, :, COL_GATE + nbr * 1024 + grp * 512:COL_GATE + nbr * 1024 + (grp + 1) * 512], 0)
                    if nbr < 2:
                        bt, bvw = load_wblk(wbr_v[l, nbr].rearrange("(k p) n -> p k n", p=128)[:, :, grp * 512:(grp + 1) * 512], 0)
                    else:
                        bt, bvw = load_wblk(wbr_v[l, nbr].rearrange("(h p) n -> p h n", p=64)[:, :, grp * 512:(grp + 1) * 512], 0)
                    for dcl in range(4):
                        pg = ps_main.get()
                        proj_fm(gt, gvw, dcl * 128, 128, hT, pg)
                        sg_ = sig_r.get()
                        ACT(sg_[:], pg[:], AF.Sigmoid, pg.k(), sg_.k())
                        pu = ps_main.get()
                        if nbr < 2:
                            src = ysg_l if nbr == 0 else ycv_l
                            for k in range(4):
                                MM(pu[:], bvw[:, k, dcl * 128:(dcl + 1) * 128], src[:, k, :], k == 0, k == 3, bt.k() + src.k(), pu.k())
                        else:
                            src = ymla if nbr == 2 else ysb
                            for hh in range(8):
                                MM(pu[:], bvw[0:64, hh, dcl * 128:(dcl + 1) * 128], src[:, hh, :], hh == 0, hh == 7, bt.k() + src.k(), pu.k())
                        if nbr == 0:
                            TT("dve", macc[:, dcl, :], sg_[:], pu[:], ALU.mult, sg_.k() + pu.k(), macc.ke(dcl * 512, 512))
                        else:
                            pr = prod_r.get()
                            TT("dve", pr[:], sg_[:], pu[:], ALU.mult, sg_.k() + pu.k(), pr.k())
                            if nbr < 3:
                                TT("pool", macc[:, dcl, :], macc[:, dcl, :], pr[:], ALU.add, macc.ke(dcl * 512, 512) + pr.k(), macc.ke(dcl * 512, 512))
                            else:
                                dc = grp * 4 + dcl
                                TT("pool", mergedT[:, dc, :], macc[:, dcl, :], pr[:], ALU.add, macc.ke(dcl * 512, 512) + pr.k(), mergedT.ke(dc * 512, 512))
            for half in range(2):
                wt, wv = load_wblk(wout_d[l].rearrange("(k p) n -> p k n", p=128)[:, :, half * 512:(half + 1) * 512], 0)
                for dcl in range(4):
                    dc = half * 4 + dcl
                    pp = ps_main.get()
                    for k in range(8):
                        MM(pp[:], wv[:, k, dcl * 128:(dcl + 1) * 128], mergedT[:, k, :], k == 0, k == 7, wt.k() + mergedT.k(), pp.k())
                    STT("dve", xT[:, dc, t0:t0 + 512], pp[:], layc(l, 2, dc), xT[:, dc, t0:t0 + 512], ALU.mult, ALU.add,
                        pp.k() + lay.k() + xk(dc, m), xk(dc, m))
            h2 = norm_hT(l, m, 1)
            for nb in range(8):
                wt, wv = load_wblk(w1_d[l].rearrange("(k p) n -> p k n", p=128)[:, :, nb * 512:(nb + 1) * 512], 0)
                for j in range(4):
                    pp = ps_main.get()
                    proj_fm(wt, wv, j * 128, 128, h2, pp)
                    rl = rl_r.get()
                    ACT(rl[:], pp[:], AF.Relu, pp.k(), rl.k())
                    fi = nb * 4 + j
                    TT("pool", aT[:, fi, :], rl[:], rl[:], ALU.mult, rl.k(), aT.ke(fi * 512, 512))
            for dc in range(8):
                wt, wtag = wget()
                wv = wt.h[:, :].rearrange("p (k n) -> p k n", k=32)
                DMA("pool", wv, w2_d[l].rearrange("(k p) n -> p k n", p=128)[:, :, dc * 128:(dc + 1) * 128], (), wt.k(), tag=wtag)
                pp = ps_main.get()
                for k in range(32):
                    MM(pp[:], wv[:, k, :], aT[:, k, :], k == 0, k == 31, wt.k() + aT.ke(k * 512, 512), pp.k())
                STT("dve", xT[:, dc, t0:t0 + 512], pp[:], layc(l, 5, dc), xT[:, dc, t0:t0 + 512], ALU.mult, ALU.add,
                    pp.k() + lay.k() + xk(dc, m), xk(dc, m))
            if mode == "B":
                for k in range(8):
                    DMA("sp", xTo_d[:, k, t0:t0 + 512], xT[:, k, t0:t0 + 512], xk(k, m), [("xTo", k, m)])
            if last:
                ss = ps_misc.get()
                for k in range(8):
                    xs = xsq_r.get()
                    ACT(xs[:], xT[:, k, t0:t0 + 512], AF.Square, xk(k, m), xs.k())
                    MM(ss[:], ones_bf[:], xs[:], k == 0, k == 7, ones_bf.k() + xs.k(), ss.k())
                ACT(rstd[:], ss[:], AF.Sqrt, ss.k(), rstd.k(), scale=1.0 / D, bias=EPS)
                RECIP(rstd[:], rstd[:], rstd.k(), rstd.k())
                for k in range(8):
                    tmp = fa.get()
                    STT("dve", tmp[:], xT[:, k, t0:t0 + 512], pv[:, PV_FNG + k:PV_FNG + k + 1], rstd[:], ALU.mult, ALU.mult,
                        xk(k, m) + pv.k() + rstd.k(), tmp.k())
                    DMA("sp", outT_d[:, k, t0:t0 + 512], tmp[:], tmp.k(), [("outT", k, m)])

    for l in range(LN):
        if doA:
            phaseA(l)
        if mode == "F":
            for (kd, j) in CHN:
                S.coll((lambda l, kd, j: lambda e: e.collective_compute(
                    "AllGather", ALU.bypass, replica_groups=[[0, 1, 2, 3], [4, 5, 6, 7]],
                    ins=[snd_t[l][(kd, j)].rearrange("(a b) -> a b", b=1024).opt()],
                    outs=[gat_t[l][(kd, j)].rearrange("(a b) -> a b", b=1024).opt()]))(l, kd, j),
                    snd_keys[l][(kd, j)], [("gat", l, kd, j)])
        if doB:
            phaseB(l, last=(mode == "B" or l == LN - 1))
    S.emit()
    return nc, S.stats


def _consts():
    tri = np.triu(np.ones((128, 128), np.float32))
    negU = -np.tril(np.ones((128, 128), np.float32))
    return tri, negU


def _masks(c):
    tri_ns = np.triu(np.ones((128, 128), np.float32))
    tri_s = np.triu(np.ones((128, 128), np.float32), 1)
    mns = np.zeros((16, 128, 512), np.float32)
    mst = np.zeros((16, 128, 512), np.float32)
    for j in range(16):
        for r in range(4):
            qb = 4 * c + r
            if j < qb:
                mns[j, :, r * 128:(r + 1) * 128] = 1.0
                mst[j, :, r * 128:(r + 1) * 128] = 1.0
            elif j == qb:
                mns[j, :, r * 128:(r + 1) * 128] = tri_ns
                mst[j, :, r * 128:(r + 1) * 128] = tri_s
    return mns, mst


def _chunkcols(v):
    return np.ascontiguousarray(v.reshape(-1, 128).T)


def make_pv(inp, b, c, layers):
    cols = [_chunkcols(inp["c"][b])]
    for l in layers:
        cols.append(_chunkcols(inp["norm1_g"][l]))
        cols.append(_chunkcols(inp["norm2_g"][l]))
        cols.append(_chunkcols(inp["sg_norm_g"][l]))
        for j in range(3):
            cols.append(_chunkcols(inp["conv_w"][l, j]))
        cols.append(_chunkcols(inp["mla_q_norm_g"][l]))
        cols.append(_chunkcols(inp["mla_kv_norm_g"][l]))
    cols.append(_chunkcols(inp["final_norm_g"]))
    sign = np.where((np.arange(128) % 32) < 16, -1.0, 1.0).astype(np.float32)[:, None]
    cols.append(sign)
    oh = np.zeros((128, 4), np.float32)
    oh[:, c] = 1.0
    cols.append(oh)
    return np.ascontiguousarray(np.concatenate(cols, axis=1).astype(np.float32))


def layer_weights(inp, layers):
    ls = list(layers)
    w = {}
    w["ada_w"] = np.ascontiguousarray(inp["ada_w"][ls])
    w["adab"] = np.ascontiguousarray(inp["ada_b"][ls].reshape(1, -1))
    w["w_in"] = np.ascontiguousarray(inp["w_in"][ls])
    w["sgwT"] = np.ascontiguousarray(np.transpose(inp["sg_w"][ls], (0, 3, 1, 2)))
    w["sgb"] = np.ascontiguousarray(np.tile(inp["sg_b"][ls][:, :, None, :], (1, 1, 4, 1)).reshape(len(ls), 1, 4 * 512))
    uq = inp["mla_w_uq"][ls].reshape(len(ls), 256, 8, 96)
    pe = uq[..., 64:96]
    pesw = np.concatenate([pe[..., 16:32], pe[..., 0:16]], axis=-1)
    w["wq"] = np.ascontiguousarray(np.concatenate([uq, pesw], axis=-1).reshape(len(ls), 256, 8 * 128))
    w["wukv"] = np.ascontiguousarray(inp["mla_w_ukv"][ls])
    kpe = inp["w_in"][ls][:, :, COL_MLA + 384:COL_MLA + 416]
    kpesw = np.concatenate([kpe[..., 16:32], kpe[..., 0:16]], axis=-1)
    z = np.zeros(kpe.shape[:2] + (64,), np.float32)
    w["wkpe"] = np.ascontiguousarray(np.concatenate([z, kpe, z, kpesw], axis=-1))
    w["w_branch"] = np.ascontiguousarray(inp["w_branch"][ls])
    w["w_out"] = np.ascontiguousarray(inp["w_out"][ls])
    w["w1"] = np.ascontiguousarray(inp["mlp_w1"][ls])
    w["w2"] = np.ascontiguousarray(inp["mlp_w2"][ls])
    return w


A_KEYS = ["ada_w", "adab", "w_in", "sgwT", "sgb", "wq", "wukv", "wkpe"]
B_KEYS = ["ada_w", "adab", "w_in", "w_branch", "w_out", "w1", "w2"]

_cache = {}


def _get(mode, NSB, LN):
    key = (mode, NSB, LN)
    if key not in _cache:
        _cache[key] = build(mode, NSB, LN)
    return _cache[key][0]


def run_model(inp, NSB, depth, fused=False):
    x = np.asarray(inp["x"], np.float32)
    Bsz, SEQ, _ = x.shape
    NTOK = NSB * 512
    assert SEQ == 4 * NTOK and Bsz == 2
    inv_freq = (10000.0 ** (-np.arange(0, 32, 2, dtype=np.float32) / 32)).astype(np.float32)
    invf = np.tile(inv_freq, 8)[None, :].astype(np.float32)
    tri, negU = _consts()
    cores = [(b, c) for b in range(2) for c in range(4)]

    def tok_idx(c):
        return np.concatenate([np.arange((4 * m + c) * 512, (4 * m + c + 1) * 512) for m in range(NSB)])

    xTs = []
    for (b, c) in cores:
        xt = x[b, tok_idx(c), :].T
        xTs.append(np.ascontiguousarray(xt.reshape(8, 128, NTOK).transpose(1, 0, 2)))
    posis = [np.ascontiguousarray(np.asarray(inp["positions"])[b, tok_idx(c)][None, :].astype(np.int32)) for (b, c) in cores]
    masks = [_masks(c) for (b, c) in cores]
    outT = None
    if fused:
        w = layer_weights(inp, range(depth))
        nc = _get("F", NSB, depth)
        in_maps = []
        for i, (b, c) in enumerate(cores):
            d = dict(xT=xTs[i], pv=make_pv(inp, b, c, range(depth)), posi=posis[i], invf=invf, tri=tri, negU=negU,
                     mask_ns=masks[i][0], mask_s=masks[i][1])
            for k_ in set(A_KEYS + B_KEYS):
                d[k_] = w[k_]
            in_maps.append(d)
        res = run_bass_kernel_spmd(nc, in_maps, core_ids=list(range(8)))
        outT = [r["outT"] for r in res.results]
    else:
        ncA = _get("A", NSB, 1)
        ncB = _get("B", NSB, 1)
        for l in range(depth):
            w = layer_weights(inp, [l])
            in_maps = []
            for i, (b, c) in enumerate(cores):
                d = dict(xT=xTs[i], pv=make_pv(inp, b, c, [l]), posi=posis[i], invf=invf, tri=tri)
                for k_ in A_KEYS:
                    d[k_] = w[k_]
                in_maps.append(d)
            resA = run_bass_kernel_spmd(ncA, in_maps, core_ids=list(range(8))).results
            in_maps = []
            for i, (b, c) in enumerate(cores):
                gats = {}
                for kd in ("HL", "KM", "VM", "KS", "VS"):
                    for j in range(1 if kd == "HL" else 4):
                        nm = "0_%s%d" % (kd, j)
                        gats["gat" + nm] = np.concatenate([resA[b * 4 + cc]["snd" + nm] for cc in range(4)])
                d = dict(xT=xTs[i], pv=make_pv(inp, b, c, [l]), negU=negU, mask_ns=masks[i][0], mask_s=masks[i][1],
                         qm=resA[i]["qm"], qs=resA[i]["qs"], ysg=resA[i]["ysg"], ycv=resA[i]["ycv"], fx=resA[i]["fx"])
                for k_ in B_KEYS:
                    d[k_] = w[k_]
                d.update(gats)
                in_maps.append(d)
            resB = run_bass_kernel_spmd(ncB, in_maps, core_ids=list(range(8))).results
            xTs = [r["xTo"] for r in resB]
            outT = [r["outT"] for r in resB]
    out = np.zeros((2, SEQ, D), np.float32)
    for i, (b, c) in enumerate(cores):
        o = outT[i].transpose(1, 0, 2).reshape(1024, NTOK).T
        out[b, tok_idx(c), :] = o
    return out


def kernel(**inputs):
    return run_model(inputs, NSB=4, depth=4, fused=True)
```

```python
import numpy as np
import concourse.bass as bass
import concourse.mybir as mybir
from concourse.bass_utils import run_bass_kernel_spmd

F32 = mybir.dt.float32
BF16 = mybir.dt.bfloat16
I32 = mybir.dt.int32
AF = mybir.ActivationFunctionType
ALU = mybir.AluOpType

SEM_LIMIT = 30000
N_DMA_SEMS = 12
SLOT = 512

D = 1024
COL_SG, COL_CONV, COL_MLA, COL_SB, COL_GATE, IN_COLS = 0, 1024, 2560, 2976, 4512, 8608
EPS = 1e-6
PVL = 35


class Sched:
    def __init__(self, nc):
        self.nc = nc
        self.ops = []
        self.tags = {}

    def hoist(self):
        groups = {}
        for i, t in self.tags.items():
            groups.setdefault(t, []).append(i)
        order = sorted(groups)
        nxt = {order[i]: order[i + 1] for i in range(len(order) - 1)}
        firstpos = {t: min(v) for t, v in groups.items()}
        new = []
        done = set()
        for i, o in enumerate(self.ops):
            t = self.tags.get(i)
            if t is None:
                new.append(o)
                continue
            if t == order[0] and t not in done:
                for j in sorted(groups[t]):
                    new.append(self.ops[j])
                done.add(t)
            if i == firstpos[t]:
                n_ = nxt.get(t)
                if n_ is not None and n_ not in done:
                    for j in sorted(groups[n_]):
                        new.append(self.ops[j])
                    done.add(n_)
        assert len(new) == len(self.ops)
        self.ops = new
        self.tags = {}

    def op(self, eng, fn, reads=(), writes=()):
        self.ops.append((eng, fn, tuple(reads), tuple(writes), False))

    def dma(self, q, fn, reads=(), writes=(), tag=None):
        self.ops.append((q, fn, tuple(reads), tuple(writes), True))
        if tag is not None:
            self.tags[len(self.ops) - 1] = tag

    def coll(self, fn, reads=(), writes=()):
        self.ops.append(("pool", fn, tuple(reads), tuple(writes), "coll"))

    def emit(self):
        self.hoist()
        nc = self.nc
        engs = {"pe": nc.tensor, "act": nc.scalar, "dve": nc.vector, "pool": nc.gpsimd, "sp": nc.sync}
        ops = self.ops
        n = len(ops)
        last_w = {}
        rd_c = {}
        rd_d = {}
        deps = [None] * n
        need_sig = [False] * n
        for i, (e, fn, rs, ws, isd) in enumerate(ops):
            d = set()
            for k in rs:
                j = last_w.get(k)
                if j is not None:
                    d.add(j)
            for k in ws:
                j = last_w.get(k)
                if j is not None:
                    d.add(j)
                rc = rd_c.get(k)
                if rc:
                    d.update(rc.values())
                rdd = rd_d.get(k)
                if rdd:
                    d.update(rdd)
            for k in rs:
                if isd:
                    rd_d.setdefault(k, []).append(i)
                else:
                    rd_c.setdefault(k, {})[e] = i
            for k in ws:
                last_w[k] = i
                rd_c[k] = {}
                rd_d[k] = []
            nd = set()
            for j in d:
                if j == i:
                    continue
                ej, _, _, _, jd = ops[j]
                if (not isd) and (not jd) and e == "pe" and ej == "pe":
                    continue
                nd.add(j)
                if not jd:
                    need_sig[j] = True
            deps[i] = nd
        self.sem_ctx = []

        def new_sem(name):
            cm = nc.semaphore(name)
            s = cm.__enter__()
            self.sem_ctx.append(cm)
            return s

        cur = {}
        sig = [None] * n
        cnt = [0]
        dsem, dstate, dcount = {}, {}, {}
        prev_on_sem = [None] * n
        for i, (e, fn, rs, ws, isd) in enumerate(ops):
            if isd == "coll":
                sig[i] = (new_sem("coll%d" % cnt[0]), 1, 1)
                cnt[0] += 1
            elif isd:
                nds = 3 if e == "pool" else N_DMA_SEMS
                if e not in dsem:
                    dsem[e] = [new_sem("d%s%d" % (e, t)) for t in range(nds)]
                    dstate[e] = [[0, None] for _ in range(nds)]
                    dcount[e] = 0
                t = dcount[e] % nds
                dcount[e] += 1
                st = dstate[e][t]
                if st[0] + 16 > SEM_LIMIT:
                    dsem[e][t] = new_sem("d%s%dx%d" % (e, t, cnt[0]))
                    cnt[0] += 1
                    st[0] = 0
                if st[1] is not None:
                    prev_on_sem[i] = st[1]
                st[0] += 16
                sig[i] = (dsem[e][t], st[0], 16)
                st[1] = i
            elif need_sig[i]:
                if e not in cur or cur[e][1] + 1 > SEM_LIMIT:
                    cur[e] = [new_sem("c%s%d" % (e, cnt[0])), 0]
                    cnt[0] += 1
                cur[e][1] += 1
                sig[i] = (cur[e][0], cur[e][1], 1)
        waited = {}
        nwaits = 0
        for i, (e, fn, rs, ws, isd) in enumerate(ops):
            eng = engs[e]
            dl = list(deps[i])
            if prev_on_sem[i] is not None:
                dl.append(prev_on_sem[i])
            mx = {}
            for j in dl:
                s, v, _ = sig[j]
                key = id(s)
                if key not in mx or mx[key][1] < v:
                    mx[key] = (s, v)
            for key, (s, v) in mx.items():
                wk = (e, key)
                if waited.get(wk, 0) >= v:
                    continue
                waited[wk] = v
                eng.wait_ge(s, v)
                nwaits += 1
            inst = fn(eng)
            if sig[i] is not None:
                inst.then_inc(sig[i][0], sig[i][2])
        feng = engs["sp"]
        for e in dsem:
            for t in range(len(dstate[e])):
                st = dstate[e][t]
                if st[1] is not None:
                    s, v, _ = sig[st[1]]
                    feng.wait_ge(s, v)
        self.stats = dict(n_ops=n, n_waits=nwaits, n_sems=len(self.sem_ctx))


_DTS = {F32: 4, BF16: 2, I32: 4}


class Tile:
    def __init__(self, h, off, nbytes, esz):
        self.h, self.off, self.nbytes, self.esz = h, off, nbytes, esz
        self._all = tuple(range(off // SLOT, (off + nbytes + SLOT - 1) // SLOT))

    def __getitem__(self, idx):
        return self.h[idx]

    def k(self):
        return self._all

    def ke(self, e0, ne):
        lo = self.off + e0 * self.esz
        hi = lo + ne * self.esz
        return tuple(range(lo // SLOT, (hi + SLOT - 1) // SLOT))


class Cursor:
    def __init__(self, nc, base, limit, tag):
        self.nc, self.cur, self.limit, self.tag, self.n = nc, base, limit, tag, 0

    def alloc(self, name, shape, dt):
        esz = _DTS[dt]
        nb = esz
        for s in shape[1:]:
            nb *= s
        off = (self.cur + SLOT - 1) // SLOT * SLOT
        assert off + nb <= self.limit, (self.tag, name, off, nb, self.limit)
        self.cur = off + nb
        self.n += 1
        h = self.nc.alloc_sbuf_tensor_at("%s_%s_%d" % (self.tag, name, self.n), list(shape), dt, offset=off)
        return Tile(h, off, nb, esz)

    def fork(self, tag):
        return Cursor(self.nc, self.cur, self.limit, tag)


class Ring:
    def __init__(self, tiles):
        self.t, self.i = tiles, 0

    def get(self):
        t = self.t[self.i % len(self.t)]
        self.i += 1
        return t


class PS:
    def __init__(self, h, bank):
        self.h, self.bank = h, bank

    def __getitem__(self, idx):
        return self.h[idx]

    def k(self):
        return (("ps", self.bank),)


def snd_layout(NTOK, NSB):
    o = {}
    cur = 0
    o["KM"] = cur; cur += 8 * 96 * NTOK
    o["VM"] = cur; cur += 8 * NTOK * 65
    o["KS"] = cur; cur += 8 * 64 * NTOK
    o["VS"] = cur; cur += 8 * NTOK * 64
    o["HALO"] = cur; cur += NSB * 4 * 128 * 2
    o["N"] = cur
    return o


def build(mode, NSB, LN):
    nc = bass.Bass("TRN2", target_bir_lowering=False)
    S = Sched(nc)
    NTOK = NSB * 512
    NPV = 8 + PVL * LN + 8 + 1 + 4
    PV_C, PV_L, PV_FNG = 0, 8, 8 + PVL * LN
    PV_SIGN, PV_OH = PV_FNG + 8, PV_FNG + 9
    SL = snd_layout(NTOK, NSB)
    NSND = SL["N"]
    doA = mode in ("A", "F")
    doB = mode in ("B", "F")

    def din(name, shape, dt=F32):
        return nc.dram_tensor(name, list(shape), dt, kind="ExternalInput").ap()

    def dout(name, shape, dt=F32):
        return nc.dram_tensor(name, list(shape), dt, kind="ExternalOutput").ap()

    def dint(name, shape, dt=F32):
        return nc.dram_tensor(name, list(shape), dt, kind="Internal").ap()

    def dAB(name, shape, dt):
        if mode == "A":
            return dout(name, shape, dt)
        if mode == "B":
            return din(name, shape, dt)
        return dint(name, shape, dt)

    xT_d = din("xT", [128, 8, NTOK])
    pv_d = din("pv", [128, NPV])
    adab_d = din("adab", [1, LN * 6144])
    adaw_d = din("ada_w", [LN, 1024, 6144])
    if doA:
        posi_d = din("posi", [1, NTOK], I32)
        invf_d = din("invf", [1, 128])
        win_d = din("w_in", [LN, 1024, IN_COLS])
        sgwT_d = din("sgwT", [LN, 128, 4, 128])
        sgb_d = din("sgb", [LN, 1, 4 * 512])
        wq_d = din("wq", [LN, 256, 8 * 128])
        wukv_d = din("wukv", [LN, 128, 1024])
        wkpe_d = din("wkpe", [LN, 1024, 2 * 96])
        tri_d = din("tri", [128, 128])
    if doB:
        if not doA:
            win_d = din("w_in", [LN, 1024, IN_COLS])
        wbr_d = din("w_branch", [LN, 4, 512, 1024])
        wout_d = din("w_out", [LN, 1024, 1024])
        w1_d = din("w1", [LN, 1024, 4096])
        w2_d = din("w2", [LN, 4096, 1024])
        mns_d = din("mask_ns", [16, 128, 512])
        mst_d = din("mask_s", [16, 128, 512])
        negu_d = din("negU", [128, 128])
        cw_unused = None
    CHK = {"KM": 2 * 96 * NTOK, "VM": 2 * NTOK * 65, "KS": 2 * 64 * NTOK, "VS": 2 * NTOK * 64, "HL": NSB * 1024}
    CHN = [("HL", 0)] + [(kd, j) for j in range(4) for kd in ("KM", "VM")] + [(kd, j) for j in range(4) for kd in ("KS", "VS")]
    snd_t = [dict() for _ in range(LN)]
    gat_t = [dict() for _ in range(LN)]
    for l in range(LN):
        for (kd, j) in CHN:
            nm = "%d_%s%d" % (l, kd, j)
            if mode != "B":
                snd_t[l][(kd, j)] = dAB("snd" + nm, [CHK[kd]], BF16)
            if mode == "B":
                gat_t[l][(kd, j)] = din("gat" + nm, [4 * CHK[kd]], BF16)
            elif mode == "F":
                gat_t[l][(kd, j)] = dint("gat" + nm, [4 * CHK[kd]], BF16)
    qm_d = dAB("qm", [NSB, 8, 96, 512], BF16)
    qs_d = dAB("qs", [NSB, 8, 64, 512], BF16)
    ysg_d = dAB("ysg", [128, 4, NTOK], BF16)
    ycv_d = dAB("ycv", [128, 4, NTOK], BF16)
    fx_d = dAB("fx", [128, NSB, 4, 4], F32)
    if mode == "B":
        xTo_d = dout("xTo", [128, 8, NTOK])
    if doB:
        outT_d = dout("outT", [128, 8, NTOK])

    BASE = 16896
    LIMIT = nc.SBUF_PARTITION_SIZE_BYTES
    R = Cursor(nc, BASE, LIMIT, "r")
    xT = R.alloc("xT", [128, 8, NTOK], F32)
    pv = R.alloc("pv", [128, NPV], F32)
    lay = R.alloc("lay", [128, LN * 48], F32)
    ones_bf = R.alloc("ones_bf", [128, 128], BF16)
    ones_f = R.alloc("ones_f", [128, 128], F32)
    wblk = Ring([R.alloc("wblk%d" % i, [128, 8 * 512], BF16) for i in range(3)])
    hTr = Ring([R.alloc("hT%d" % i, [128, 8, 512], BF16) for i in range(2)])
    xsq_r = Ring([R.alloc("xsq%d" % i, [128, 512], BF16) for i in range(2)])
    rstd = R.alloc("rstd", [128, 512], F32)
    fa = Ring([R.alloc("fa%d" % i, [128, 512], F32) for i in range(3)])
    if doA:
        WsT = R.alloc("WsT", [128, 4, 128], BF16)
        tri = R.alloc("tri", [128, 128], BF16)
        bsb = R.alloc("bsb", [128, 4, 512], F32)
        wq = R.alloc("wq", [128, 2, 8 * 128], BF16)
        wukv = R.alloc("wukv", [128, 1024], BF16)
        wkpe = R.alloc("wkpe", [128, 8, 192], BF16)
    if doB:
        negU = R.alloc("negU", [128, 128], BF16)
        negones = R.alloc("negones", [128, 128], BF16)
    ARENA = R.cur

    psb = []
    for i in range(8):
        cm = nc.psum_tensor("psb%d" % i, [128, 512], F32)
        psb.append(PS(cm.__enter__(), i))
    ps_main = Ring(psb[0:4])
    ps_acc = Ring(psb[4:6])
    ps_misc = Ring(psb[6:8])

    def MM(out, lhsT, rhs, start, stop, r, w, **kw):
        S.op("pe", lambda e: e.matmul(out, lhsT, rhs, start=start, stop=stop, **kw), r, w)

    def ACT(out, in_, func, r, w, **kw):
        S.op("act", lambda e: e.activation(out=out, in_=in_, func=func, **kw), r, w)

    def TT(eng, out, a, b, op, r, w):
        S.op(eng, lambda e: e.tensor_tensor(out=out, in0=a, in1=b, op=op), r, w)

    def TS(eng, out, a, s1, s2, op0, op1, r, w):
        if op1 is None:
            S.op(eng, lambda e: e.tensor_scalar(out=out, in0=a, scalar1=s1, scalar2=None, op0=op0), r, w)
        else:
            S.op(eng, lambda e: e.tensor_scalar(out=out, in0=a, scalar1=s1, scalar2=s2, op0=op0, op1=op1), r, w)

    def STT(eng, out, in0, scalar, in1, op0, op1, r, w):
        S.op(eng, lambda e: e.scalar_tensor_tensor(out=out, in0=in0, scalar=scalar, in1=in1, op0=op0, op1=op1), r, w)

    def CP(eng, out, in_, r, w):
        if eng == "act":
            S.op("act", lambda e: e.activation(out=out, in_=in_, func=AF.Identity), r, w)
        else:
            S.op(eng, lambda e: e.tensor_copy(out=out, in_=in_), r, w)

    def RECIP(out, in_, r, w):
        S.op("dve", lambda e: e.reciprocal(out=out, in_=in_), r, w)

    def MEMSET(eng, out, val, w):
        S.op(eng, lambda e: e.memset(out, val), (), w)

    def DMA(q, out, in_, r, w, tag=None):
        S.dma(q, lambda e: e.dma_start(out=out, in_=in_), r, w, tag=tag)

    wcount = [0]

    def wget():
        wcount[0] += 1
        return wblk.get(), wcount[0]

    def xk(k, m):
        return xT.ke(k * NTOK + m * 512, 512)

    def layc(l, j, k):
        c = l * 48 + j * 8 + k
        return lay[:, c:c + 1]

    def pvl(l, j):
        c = PV_L + l * PVL + j
        return pv[:, c:c + 1]

    DMA("sp", pv[:], pv_d[:, :], (), pv.k())
    for k in range(8):
        DMA("sp", xT[:, k, :], xT_d[:, k, :], (), xT.ke(k * NTOK, NTOK))
    MEMSET("dve", ones_bf[:], 1.0, ones_bf.k())
    MEMSET("dve", ones_f[:], 1.0, ones_f.k())
    if doB:
        MEMSET("dve", negones[:], -1.0, negones.k())
        DMA("pool", negU[:], negu_d[:, :], (), negU.k())
    if doA:
        DMA("pool", tri[:], tri_d[:, :], (), tri.k())

    P = Cursor(nc, ARENA, LIMIT, "p")
    siluc = P.alloc("siluc", [128, 8], BF16)
    modrow = P.alloc("modrow", [1, 6144], F32)
    adab_t = P.alloc("adab", [1, 6144], F32)
    modT = P.alloc("modT", [128, LN * 48], F32)
    ACT(siluc[:], pv[:, PV_C:PV_C + 8], AF.Silu, pv.k(), siluc.k())
    adaw_v = adaw_d.rearrange("l (k p) n -> l p k n", p=128)
    for l in range(LN):
        DMA("sp", adab_t[:], adab_d[:, l * 6144:(l + 1) * 6144], (), adab_t.k())
        for nb in range(12):
            wt, wtag = wget()
            wv = wt.h[:, :].rearrange("p (k n) -> p k n", k=8)
            DMA("pool", wv, adaw_v[l, :, :, nb * 512:(nb + 1) * 512], (), wt.k(), tag=wtag)
            pp = ps_main.get()
            for k in range(8):
                MM(pp[0:1, :], siluc[:, k:k + 1], wv[:, k, :], k == 0, k == 7, siluc.k() + wt.k(), pp.k())
            c0 = nb * 512
            TT("dve", modrow[0:1, c0:c0 + 512], pp[0:1, :], adab_t[0:1, c0:c0 + 512], ALU.add,
               pp.k() + adab_t.ke(c0, 512), modrow.ke(c0, 512))
        pm = ps_misc.get()
        for j in range(48):
            c0 = j * 128
            MM(pm[:, j:j + 1], modrow[0:1, c0:c0 + 128], ones_f[0:1, 0:1], True, True,
               modrow.ke(c0, 128) + ones_f.k(), pm.k())
        CP("dve", modT[:, l * 48:(l + 1) * 48], pm[:, 0:48], pm.k(), modT.k())
        b = l * 48
        for which in range(2):
            sh = modT[:, b + which * 24:b + which * 24 + 8]
            sc = modT[:, b + which * 24 + 8:b + which * 24 + 16]
            g = modT[:, b + which * 24 + 16:b + which * 24 + 24]
            gain = pv[:, PV_L + l * PVL + which * 8:PV_L + l * PVL + which * 8 + 8]
            Acol = lay[:, b + which * 24:b + which * 24 + 8]
            Bcol = lay[:, b + which * 24 + 8:b + which * 24 + 16]
            Gcol = lay[:, b + which * 24 + 16:b + which * 24 + 24]
            STT("dve", Acol, sc, 1.0, gain, ALU.add, ALU.mult, modT.k() + pv.k(), lay.k())
            CP("dve", Bcol, sh, modT.k(), lay.k())
            CP("dve", Gcol, g, modT.k(), lay.k())

    def norm_hT(l, m, which):
        hT = hTr.get()
        ss = ps_misc.get()
        for k in range(8):
            xs = xsq_r.get()
            ACT(xs[:], xT[:, k, m * 512:(m + 1) * 512], AF.Square, xk(k, m), xs.k())
            MM(ss[:], ones_bf[:], xs[:], k == 0, k == 7, ones_bf.k() + xs.k(), ss.k())
        ACT(rstd[:], ss[:], AF.Sqrt, ss.k(), rstd.k(), scale=1.0 / D, bias=EPS)
        RECIP(rstd[:], rstd[:], rstd.k(), rstd.k())
        for k in range(8):
            tmp = fa.get()
            TT("dve", tmp[:], xT[:, k, m * 512:(m + 1) * 512], rstd[:], ALU.mult, xk(k, m) + rstd.k(), tmp.k())
            ACT(hT[:, k, :], tmp[:], AF.Identity, tmp.k() + lay.k(), hT.ke(k * 512, 512),
                scale=layc(l, which * 3, k), bias=layc(l, which * 3 + 1, k))
        return hT

    def load_wblk(src_ap, ncols_total):
        wt, wtag = wget()
        Pn, Kc, n = src_ap.shape[0], src_ap.shape[1], src_ap.shape[2]
        wv = wt.h[0:Pn, 0:Kc * n].rearrange("p (k n) -> p k n", k=Kc)
        DMA("pool", wv, src_ap, (), wt.k(), tag=wtag)
        return wt, wv

    win_v = win_d.rearrange("l (k p) n -> l p k n", p=128)

    def proj_fm(wt, wv, c0, M, hT, pp, prow=None):
        for k in range(8):
            MM(pp[0:M, :], wv[:, k, c0:c0 + M], hT[:, k, :], k == 0, k == 7, wt.k() + hT.k(), pp.k())

    if doA:
        A = Cursor(nc, ARENA, LIMIT, "a")
        Ct = A.alloc("C", [128, 512], F32)
        Sgt = A.alloc("Sg", [128, 512], F32)
        posi = A.alloc("posi", [1, 512], I32)
        posf = A.alloc("posf", [1, 512], F32)
        invf = A.alloc("invf", [1, 128], F32)
        angi = A.alloc("angi", [128, 512], I32)
        uT = A.alloc("uT", [128, 4, 512], BF16)
        vhat = A.alloc("vhat", [128, 4, 512], BF16)
        ysgo = A.alloc("ysgo", [128, 4, 512], BF16)
        ycvo = A.alloc("ycvo", [128, 4, 512], BF16)
        tt_r = Ring([A.alloc("tt%d" % i, [128, 514], F32) for i in range(2)])
        cq_sb = A.alloc("cq_sb", [128, 3, 512], F32)
        cn = A.alloc("cn", [128, 3, 512], BF16)
        qo_r = Ring([A.alloc("qo%d" % i, [96, 512], BF16) for i in range(2)])
        kTo = A.alloc("kTo", [96, 8, 512], BF16)
        vxo = A.alloc("vxo", [128, 4, 8 * 65], BF16)
        sbp_r = Ring([A.alloc("sbp%d" % i, [128, 512], BF16) for i in range(3)])
        vso = A.alloc("vso", [128, 4, 512], BF16)
        fxt = A.alloc("fxt", [128, 4, 4], F32)
        halo_o = A.alloc("halo_o", [128, 4, 2], BF16)
        mvst = A.alloc("mvst", [128, 8], F32)
        brow = A.alloc("brow", [1, 512], F32)
        fb = fa


    snd_keys = [dict() for _ in range(LN)]

    def sk(l, kd, j):
        lst = snd_keys[l].setdefault((kd, j), [])
        k_ = ("snd", l, kd, j, len(lst))
        lst.append(k_)
        return [k_]

    def phaseA(l):
        DMA("sp", invf[:], invf_d[:, :], (), invf.k())
        MEMSET("dve", vxo[:], 1.0, vxo.k())
        for i in range(2):
            t_ = tt_r.t[i]
            MEMSET("dve", t_[:, 0:2], 0.0, t_.k())
        sgw_t, sgw_v = load_wblk(sgwT_d[l, :, :, :], 0)
        for g in range(4):
            TT("dve", WsT[:, g, :], sgw_v[:, g, :], tri[:], ALU.mult, sgw_t.k() + tri.k(), WsT.k())
        for g in range(4):
            DMA("sp", brow[:], sgb_d[l, :, g * 512:(g + 1) * 512], (), brow.k())
            pp = ps_misc.get()
            MM(pp[:], ones_f[0:1, :], brow[0:1, :], True, True, ones_f.k() + brow.k(), pp.k())
            CP("act", bsb[:, g, :], pp[:], pp.k(), bsb.ke(g * 512, 512))
        DMA("pool", wq[:], wq_d[l].rearrange("(k p) n -> p k n", p=128), (), wq.k())
        DMA("pool", wukv[:], wukv_d[l, :, :], (), wukv.k())
        DMA("pool", wkpe[:], wkpe_d[l].rearrange("(k p) n -> p k n", p=128), (), wkpe.k())
        sndt = snd_t[l]
        for m in range(NSB):
            t0 = m * 512
            DMA("sp", posi[:], posi_d[:, t0:t0 + 512], (), posi.k())
            CP("dve", posf[:], posi[:], posi.k(), posf.k())
            pa = ps_misc.get()
            MM(pa[:], invf[0:1, :], posf[0:1, :], True, True, invf.k() + posf.k(), pa.k())
            for (dst, shift) in ((Sgt, 0.0), (Ct, 0.25)):
                y_ = fb.get()
                TS("dve", y_[:], pa[:], 1.0 / (2 * np.pi), shift, ALU.mult, ALU.add, pa.k(), y_.k())
                CP("dve", angi[:], y_[:], y_.k(), angi.k())
                y2 = fb.get()
                CP("dve", y2[:], angi[:], angi.k(), y2.k())
                TT("dve", y_[:], y_[:], y2[:], ALU.subtract, y_.k() + y2.k(), y_.k())
                ACT(dst[:], y_[:], AF.Sin, y_.k(), dst.k(), scale=float(2 * np.pi))
            TS("dve", Sgt[:], Sgt[:], pv[:, PV_SIGN:PV_SIGN + 1], None, ALU.mult, None, Sgt.k() + pv.k(), Sgt.k())

            hT = norm_hT(l, m, 0)
            wt, wv = load_wblk(win_v[l, :, :, COL_SG:COL_SG + 512], 0)
            for j in range(4):
                pp = ps_main.get()
                proj_fm(wt, wv, j * 128, 128, hT, pp)
                ACT(uT[:, j, :], pp[:], AF.Gelu, pp.k(), uT.ke(j * 512, 512))
            wt, wv = load_wblk(win_v[l, :, :, COL_SG + 512:COL_SG + 1024], 0)
            for r in range(4):
                pp = ps_main.get()
                for k in range(8):
                    MM(pp[:], hT[:, k, r * 128:(r + 1) * 128], wv[:, k, :], k == 0, k == 7, wt.k() + hT.k(), pp.k())
                g_ = fb.get()
                ACT(g_[:], pp[:], AF.Gelu, pp.k(), g_.k())
                S.op("dve", (lambda g_: lambda e: e.bn_stats(out=mvst[:, 0:6], in_=g_[:]))(g_), g_.k(), mvst.k())
                S.op("dve", lambda e: e.bn_aggr(out=mvst[:, 6:8], in_=mvst[:, 0:6]), mvst.k(), mvst.k())
                ACT(mvst[:, 7:8], mvst[:, 7:8], AF.Sqrt, mvst.k(), mvst.k(), bias=EPS)
                RECIP(mvst[:, 7:8], mvst[:, 7:8], mvst.k(), mvst.k())
                TS("dve", vhat[:, r, :], g_[:], mvst[:, 6:7], mvst[:, 7:8], ALU.subtract, ALU.mult,
                   g_.k() + mvst.k(), vhat.ke(r * 512, 512))
            for g in range(4):
                pp = ps_main.get()
                for r in range(4):
                    MM(pp[:, r * 128:(r + 1) * 128], vhat[:, r, g * 128:(g + 1) * 128], WsT[:, g, :], True, True,
                       vhat.k() + WsT.k(), pp.k())
                tmp = fb.get()
                STT("dve", tmp[:], pp[:], pvl(l, 16 + g), bsb[:, g, :], ALU.mult, ALU.add,
                    pp.k() + pv.k() + bsb.ke(g * 512, 512), tmp.k())
                TT("pool", ysgo[:, g, :], tmp[:], uT[:, g, :], ALU.mult, tmp.k() + uT.ke(g * 512, 512), ysgo.ke(g * 512, 512))
            DMA("sp", ysg_d[:, :, t0:t0 + 512], ysgo[:], ysgo.k(), [("ysg", m)])
            for j in range(4):
                cwt, cwtag = wget()
                cwv = cwt.h[:, 0:8 * 384].rearrange("p (k n) -> p k n", k=8)
                for i in range(3):
                    DMA("pool", cwv[:, :, i * 128:(i + 1) * 128],
                        win_v[l, :, :, COL_CONV + i * 512 + j * 128:COL_CONV + i * 512 + (j + 1) * 128], (), cwt.k(), tag=cwtag)
                pgc = ps_main.get()
                proj_fm(cwt, cwv, 128, 128, hT, pgc)
                gc = fb.get()
                CP("act", gc[:], pgc[:], pgc.k(), gc.k())
                pxv = ps_main.get()
                proj_fm(cwt, cwv, 256, 128, hT, pxv)
                t_ = tt_r.get()
                TT("dve", t_[:, 2:514], gc[:], pxv[:], ALU.mult, gc.k() + pxv.k(), t_.k())
                acc = fb.get()
                TS("dve", acc[:], t_[:, 2:514], pvl(l, 20 + 8 + j), None, ALU.mult, None, t_.k() + pv.k(), acc.k())
                STT("dve", acc[:], t_[:, 1:513], pvl(l, 20 + 4 + j), acc[:], ALU.mult, ALU.add, t_.k() + pv.k() + acc.k(), acc.k())
                STT("dve", acc[:], t_[:, 0:512], pvl(l, 20 + j), acc[:], ALU.mult, ALU.add, t_.k() + pv.k() + acc.k(), acc.k())
                pgb = ps_main.get()
                proj_fm(cwt, cwv, 0, 128, hT, pgb)
                TT("dve", ycvo[:, j, :], acc[:], pgb[:], ALU.mult, acc.k() + pgb.k(), ycvo.ke(j * 512, 512))
                CP("pool", fxt[:, j, 0:2], acc[:, 0:2], acc.k(), fxt.k())
                CP("dve", fxt[:, j, 2:4], pgb[:, 0:2], pgb.k(), fxt.k())
                CP("pool", halo_o[:, j, :], t_[:, 512:514], t_.k(), halo_o.k())
            DMA("sp", ycv_d[:, :, t0:t0 + 512], ycvo[:], ycvo.k(), [("ycv", m)])
            DMA("sp", fx_d[:, m, :, :], fxt[:], fxt.k(), [("fx", m)])
            ho = SL["HALO"] + m * 1024
            DMA("sp", sndt[("HL", 0)][m * 1024:(m + 1) * 1024].rearrange("(j p e) -> p j e", j=4, p=128), halo_o[:], halo_o.k(), sk(l, "HL", 0))
            wt, wv = load_wblk(win_v[l, :, :, COL_MLA:COL_MLA + 384], 0)
            for j in range(3):
                pp = ps_main.get()
                proj_fm(wt, wv, j * 128, 128, hT, pp)
                CP("act", cq_sb[:, j, :], pp[:], pp.k(), cq_sb.ke(j * 512, 512))
            for (j0, nj, gcol) in ((0, 2, 32), (2, 1, 34)):
                ss = ps_misc.get()
                for j in range(j0, j0 + nj):
                    xs = xsq_r.get()
                    ACT(xs[:], cq_sb[:, j, :], AF.Square, cq_sb.ke(j * 512, 512), xs.k())
                    MM(ss[:], ones_bf[:], xs[:], j == j0, j == j0 + nj - 1, ones_bf.k() + xs.k(), ss.k())
                rs_ = fb.get()
                ACT(rs_[:], ss[:], AF.Sqrt, ss.k(), rs_.k(), scale=1.0 / (128 * nj), bias=EPS)
                RECIP(rs_[:], rs_[:], rs_.k(), rs_.k())
                for j in range(j0, j0 + nj):
                    STT("dve", cn[:, j, :], cq_sb[:, j, :], pvl(l, gcol + (j - j0)), rs_[:], ALU.mult, ALU.mult,
                        cq_sb.ke(j * 512, 512) + pv.k() + rs_.k(), cn.ke(j * 512, 512))
            pka = ps_main.get()
            pkb = ps_main.get()
            for k in range(8):
                MM(pka[0:96, :], wkpe[:, k, 0:96], hT[:, k, :], k == 0, k == 7, wkpe.k() + hT.k(), pka.k())
            for k in range(8):
                MM(pkb[0:96, :], wkpe[:, k, 96:192], hT[:, k, :], k == 0, k == 7, wkpe.k() + hT.k(), pkb.k())
            t1 = fb.get()
            t2 = fb.get()
            TT("dve", t1[64:96, :], pka[64:96, :], Ct[64:96, :], ALU.mult, pka.k() + Ct.k(), t1.k())
            TT("dve", t2[64:96, :], pkb[64:96, :], Sgt[64:96, :], ALU.mult, pkb.k() + Sgt.k(), t2.k())
            for h in range(8):
                TT("pool", kTo[64:96, h, :], t1[64:96, :], t2[64:96, :], ALU.add, t1.k() + t2.k(), kTo.ke(h * 512, 512))
            for h in range(8):
                pqa = ps_main.get()
                pqb = ps_main.get()
                for k in range(2):
                    MM(pqa[0:96, :], wq[:, k, h * 128:h * 128 + 96], cn[:, k, :], k == 0, k == 1, wq.k() + cn.k(), pqa.k())
                for k in range(2):
                    MM(pqb[0:96, :], wq[:, k, h * 128 + 32:h * 128 + 128], cn[:, k, :], k == 0, k == 1, wq.k() + cn.k(), pqb.k())
                qo = qo_r.get()
                CP("act", qo[0:64, :], pqa[0:64, :], pqa.k(), qo.k())
                t1 = fb.get()
                t2 = fb.get()
                TT("dve", t1[64:96, :], pqa[64:96, :], Ct[64:96, :], ALU.mult, pqa.k() + Ct.k(), t1.k())
                TT("dve", t2[64:96, :], pqb[64:96, :], Sgt[64:96, :], ALU.mult, pqb.k() + Sgt.k(), t2.k())
                TT("pool", qo[64:96, :], t1[64:96, :], t2[64:96, :], ALU.add, t1.k() + t2.k(), qo.k())
                DMA("sp", qm_d[m, h, :, :], qo[:], qo.k(), [("qm", m, h)])
                pkn = ps_main.get()
                MM(pkn[0:64, :], wukv[:, h * 128:h * 128 + 64], cn[:, 2, :], True, True, wukv.k() + cn.k(), pkn.k())
                CP("act", kTo[0:64, h, :], pkn[0:64, :], pkn.k(), kTo.ke(h * 512, 512))
            for j in range(4):
                DMA("sp", sndt[("KM", j)].rearrange("(h r t) -> r h t", h=2, r=96)[:, :, t0:t0 + 512],
                    kTo[:, 2 * j:2 * j + 2, :], kTo.k(), sk(l, "KM", j))
            wv_v = wukv.h[:, :].rearrange("p (h e) -> p h e", h=8)[:, :, 64:128]
            for r in range(4):
                pp = ps_main.get()
                MM(pp[:].rearrange("p (h e) -> p h e", h=8), cn[:, 2, r * 128:(r + 1) * 128], wv_v, True, True,
                   wukv.k() + cn.k(), pp.k())
                CP("act", vxo[:, r, :].rearrange("p (h e) -> p h e", h=8)[:, :, 0:64],
                   pp[:].rearrange("p (h e) -> p h e", h=8), pp.k(), vxo.ke(r * 520, 520))
            for j in range(4):
                vm = sndt[("VM", j)].rearrange("(h t e) -> t h e", h=2, e=65)
                for r in range(4):
                    DMA("sp", vm[t0 + r * 128:t0 + (r + 1) * 128, :, :],
                        vxo[:, r, :].rearrange("p (h e) -> p h e", h=8)[:, 2 * j:2 * j + 2, :],
                        vxo.ke(r * 520, 520), sk(l, "VM", j))
            for part in range(2):
                wt, wv = load_wblk(win_v[l, :, :, COL_SB + part * 512:COL_SB + (part + 1) * 512], 0)
                for j in range(4):
                    pp = ps_main.get()
                    proj_fm(wt, wv, j * 128, 128, hT, pp)
                    sp_ = sbp_r.get()
                    if part == 0:
                        ACT(sp_[:], pp[:], AF.Identity, pp.k(), sp_.k(), scale=0.125)
                        for hh in range(2):
                            DMA("sp", qs_d[m, 2 * j + hh, :, :], sp_[hh * 64:(hh + 1) * 64, :], sp_.k(), [("qs", m, 2 * j + hh)])
                    else:
                        CP("act", sp_[:], pp[:], pp.k(), sp_.k())
                        ks = sndt[("KS", j)].rearrange("(h r t) -> h r t", h=2, r=64)
                        for hh in range(2):
                            DMA("sp", ks[hh, :, t0:t0 + 512], sp_[hh * 64:(hh + 1) * 64, :], sp_.k(), sk(l, "KS", j))
            wt, wv = load_wblk(win_v[l, :, :, COL_SB + 1024:COL_SB + 1536], 0)
            for r in range(4):
                pp = ps_main.get()
                for k in range(8):
                    MM(pp[:], hT[:, k, r * 128:(r + 1) * 128], wv[:, k, :], k == 0, k == 7, wt.k() + hT.k(), pp.k())
                CP("act", vso[:, r, :], pp[:], pp.k(), vso.ke(r * 512, 512))
            for j in range(4):
                vs = sndt[("VS", j)].rearrange("(h t e) -> t h e", h=2, e=64)
                for r in range(4):
                    DMA("sp", vs[t0 + r * 128:t0 + (r + 1) * 128, :, :],
                        vso[:, r, :].rearrange("p (h e) -> p h e", h=8)[:, 2 * j:2 * j + 2, :],
                        vso.ke(r * 512, 512), sk(l, "VS", j))

    if doB:
        B = Cursor(nc, ARENA, LIMIT, "b")
        ymla = B.alloc("ymla", [64, 8, 512], BF16)
        ysb = B.alloc("ysb", [64, 8, 512], BF16)
        ysg_l = B.alloc("ysg_l", [128, 4, 512], BF16)
        ycv_l = B.alloc("ycv_l", [128, 4, 512], BF16)
        fx_l = B.alloc("fx_l", [128, 4, 4], F32)
        hsel = B.alloc("hsel", [128, 4, 4, 2], BF16)
        hf = B.alloc("hf", [128, 4, 8], F32)
        X = B.fork("x")
        kc_r = Ring([X.alloc("kc%d" % i, [96, 2048], BF16) for i in range(2)])
        vc_r = Ring([X.alloc("vc%d" % i, [128, 16, 65], BF16) for i in range(2)])
        qT_r = Ring([X.alloc("qT%d" % i, [96, 512], BF16) for i in range(2)])
        pa_r = Ring([X.alloc("pa%d" % i, [128, 512], BF16) for i in range(3)])
        e_r = fa
        lp_r = Ring([X.alloc("lp%d" % i, [128, 512], BF16) for i in range(3)])
        lsum_r = Ring([X.alloc("lsum%d" % i, [128, 512], F32) for i in range(2)])
        lsbf_r = Ring([X.alloc("lsbf%d" % i, [128, 512], BF16) for i in range(2)])
        masks_t = X.alloc("masks", [128, 16, 512], BF16)

        Y = B.fork("y")
        macc = Y.alloc("macc", [128, 4, 512], F32)
        sig_r = Ring([Y.alloc("sig%d" % i, [128, 512], F32) for i in range(2)])
        prod_r = Ring([Y.alloc("prod%d" % i, [128, 512], F32) for i in range(2)])
        mergedT = Y.alloc("mergedT", [128, 8, 512], BF16)
        Z = B.fork("z")
        aT = Z.alloc("aT", [128, 32, 512], BF16)
        rl_r = Ring([Z.alloc("rl%d" % i, [128, 512], BF16) for i in range(2)])

    def attention(l, m, kind):
        gatt = gat_t[l]
        nkb = 16 * m + 16
        nch = m + 1
        mla = kind == "mla"
        if mla:
            KK, VK, KR, VE = "KM", "VM", 96, 65
            mask_d = mns_d
        else:
            KK, VK, KR, VE = "KS", "VS", 64, 64
            mask_d = mst_d
        kviews = [gatt[(KK, j)].rearrange("(c h r t) -> h r c t", c=4, h=2, r=KR) for j in range(4)]
        vviews = [gatt[(VK, j)].rearrange("(c h t e) -> h t c e", c=4, h=2, e=VE) for j in range(4)]
        DMA("pool", masks_t[:], mask_d.rearrange("j s t -> s j t"), (), masks_t.k())
        chorder = list(range(nch)) if mla else list(range(nch - 1, -1, -1))
        kborder = list(range(16)) if mla else list(range(15, -1, -1))
        chunks = [(h, ch) for h in range(8) for ch in chorder]
        loaded = {}

        def load_chunk(ci):
            if ci >= len(chunks) or ci in loaded:
                return
            h, ch = chunks[ci]
            kc = kc_r.get()
            vc = vc_r.get()
            DMA("sp", kc.h[0:KR, :].rearrange("r (c t) -> r c t", c=4),
                kviews[h // 2][h % 2, :, :, ch * 512:(ch + 1) * 512], [("gat", l, KK, h // 2)], kc.k())
            for c_ in range(4):
                DMA("sp", vc.h[:, c_ * 4:(c_ + 1) * 4, 0:VE],
                    vviews[h // 2][h % 2, ch * 512:(ch + 1) * 512, c_, :].rearrange("(j p) e -> p j e", p=128),
                    [("gat", l, VK, h // 2)], vc.k())
            loaded[ci] = (kc, vc)

        qts = {}

        def load_q(h):
            if h >= 8 or h in qts:
                return
            qT = qT_r.get()
            if mla:
                DMA("sp", qT[0:96, :], qm_d[m, h, :, :], [("qm", m, h)], qT.k())
            else:
                DMA("sp", qT[0:64, :], qs_d[m, h, :, :], [("qs", m, h)], qT.k())
            qts[h] = qT

        load_q(0)
        load_chunk(0)
        for h in range(8):
            qT = qts[h]
            O = ps_acc.get()
            lsum = lsum_r.get() if not mla else None
            items = []
            for ci_l, ch in enumerate(chorder):
                for kb in kborder:
                    items.append((h * nch + ci_l, ch, kb))
            n = len(items)
            st = [dict() for _ in range(n)]

            def s1(i):
                ci, ch, kb = items[i]
                if kb == kborder[0]:
                    load_chunk(ci)
                if kb == kborder[3]:
                    load_chunk(ci + 1)
                    load_q(h + 1)
                kc, vc = loaded[ci]
                kbg = ch * 16 + kb
                d = st[i]
                d["vc"], d["kb"] = vc, kb
                d["masked"] = kbg >= nkb - 16
                d["mj"] = kbg - (nkb - 16)
                Sp = ps_main.get()
                d["Sp"] = Sp
                if mla:
                    MM(Sp[:], kc[0:96, kb * 128:(kb + 1) * 128], qT[0:96, :], True, True, kc.k() + qT.k(), Sp.k())
                    Pt = pa_r.get()
                    ACT(Pt[:], Sp[:], AF.Exp, Sp.k(), Pt.k(), scale=float(96 ** -0.5))
                    if d["masked"]:
                        TT("dve", Pt[:], Pt[:], masks_t[:, d["mj"], :], ALU.mult, Pt.k() + masks_t.k(), Pt.k())
                    d["A"] = Pt
                else:
                    MM(Sp[:], kc[0:64, kb * 128:(kb + 1) * 128], qT[0:64, :], True, False, kc.k() + qT.k(), Sp.k())
                    E = e_r.get()
                    ACT(E[:], Sp[:], AF.Exp, Sp.k(), E.k())
                    Lp = lp_r.get()
                    ACT(Lp[:], E[:], AF.Ln, E.k(), Lp.k(), bias=1.0)
                    if d["masked"]:
                        TT("dve", Lp[:], Lp[:], masks_t[:, d["mj"], :], ALU.mult, Lp.k() + masks_t.k(), Lp.k())
                    d["Lp"] = Lp

            def s2(i):
                d = st[i]
                Sp, Lp = d["Sp"], d["Lp"]
                first = i == 0
                MM(Sp[:], negU[:], Lp[:], False, first, negU.k() + Lp.k(), Sp.k(), skip_group_check=True)
                if not first:
                    lsbf = lsbf_r.get()
                    CP("dve", lsbf[:], lsum[:], lsum.k(), lsbf.k())
                    MM(Sp[:], negones[:], lsbf[:], False, True, negones.k() + lsbf.k(), Sp.k(), skip_group_check=True)
                    if i < n - 1:
                        TT("dve", lsum[:], lsum[:], Lp[:], ALU.add, lsum.k() + Lp.k(), lsum.k())
                else:
                    CP("dve", lsum[:], Lp[:], Lp.k(), lsum.k())
                At = pa_r.get()
                ACT(At[:], Sp[:], AF.Exp, Sp.k(), At.k())
                if d["masked"]:
                    TT("dve", At[:], At[:], masks_t[:, d["mj"], :], ALU.mult, At.k() + masks_t.k(), At.k())
                d["A"] = At

            def s3(i):
                d = st[i]
                vc, kb, At = d["vc"], d["kb"], d["A"]
                if mla:
                    MM(O[0:65, :], vc[:, kb, 0:65], At[:], i == 0, i == n - 1, vc.k() + At.k(), O.k())
                else:
                    MM(O[0:64, :], vc[:, kb, 0:64], At[:], i == 0, i == n - 1, vc.k() + At.k(), O.k())

            if mla:
                for t in range(n + 1):
                    if t < n:
                        s1(t)
                    if t >= 1:
                        s3(t - 1)
            else:
                for t in range(n + 2):
                    if t < n:
                        s1(t)
                    if 1 <= t <= n:
                        s2(t - 1)
                    if t >= 2:
                        s3(t - 2)
            if mla:
                rs_t = fa.get()
                bc_t = fa.get()
                CP("act", rs_t[64:65, :], O[64:65, :], O.k(), rs_t.k())
                RECIP(rs_t[64:65, :], rs_t[64:65, :], rs_t.k(), rs_t.k())
                pb = ps_misc.get()
                MM(pb[0:64, :], ones_f[64:65, 0:64], rs_t[64:65, :], True, True, ones_f.k() + rs_t.k(), pb.k())
                CP("act", bc_t[0:64, :], pb[0:64, :], pb.k(), bc_t.k())
                TT("dve", ymla[:, h, :], O[0:64, :], bc_t[0:64, :], ALU.mult, O.k() + bc_t.k(), ymla.ke(h * 512, 512))
            else:
                CP("act", ysb[:, h, :], O[0:64, :], O.k(), ysb.ke(h * 512, 512))

    def phaseB(l, last):
        wbr_v = wbr_d
        for m in range(NSB):
            t0 = m * 512
            DMA("sp", ysg_l[:], ysg_d[:, :, t0:t0 + 512], [("ysg", m)], ysg_l.k())
            DMA("sp", ycv_l[:], ycv_d[:, :, t0:t0 + 512], [("ycv", m)], ycv_l.k())
            DMA("sp", fx_l[:], fx_d[:, m, :, :], [("fx", m)], fx_l.k())
            hv = gat_t[l][("HL", 0)].rearrange("(c m j p e) -> c m p j e", c=4, m=NSB, j=4, p=128)
            MEMSET("dve", hsel[:], 0.0, hsel.k())
            for q in range(4):
                if q == 0:
                    if m == 0:
                        continue
                    src = hv[3, m - 1]
                else:
                    src = hv[q - 1, m]
                DMA("sp", hsel[:, q, :, :], src, [("gat", l, "HL", 0)], hsel.k())
            MEMSET("dve", hf[:], 0.0, hf.k())
            for q in range(4):
                STT("dve", hf[:, :, 0:2], hsel[:, q, :, :], pv[:, PV_OH + q:PV_OH + q + 1], hf[:, :, 0:2], ALU.mult, ALU.add,
                    hsel.k() + pv.k() + hf.k(), hf.k())
            for j in range(4):
                w0, w1 = pvl(l, 20 + j), pvl(l, 24 + j)
                TS("dve", hf[:, j, 2:3], hf[:, j, 1:2], w1, None, ALU.mult, None, hf.k() + pv.k(), hf.k())
                STT("dve", hf[:, j, 2:3], hf[:, j, 0:1], w0, hf[:, j, 2:3], ALU.mult, ALU.add, hf.k() + pv.k(), hf.k())
                TS("dve", hf[:, j, 3:4], hf[:, j, 1:2], w0, None, ALU.mult, None, hf.k() + pv.k(), hf.k())
                TT("dve", hf[:, j, 2:4], hf[:, j, 2:4], fx_l[:, j, 0:2], ALU.add, hf.k() + fx_l.k(), hf.k())
                TT("dve", ycv_l[:, j, 0:2], hf[:, j, 2:4], fx_l[:, j, 2:4], ALU.mult, hf.k() + fx_l.k(), ycv_l.ke(j * 512, 512))
            attention(l, m, "mla")
            attention(l, m, "sb")
            hT = norm_hT(l, m, 0)
            for grp in range(2):
                for nbr in range(4):
                    gt, gvw = load_wblk(win_v[l, :, :, COL_GATE + nbr * 1024 + grp * 512:COL_GATE + nbr * 1024 + (grp + 1) * 512], 0)
                    if nbr < 2:
                        bt, bvw = load_wblk(wbr_v[l, nbr].rearrange("(k p) n -> p k n", p=128)[:, :, grp * 512:(grp + 1) * 512], 0)
                    else:
                        bt, bvw = load_wblk(wbr_v[l, nbr].rearrange("(h p) n -> p h n", p=64)[:, :, grp * 512:(grp + 1) * 512], 0)
                    for dcl in range(4):
                        pg = ps_main.get()
                        proj_fm(gt, gvw, dcl * 128, 128, hT, pg)
                        sg_ = sig_r.get()
                        ACT(sg_[:], pg[:], AF.Sigmoid, pg.k(), sg_.k())
                        pu = ps_main.get()
                        if nbr < 2:
                            src = ysg_l if nbr == 0 else ycv_l
                            for k in range(4):
                                MM(pu[:], bvw[:, k, dcl * 128:(dcl + 1) * 128], src[:, k, :], k == 0, k == 3, bt.k() + src.k(), pu.k())
                        else:
                            src = ymla if nbr == 2 else ysb
                            for hh in range(8):
                                MM(pu[:], bvw[0:64, hh, dcl * 128:(dcl + 1) * 128], src[:, hh, :], hh == 0, hh == 7, bt.k() + src.k(), pu.k())
                        if nbr == 0:
                            TT("dve", macc[:, dcl, :], sg_[:], pu[:], ALU.mult, sg_.k() + pu.k(), macc.ke(dcl * 512, 512))
                        else:
                            pr = prod_r.get()
                            TT("dve", pr[:], sg_[:], pu[:], ALU.mult, sg_.k() + pu.k(), pr.k())
                            if nbr < 3:
                                TT("pool", macc[:, dcl, :], macc[:, dcl, :], pr[:], ALU.add, macc.ke(dcl * 512, 512) + pr.k(), macc.ke(dcl * 512, 512))
                            else:
                                dc = grp * 4 + dcl
                                TT("pool", mergedT[:, dc, :], macc[:, dcl, :], pr[:], ALU.add, macc.ke(dcl * 512, 512) + pr.k(), mergedT.ke(dc * 512, 512))
            for half in range(2):
                wt, wv = load_wblk(wout_d[l].rearrange("(k p) n -> p k n", p=128)[:, :, half * 512:(half + 1) * 512], 0)
                for dcl in range(4):
                    dc = half * 4 + dcl
                    pp = ps_main.get()
                    for k in range(8):
                        MM(pp[:], wv[:, k, dcl * 128:(dcl + 1) * 128], mergedT[:, k, :], k == 0, k == 7, wt.k() + mergedT.k(), pp.k())
                    STT("dve", xT[:, dc, t0:t0 + 512], pp[:], layc(l, 2, dc), xT[:, dc, t0:t0 + 512], ALU.mult, ALU.add,
                        pp.k() + lay.k() + xk(dc, m), xk(dc, m))
            h2 = norm_hT(l, m, 1)
            for nb in range(8):
                wt, wv = load_wblk(w1_d[l].rearrange("(k p) n -> p k n", p=128)[:, :, nb * 512:(nb + 1) * 512], 0)
                for j in range(4):
                    pp = ps_main.get()
                    proj_fm(wt, wv, j * 128, 128, h2, pp)
                    rl = rl_r.get()
                    ACT(rl[:], pp[:], AF.Relu, pp.k(), rl.k())
                    fi = nb * 4 + j
                    TT("pool", aT[:, fi, :], rl[:], rl[:], ALU.mult, rl.k(), aT.ke(fi * 512, 512))
            for dc in range(8):
                wt, wtag = wget()
                wv = wt.h[:, :].rearrange("p (k n) -> p k n", k=32)
                DMA("pool", wv, w2_d[l].rearrange("(k p) n -> p k n", p=128)[:, :, dc * 128:(dc + 1) * 128], (), wt.k(), tag=wtag)
                pp = ps_main.get()
                for k in range(32):
                    MM(pp[:], wv[:, k, :], aT[:, k, :], k == 0, k == 31, wt.k() + aT.ke(k * 512, 512), pp.k())
                STT("dve", xT[:, dc, t0:t0 + 512], pp[:], layc(l, 5, dc), xT[:, dc, t0:t0 + 512], ALU.mult, ALU.add,
                    pp.k() + lay.k() + xk(dc, m), xk(dc, m))
            if mode == "B":
                for k in range(8):
                    DMA("sp", xTo_d[:, k, t0:t0 + 512], xT[:, k, t0:t0 + 512], xk(k, m), [("xTo", k, m)])
            if last:
                ss = ps_misc.get()
                for k in range(8):
                    xs = xsq_r.get()
                    ACT(xs[:], xT[:, k, t0:t0 + 512], AF.Square, xk(k, m), xs.k())
                    MM(ss[:], ones_bf[:], xs[:], k == 0, k == 7, ones_bf.k() + xs.k(), ss.k())
                ACT(rstd[:], ss[:], AF.Sqrt, ss.k(), rstd.k(), scale=1.0 / D, bias=EPS)
                RECIP(rstd[:], rstd[:], rstd.k(), rstd.k())
                for k in range(8):
                    tmp = fa.get()
                    STT("dve", tmp[:], xT[:, k, t0:t0 + 512], pv[:, PV_FNG + k:PV_FNG + k + 1], rstd[:], ALU.mult, ALU.mult,
                        xk(k, m) + pv.k() + rstd.k(), tmp.k())
                    DMA("sp", outT_d[:, k, t0:t0 + 512], tmp[:], tmp.k(), [("outT", k, m)])

    for l in range(LN):
        if doA:
            phaseA(l)
        if mode == "F":
            for (kd, j) in CHN:
                S.coll((lambda l, kd, j: lambda e: e.collective_compute(
                    "AllGather", ALU.bypass, replica_groups=[[0, 1, 2, 3], [4, 5, 6, 7]],
                    ins=[snd_t[l][(kd, j)].rearrange("(a b) -> a b", b=1024).opt()],
                    outs=[gat_t[l][(kd, j)].rearrange("(a b) -> a b", b=1024).opt()]))(l, kd, j),
                    snd_keys[l][(kd, j)], [("gat", l, kd, j)])
        if doB:
            phaseB(l, last=(mode == "B" or l == LN - 1))
    S.emit()
    return nc, S.stats


def _consts():
    tri = np.triu(np.ones((128, 128), np.float32))
    negU = -np.tril(np.ones((128, 128), np.float32))
    return tri, negU


def _masks(c):
    tri_ns = np.triu(np.ones((128, 128), np.float32))
    tri_s = np.triu(np.ones((128, 128), np.float32), 1)
    mns = np.zeros((16, 128, 512), np.float32)
    mst = np.zeros((16, 128, 512), np.float32)
    for j in range(16):
        for r in range(4):
            qb = 4 * c + r
            if j < qb:
                mns[j, :, r * 128:(r + 1) * 128] = 1.0
                mst[j, :, r * 128:(r + 1) * 128] = 1.0
            elif j == qb:
                mns[j, :, r * 128:(r + 1) * 128] = tri_ns
                mst[j, :, r * 128:(r + 1) * 128] = tri_s
    return mns, mst


def _chunkcols(v):
    return np.ascontiguousarray(v.reshape(-1, 128).T)


def make_pv(inp, b, c, layers):
    cols = [_chunkcols(inp["c"][b])]
    for l in layers:
        cols.append(_chunkcols(inp["norm1_g"][l]))
        cols.append(_chunkcols(inp["norm2_g"][l]))
        cols.append(_chunkcols(inp["sg_norm_g"][l]))
        for j in range(3):
            cols.append(_chunkcols(inp["conv_w"][l, j]))
        cols.append(_chunkcols(inp["mla_q_norm_g"][l]))
        cols.append(_chunkcols(inp["mla_kv_norm_g"][l]))
    cols.append(_chunkcols(inp["final_norm_g"]))
    sign = np.where((np.arange(128) % 32) < 16, -1.0, 1.0).astype(np.float32)[:, None]
    cols.append(sign)
    oh = np.zeros((128, 4), np.float32)
    oh[:, c] = 1.0
    cols.append(oh)
    return np.ascontiguousarray(np.concatenate(cols, axis=1).astype(np.float32))


def layer_weights(inp, layers):
    ls = list(layers)
    w = {}
    w["ada_w"] = np.ascontiguousarray(inp["ada_w"][ls])
    w["adab"] = np.ascontiguousarray(inp["ada_b"][ls].reshape(1, -1))
    w["w_in"] = np.ascontiguousarray(inp["w_in"][ls])
    w["sgwT"] = np.ascontiguousarray(np.transpose(inp["sg_w"][ls], (0, 3, 1, 2)))
    w["sgb"] = np.ascontiguousarray(np.tile(inp["sg_b"][ls][:, :, None, :], (1, 1, 4, 1)).reshape(len(ls), 1, 4 * 512))
    uq = inp["mla_w_uq"][ls].reshape(len(ls), 256, 8, 96)
    pe = uq[..., 64:96]
    pesw = np.concatenate([pe[..., 16:32], pe[..., 0:16]], axis=-1)
    w["wq"] = np.ascontiguousarray(np.concatenate([uq, pesw], axis=-1).reshape(len(ls), 256, 8 * 128))
    w["wukv"] = np.ascontiguousarray(inp["mla_w_ukv"][ls])
    kpe = inp["w_in"][ls][:, :, COL_MLA + 384:COL_MLA + 416]
    kpesw = np.concatenate([kpe[..., 16:32], kpe[..., 0:16]], axis=-1)
    z = np.zeros(kpe.shape[:2] + (64,), np.float32)
    w["wkpe"] = np.ascontiguousarray(np.concatenate([z, kpe, z, kpesw], axis=-1))
    w["w_branch"] = np.ascontiguousarray(inp["w_branch"][ls])
    w["w_out"] = np.ascontiguousarray(inp["w_out"][ls])
    w["w1"] = np.ascontiguousarray(inp["mlp_w1"][ls])
    w["w2"] = np.ascontiguousarray(inp["mlp_w2"][ls])
    return w


A_KEYS = ["ada_w", "adab", "w_in", "sgwT", "sgb", "wq", "wukv", "wkpe"]
B_KEYS = ["ada_w", "adab", "w_in", "w_branch", "w_out", "w1", "w2"]

_cache = {}


def _get(mode, NSB, LN):
    key = (mode, NSB, LN)
    if key not in _cache:
        _cache[key] = build(mode, NSB, LN)
    return _cache[key][0]


def run_model(inp, NSB, depth, fused=False):
    x = np.asarray(inp["x"], np.float32)
    Bsz, SEQ, _ = x.shape
    NTOK = NSB * 512
    assert SEQ == 4 * NTOK and Bsz == 2
    inv_freq = (10000.0 ** (-np.arange(0, 32, 2, dtype=np.float32) / 32)).astype(np.float32)
    invf = np.tile(inv_freq, 8)[None, :].astype(np.float32)
    tri, negU = _consts()
    cores = [(b, c) for b in range(2) for c in range(4)]

    def tok_idx(c):
        return np.concatenate([np.arange((4 * m + c) * 512, (4 * m + c + 1) * 512) for m in range(NSB)])

    xTs = []
    for (b, c) in cores:
        xt = x[b, tok_idx(c), :].T
        xTs.append(np.ascontiguousarray(xt.reshape(8, 128, NTOK).transpose(1, 0, 2)))
    posis = [np.ascontiguousarray(np.asarray(inp["positions"])[b, tok_idx(c)][None, :].astype(np.int32)) for (b, c) in cores]
    masks = [_masks(c) for (b, c) in cores]
    outT = None
    if fused:
        w = layer_weights(inp, range(depth))
        nc = _get("F", NSB, depth)
        in_maps = []
        for i, (b, c) in enumerate(cores):
            d = dict(xT=xTs[i], pv=make_pv(inp, b, c, range(depth)), posi=posis[i], invf=invf, tri=tri, negU=negU,
                     mask_ns=masks[i][0], mask_s=masks[i][1])
            for k_ in set(A_KEYS + B_KEYS):
                d[k_] = w[k_]
            in_maps.append(d)
        res = run_bass_kernel_spmd(nc, in_maps, core_ids=list(range(8)))
        outT = [r["outT"] for r in res.results]
    else:
        ncA = _get("A", NSB, 1)
        ncB = _get("B", NSB, 1)
        for l in range(depth):
            w = layer_weights(inp, [l])
            in_maps = []
            for i, (b, c) in enumerate(cores):
                d = dict(xT=xTs[i], pv=make_pv(inp, b, c, [l]), posi=posis[i], invf=invf, tri=tri)
                for k_ in A_KEYS:
                    d[k_] = w[k_]
                in_maps.append(d)
            resA = run_bass_kernel_spmd(ncA, in_maps, core_ids=list(range(8))).results
            in_maps = []
            for i, (b, c) in enumerate(cores):
                gats = {}
                for kd in ("HL", "KM", "VM", "KS", "VS"):
                    for j in range(1 if kd == "HL" else 4):
                        nm = "0_%s%d" % (kd, j)
                        gats["gat" + nm] = np.concatenate([resA[b * 4 + cc]["snd" + nm] for cc in range(4)])
                d = dict(xT=xTs[i], pv=make_pv(inp, b, c, [l]), negU=negU, mask_ns=masks[i][0], mask_s=masks[i][1],
                         qm=resA[i]["qm"], qs=resA[i]["qs"], ysg=resA[i]["ysg"], ycv=resA[i]["ycv"], fx=resA[i]["fx"])
                for k_ in B_KEYS:
                    d[k_] = w[k_]
                d.update(gats)
                in_maps.append(d)
            resB = run_bass_kernel_spmd(ncB, in_maps, core_ids=list(range(8))).results
            xTs = [r["xTo"] for r in resB]
            outT = [r["outT"] for r in resB]
    out = np.zeros((2, SEQ, D), np.float32)
    for i, (b, c) in enumerate(cores):
        o = outT[i].transpose(1, 0, 2).reshape(1024, NTOK).T
        out[b, tok_idx(c), :] = o
    return out


def kernel(**inputs):
    return run_model(inputs, NSB=4, depth=4, fused=True)
```
